# Optimizing a Trainium2 kernel written in Bass

```python
import jax, jax.numpy as jnp
from jax import lax
import numpy as np

D_MODEL = 1024
BATCH = 16
SEQ = 2048
DEPTH = 1

CHUNK = 64
GMLP_BLOCK = 128
GMLP_WIDTH = D_MODEL
GMLP_GROUPS = 8
GMLP_GROUP_DIM = GMLP_WIDTH // GMLP_GROUPS
MLSTM_HEADS = 4
MLSTM_HEAD_DIM = 256
MLSTM_WIDTH = MLSTM_HEADS * MLSTM_HEAD_DIM
CONV_K = 4
FFN_DIM = 4 * D_MODEL
EPS = 1e-6

OFF_U = 0
OFF_V = OFF_U + GMLP_WIDTH
OFF_Q = OFF_V + GMLP_WIDTH
OFF_K = OFF_Q + MLSTM_WIDTH
OFF_MV = OFF_K + MLSTM_WIDTH
OFF_O = OFF_MV + MLSTM_WIDTH
OFF_I = OFF_O + MLSTM_WIDTH
OFF_F = OFF_I + MLSTM_HEADS
OFF_GA = OFF_F + MLSTM_HEADS
OFF_GB = OFF_GA + D_MODEL
IN_COLS = OFF_GB + D_MODEL

kernel_name = "hybrid_gmlp_mlstm_adaln_block"


def rmsnorm(x, g):
    xf = x.astype(jnp.float32)
    y = xf * lax.rsqrt(jnp.mean(xf * xf, axis=-1, keepdims=True) + EPS)
    return y.astype(x.dtype) * g


def layernorm(x, g, b):
    xf = x.astype(jnp.float32)
    mu = jnp.mean(xf, axis=-1, keepdims=True)
    var = jnp.mean(jnp.square(xf - mu), axis=-1, keepdims=True)
    y = (xf - mu) * lax.rsqrt(var + EPS)
    return y.astype(x.dtype) * g + b


def modulate(xn, shift, scale):
    return xn * (1.0 + scale[:, None, :]) + shift[:, None, :]


def causal_dwconv(x, w, b):
    C = x.shape[-1]
    y = lax.conv_general_dilated(x, w[:, None, :], window_strides=(1,),
                                 padding=[(CONV_K - 1, 0)],
                                 dimension_numbers=('NWC', 'WIO', 'NWC'),
                                 feature_group_count=C)
    return y + b


def gmlp_branch(z_u, z_v, ln_g, ln_b, w_s, b_s):
    u = jax.nn.gelu(z_u)
    v = layernorm(jax.nn.gelu(z_v), ln_g, ln_b)
    B, S, _ = v.shape
    nb = S // GMLP_BLOCK
    v = v.reshape(B, nb, GMLP_BLOCK, GMLP_GROUPS, GMLP_GROUP_DIM)
    cid = jnp.arange(GMLP_BLOCK) // CHUNK
    mask = cid[None, :] <= cid[:, None]
    ws = jnp.where(mask[None], w_s, jnp.zeros_like(w_s))
    mixed = jnp.einsum('gts,bnsgc->bntgc', ws, v) + b_s.T[None, None, :, :, None]
    return u * mixed.reshape(B, S, GMLP_WIDTH)


def mlstm_branch(q, k, v, o_pre, i_pre, f_pre, hn_g):
    out_dtype = q.dtype
    B, S, _ = q.shape
    H, dh, L = MLSTM_HEADS, MLSTM_HEAD_DIM, CHUNK
    nc = S // L

    def heads(t):
        return t.astype(jnp.float32).reshape(B, nc, L, H, dh).transpose(1, 0, 3, 2, 4)

    def gates(t):
        return t.astype(jnp.float32).reshape(B, nc, L, H).transpose(1, 0, 3, 2)

    qh = heads(q) * (dh ** -0.5)
    kh = heads(k)
    vh = heads(v)
    li = gates(i_pre)
    lf = jax.nn.log_sigmoid(gates(f_pre))
    causal = jnp.tril(jnp.ones((L, L), dtype=bool))

    def step(carry, xs):
        C, n, m = carry
        qc, kc, vc, lic, lfc = xs
        bcum = jnp.cumsum(lfc, axis=-1)
        dmat = bcum[..., :, None] - bcum[..., None, :] + lic[..., None, :]
        dmat = jnp.where(causal, dmat, -jnp.inf)
        m_inter = bcum + m[..., None]
        m_t = jnp.maximum(m_inter, jnp.max(dmat, axis=-1))
        wts = jnp.exp(dmat - m_t[..., None])
        s = jnp.einsum('bhtd,bhsd->bhts', qc, kc) * wts
        inter = jnp.exp(m_inter - m_t)
        num = (jnp.einsum('bhts,bhsv->bhtv', s, vc)
               + inter[..., None] * jnp.einsum('bhtk,bhkv->bhtv', qc, C))
        den = jnp.sum(s, axis=-1) + inter * jnp.einsum('bhtk,bhk->bht', qc, n)
        h = num / jnp.maximum(jnp.abs(den), jnp.exp(-m_t))[..., None]
        b_last = bcum[..., -1]
        g = b_last[..., None] - bcum + lic
        m_new = jnp.maximum(b_last + m, jnp.max(g, axis=-1))
        decay = jnp.exp(b_last + m - m_new)
        wk = jnp.exp(g - m_new[..., None])[..., None] * kc
        C = decay[..., None, None] * C + jnp.einsum('bhsk,bhsv->bhkv', wk, vc)
        n = decay[..., None] * n + jnp.sum(wk, axis=2)
        return (C, n, m_new), h

    init = (jnp.zeros((B, H, dh, dh), jnp.float32),
            jnp.zeros((B, H, dh), jnp.float32),
            jnp.zeros((B, H), jnp.float32))
    _, h = lax.scan(step, init, (qh, kh, vh, li, lf))
    h = h.transpose(1, 0, 3, 2, 4).reshape(B, S, H, dh)
    mu = jnp.mean(h, axis=-1, keepdims=True)
    var = jnp.mean(jnp.square(h - mu), axis=-1, keepdims=True)
    h = (h - mu) * lax.rsqrt(var + EPS) * hn_g.astype(jnp.float32).reshape(H, dh)
    h = h.reshape(B, S, MLSTM_WIDTH) * jax.nn.sigmoid(o_pre.astype(jnp.float32))
    return h.astype(out_dtype)


def setup_inputs(seed: int = 0) -> dict:
    key = jax.random.key(seed)
    ks = jax.random.split(key, 24)
    f32 = jnp.float32
    D = D_MODEL

    def nrm(k, shape, scale):
        return jax.random.normal(k, shape, f32) * scale

    f_bias = jnp.broadcast_to(jnp.linspace(3.0, 6.0, MLSTM_HEADS, dtype=f32), (DEPTH, MLSTM_HEADS))
    gate_b = jnp.stack([nrm(ks[7], (DEPTH, MLSTM_HEADS), 0.1),
                        f_bias + nrm(ks[8], (DEPTH, MLSTM_HEADS), 0.1)], axis=1)
    return {
        "x": nrm(ks[0], (BATCH, SEQ, D), 1.0),
        "c": nrm(ks[1], (BATCH, D), 1.0),
        "w_ada": nrm(ks[2], (DEPTH, D, 6 * D), 0.5 * D ** -0.5),
        "b_ada": nrm(ks[3], (DEPTH, 6 * D), 0.02),
        "norm1_g": 1.0 + nrm(ks[4], (DEPTH, D), 0.02),
        "w_in": nrm(ks[5], (DEPTH, D, IN_COLS), D ** -0.5),
        "conv_w": nrm(ks[6], (DEPTH, CONV_K, 2 * MLSTM_WIDTH), CONV_K ** -0.5),
        "conv_b": nrm(ks[9], (DEPTH, 2 * MLSTM_WIDTH), 0.02),
        "mlstm_gate_b": gate_b,
        "gmlp_ln_g": 1.0 + nrm(ks[10], (DEPTH, GMLP_WIDTH), 0.02),
        "gmlp_ln_b": nrm(ks[11], (DEPTH, GMLP_WIDTH), 0.02),
        "gmlp_ws": nrm(ks[12], (DEPTH, GMLP_GROUPS, GMLP_BLOCK, GMLP_BLOCK), 0.5 * GMLP_BLOCK ** -0.5),
        "gmlp_bs": 1.0 + nrm(ks[13], (DEPTH, GMLP_GROUPS, GMLP_BLOCK), 0.1),
        "mlstm_hn_g": 1.0 + nrm(ks[14], (DEPTH, MLSTM_WIDTH), 0.02),
        "w_out": nrm(ks[15], (DEPTH, D, D), D ** -0.5),
        "norm2_g": 1.0 + nrm(ks[16], (DEPTH, D), 0.02),
        "w_ff1": nrm(ks[17], (DEPTH, D, FFN_DIM), D ** -0.5),
        "w_ff2": nrm(ks[18], (DEPTH, FFN_DIM, D), FFN_DIM ** -0.5),
        "final_g": 1.0 + nrm(ks[19], (D,), 0.02),
    }


def reference(x, c, w_ada, b_ada, norm1_g, w_in, conv_w, conv_b, mlstm_gate_b,
              gmlp_ln_g, gmlp_ln_b, gmlp_ws, gmlp_bs, mlstm_hn_g, w_out,
              norm2_g, w_ff1, w_ff2, final_g):
    h = x
    c_act = jax.nn.silu(c)
    for l in range(DEPTH):
        mod = c_act @ w_ada[l] + b_ada[l]
        sh1, sc1, g1, sh2, sc2, g2 = jnp.split(mod, 6, axis=-1)

        xn = modulate(rmsnorm(h, norm1_g[l]), sh1, sc1)
        p = xn @ w_in[l]
        qk = jax.nn.silu(causal_dwconv(p[..., OFF_Q:OFF_MV], conv_w[l], conv_b[l]))
        q, k = jnp.split(qk, 2, axis=-1)
        y_a = gmlp_branch(p[..., OFF_U:OFF_V], p[..., OFF_V:OFF_Q],
                          gmlp_ln_g[l], gmlp_ln_b[l], gmlp_ws[l], gmlp_bs[l])
        y_b = mlstm_branch(q, k, p[..., OFF_MV:OFF_O], p[..., OFF_O:OFF_I],
                           p[..., OFF_I:OFF_F] + mlstm_gate_b[l, 0],
                           p[..., OFF_F:OFF_GA] + mlstm_gate_b[l, 1],
                           mlstm_hn_g[l])
        merged = (jax.nn.sigmoid(p[..., OFF_GA:OFF_GB]) * y_a
                  + jax.nn.sigmoid(p[..., OFF_GB:IN_COLS]) * y_b)
        h = h + g1[:, None, :] * (merged @ w_out[l])

        xn = modulate(rmsnorm(h, norm2_g[l]), sh2, sc2)
        ff = jnp.square(jax.nn.relu(xn @ w_ff1[l])) @ w_ff2[l]
        h = h + g2[:, None, :] * ff
    return rmsnorm(h, final_g)
```

```python
import math
import numpy as np
import ml_dtypes
from contextlib import ExitStack
import concourse.bass as bass
import concourse.mybir as mybir
from concourse.bass_utils import run_bass_kernel_spmd

F32 = mybir.dt.float32
BF16 = mybir.dt.bfloat16
AF = mybir.ActivationFunctionType
ALU = mybir.AluOpType
AX = mybir.AxisListType

D = 1024
SEQ = 2048
NT_SEQ = SEQ // 128
NSEQ = 2
NT = NT_SEQ * NSEQ
OFF_U, OFF_V, OFF_Q, OFF_K, OFF_MV, OFF_O, OFF_I, OFF_F, OFF_GA, OFF_GB, IN_COLS = (
    0, 1024, 2048, 3072, 4096, 5120, 6144, 6148, 6152, 7176, 8200)
EPS = 1e-6
SEM_M = 8000
SEM_K = 6
N_DMA_SP = 12
N_DMA_POOL = 6
STRICT = True


class Op:
    __slots__ = ("eng", "fn", "dma", "gid", "deps", "pos", "ms", "msn", "sem", "val")


class Prog:
    ENG = ("pe", "act", "dve", "pool", "sp")

    def __init__(self):
        self.q = {e: [] for e in self.ENG}
        self.lastw = {}
        self.readers = {}
        self.n = 0

    def add(self, eng, fn, r=(), w=(), dma=False):
        op = Op()
        op.eng, op.fn, op.dma, op.gid = eng, fn, dma, self.n
        self.n += 1
        op.ms = False
        r = list(r)
        w = list(w)
        for k in list(r):
            if k.startswith("ps"):
                r.remove(k)
                if k not in w:
                    w.append(k)
        deps = {}
        for k in r:
            o = self.lastw.get(k)
            if o is not None:
                deps[o.gid] = o
        for k in w:
            o = self.lastw.get(k)
            if o is not None:
                deps[o.gid] = o
            for o in self.readers.get(k, ()):
                deps[o.gid] = o
        for k in w:
            self.lastw[k] = op
            self.readers[k] = []
        for k in r:
            self.readers.setdefault(k, []).append(op)
        deps.pop(op.gid, None)
        op.deps = list(deps.values())
        op.pos = len(self.q[eng])
        self.q[eng].append(op)
        return op

    def _needs_wait(self, op, d):
        if d.dma:
            return True
        if d.eng == op.eng:
            if op.eng == "pe":
                return False
            if op.dma:
                return True
            return STRICT or (op.pos - d.pos <= 2)
        return True

    def finalize(self):
        for e in self.ENG:
            for op in self.q[e]:
                for d in op.deps:
                    if (not d.dma) and self._needs_wait(op, d):
                        d.ms = True
        self.dma_count = {}
        for e in self.ENG:
            c = 0
            j = 0
            npool = N_DMA_SP if e == "sp" else N_DMA_POOL
            for op in self.q[e]:
                if op.dma:
                    op.sem = j % npool
                    op.val = 16 * (j // npool + 1)
                    j += 1
                elif op.ms:
                    c += 1
                    op.msn = c
            self.dma_count[e] = j
            assert c <= SEM_M * SEM_K, (e, c)

    def emit(self, e, E, sems, dsems):
        waited = {}

        def wait(sem, val):
            key = id(sem)
            if waited.get(key, 0) >= val:
                return
            E.wait_ge(sem, val)
            waited[key] = val

        def semfor(eng, n):
            return sems[eng][(n - 1) // SEM_M], (n - 1) % SEM_M + 1

        for op in self.q[e]:
            for d in op.deps:
                if not self._needs_wait(op, d):
                    continue
                if d.dma:
                    wait(dsems[d.eng][d.sem], d.val)
                else:
                    s, v = semfor(d.eng, d.msn)
                    wait(s, v)
            if op.dma:
                if op.val > 16:
                    wait(dsems[e][op.sem], op.val - 16)
                ins = op.fn(E)
                ins.then_inc(dsems[e][op.sem], 16)
            else:
                ins = op.fn(E)
                if op.ms:
                    s, v = semfor(e, op.msn)
                    ins.then_inc(s, 1)
        j = self.dma_count.get(e, 0)
        npool = N_DMA_SP if e == "sp" else N_DMA_POOL
        for si in range(min(j, npool)):
            last = (j - 1 - si) // npool * npool + si
            if last < 0:
                continue
            wait(dsems[e][si], 16 * (last // npool + 1))


def build_program(nt=NT, phase2=True, dbg=False):
    nc = bass.Bass("TRN2", target_bir_lowering=False)

    def din(name, shape, dt=F32):
        return nc.dram_tensor(name, list(shape), dt, kind="ExternalInput").ap()

    x = din("x", [NT * 128, D])
    cT = din("cT", [128, 16])
    w_ada = din("w_ada", [D, 6 * D])
    b_ada = din("b_ada", [6 * D])
    n1g = din("n1g", [128, 8])
    n2g = din("n2g", [128, 8])
    w_in = din("w_in", [D, IN_COLS])
    cw = din("cw", [128, 64])
    cb = din("cb", [128, 16])
    gate_b = din("gate_b", [8])
    ln_g = din("ln_g", [D])
    ln_b = din("ln_b", [D])
    hn_g = din("hn_g", [D])
    final_g = din("final_g", [D])
    wsT = din("wsT", [128, 8 * 128])
    bs_tok = din("bs_tok", [128, 8])
    w_out = din("w_out", [D, D])
    w_ff1 = din("w_ff1", [D, 4 * D])
    w_ff2 = din("w_ff2", [4 * D, D])
    cf = din("cf", [128, 4 * 128])
    cb16 = din("cb16", [128, 2 * 128], BF16)
    out = nc.dram_tensor("out", [NT * 128, D], F32, kind="ExternalOutput").ap()
    if dbg:
        merged_d = nc.dram_tensor("merged_d", [NT * 128, D], BF16, kind="ExternalOutput").ap()
        mod_d = nc.dram_tensor("mod_d", [2, 6 * D], F32, kind="ExternalOutput").ap()
    else:
        merged_d = nc.dram_tensor("merged_d", [NT * 128, D], BF16, kind="Internal").ap()
        mod_d = nc.dram_tensor("mod_d", [2, 6 * D], F32, kind="Internal").ap()

    P = Prog()
    es = ExitStack()
    ARENA_BYTES = 212800
    arena = es.enter_context(nc.sbuf_tensor("arena", [128, ARENA_BYTES // 2], BF16))
    psum = [es.enter_context(nc.psum_tensor(f"psb{i}", [128, 512], F32)) for i in range(8)]

    class Alloc:
        def __init__(self, base=0):
            self.off = base

        def take(self, nbytes):
            o = (self.off + 7) // 8 * 8
            self.off = o + nbytes
            assert self.off <= ARENA_BYTES, self.off
            return o

        def bf(self, shape):
            n = int(np.prod(shape[1:]))
            o = self.take(n * 2)
            return self._shape(arena[:, o // 2:o // 2 + n], shape)

        def f32(self, shape):
            n = int(np.prod(shape[1:]))
            o = self.take(n * 4)
            return self._shape(arena[:, o // 2:o // 2 + 2 * n].bitcast(F32), shape)

        @staticmethod
        def _shape(v, shape):
            if len(shape) == 3:
                return v.rearrange("p (a b) -> p a b", a=shape[1])
            if len(shape) == 4:
                return v.rearrange("p (a b c) -> p a b c", a=shape[1], b=shape[2])
            return v

    def psf(i, lo=0, n=512):
        return psum[i][:, lo:lo + n]

    def psb(i, lo=0, n=1024):
        return psum[i][:].bitcast(BF16)[:, lo:lo + n]

    A0 = Alloc(0)
    CF = A0.f32([128, 4, 128])
    CB = A0.bf([128, 2, 128])
    identf, Umat, onesf, mask2 = CF[:, 0, :], CF[:, 1, :], CF[:, 2, :], CF[:, 3, :]
    identb, maskT = CB[:, 0, :], CB[:, 1, :]
    modT = A0.f32([128, 96])
    n1g_s = A0.f32([128, 8])
    n2g_s = A0.f32([128, 8])
    Gsh = A0.f32([128, 4, 16])
    mhalf = A0.f32([128, 4])
    onesb = A0.bf([128, 4])
    scr = {e: A0.f32([128, 4]) for e in ("act", "dve", "pool")}
    base1 = A0.off

    def mm(outap, lhsT, rhs, start, stop, r=(), w=()):
        return P.add("pe", lambda E: E.matmul(outap, lhsT=lhsT, rhs=rhs, start=start, stop=stop), r=r, w=w)

    def tr(outap, inap, ident, r=(), w=()):
        return P.add("pe", lambda E: E.transpose(outap, inap, ident), r=r, w=w)

    def act(outap, inap, func, r=(), w=(), scale=1.0, bias=None):
        def fn(E):
            if bias is not None:
                return E.activation(out=outap, in_=inap, func=func, scale=scale, bias=bias)
            return E.activation(out=outap, in_=inap, func=func, scale=scale)
        return P.add("act", fn, r=r, w=w)

    def ts(eng, outap, in0, s1, s2, op0, op1=None, r=(), w=()):
        def fn(E):
            if op1 is None:
                return E.tensor_scalar(out=outap, in0=in0, scalar1=s1, scalar2=None, op0=op0)
            return E.tensor_scalar(out=outap, in0=in0, scalar1=s1, scalar2=s2, op0=op0, op1=op1)
        return P.add(eng, fn, r=r, w=w)

    def tt(eng, outap, in0, in1, op, r=(), w=()):
        return P.add(eng, lambda E: E.tensor_tensor(out=outap, in0=in0, in1=in1, op=op), r=r, w=w)

    def stt(outap, in0, scalar, in1, op0, op1, r=(), w=()):
        return P.add("dve", lambda E: E.scalar_tensor_tensor(out=outap, in0=in0, scalar=scalar, in1=in1,
                                                             op0=op0, op1=op1), r=r, w=w)

    def dma(eng, outap, inap, r=(), w=()):
        return P.add(eng, lambda E: E.dma_start(out=outap, in_=inap), r=r, w=w, dma=True)

    def memset(eng, ap, val, r=(), w=()):
        return P.add(eng, lambda E: E.memset(ap, val), r=r, w=w)

    def sumsq(src, junk, dst, rk, jk, dk):
        def fn(E):
            return E.activation(out=junk, in_=src, func=AF.Square, accum_out=dst)
        P.add("act", fn, r=rk, w=[jk, dk + "_raw"])
        P.add("act", lambda E: E.activation(out=scr["act"][:, 0:1], in_=scr["act"][:, 1:2], func=AF.Copy),
              r=[dk + "_raw"], w=[dk, "scr_act"])

    dma("sp", CF, cf.rearrange("p (a b) -> p a b", a=4), w=["CF"])
    dma("sp", CB, cb16.rearrange("p (a b) -> p a b", a=2), w=["CB"])
    dma("sp", n1g_s, n1g, w=["n1g"])
    dma("sp", n2g_s, n2g, w=["n2g"])
    memset("pool", mhalf, -0.5, w=["mhalf"])
    memset("pool", onesb, 1.0, w=["onesb"])
    memset("pool", scr["act"], 0.0, w=["scr_act"])

    A1 = Alloc(base1)
    WIN = A1.bf([128, 8, IN_COLS])
    wsTm = A1.bf([128, 8, 128])
    bst = A1.f32([128, 8])
    cw_s = A1.f32([128, 16, 4])
    cb_s = A1.f32([128, 16])
    gb8 = A1.f32([128, 8])
    lng = A1.f32([128, D])
    lnb = A1.f32([128, D])
    hng = A1.f32([128, D])
    Cst = A1.f32([128, 4, 512])
    nst = A1.f32([128, 4, 2])
    Cdec = A1.bf([128, 4, 512])
    vg = Cdec.rearrange("p a b -> p (a b)").bitcast(F32)
    ndec = A1.bf([128, 4, 2])
    carry = A1.f32([128, 4])
    Rprev = A1.f32([128, 4])
    hal = A1.f32([128, 16, 3])
    pq4 = [A1.f32([128, 4, 131]) for _ in range(2)]
    xt = A1.f32([128, D])
    xhat = A1.bf([128, D])
    xnT2 = [A1.bf([128, 8, 128]) for _ in range(2)]
    Vb = A1.bf([128, D])
    qkT2 = [A1.bf([128, 16, 128]) for _ in range(2)]
    acc4s = [A1.f32([128, 4, 128]) for _ in range(2)]
    th4s = [A1.f32([128, 4, 128]) for _ in range(2)]
    wk = A1.bf([128, D])
    PT = A1.bf([128, 4, 128])
    yb = A1.f32([128, D])
    tmpA = [A1.f32([128, 512]) for _ in range(2)]
    tmpB = [A1.f32([128, 512]) for _ in range(2)]
    mg = A1.bf([128, D])
    vn = mg
    sm = A1.f32([128, 160])
    ssq, mse, rstd = sm[:, 0:1], sm[:, 1:2], sm[:, 2:3]
    g8 = sm[:, 8:16]
    li, fp = g8[:, 0:4], g8[:, 4:8]
    af, ee, dd, uu = sm[:, 16:20], sm[:, 20:24], sm[:, 24:28], sm[:, 28:32]
    s2, tq, mn, lf = sm[:, 32:36], sm[:, 36:40], sm[:, 40:44], sm[:, 44:48]
    Bt, ah, amax, Rnew = sm[:, 48:52], sm[:, 52:56], sm[:, 56:60], sm[:, 60:64]
    eargs, eout = sm[:, 64:76], sm[:, 76:88]
    wv, flv, decv = eout[:, 0:4], eout[:, 4:8], eout[:, 8:12]
    den4, rr = sm[:, 88:92], sm[:, 92:96]
    bst6 = sm[:, 96:120].rearrange("p (a b) -> p a b", a=4)
    mv4 = sm[:, 120:128].rearrange("p (a b) -> p a b", a=4)
    rs4 = sm[:, 128:132]
    vst, vmv, vrs = sm[:, 132:144], sm[:, 144:146], sm[:, 146:147]
    sm2 = A1.f32([128, 16])
    uu2 = sm2[:, 0:4]
    eouts = [sm[:, 76:88], sm2[:, 4:16]]
    nb4 = sm[:, 148:152]
    nmean = sm[:, 152:153]
    end1 = A1.off

    wsT_f = tmpA[0].rearrange("p (a b) -> p a b", a=4)
    wsT_v = wsT.rearrange("p (a b) -> p a b", a=8)
    for hlf in range(2):
        dma("sp", wsT_f, wsT_v[:, hlf * 4:(hlf + 1) * 4, :], w=["tmpA0"])
        tt("dve", wsTm[:, hlf * 4:(hlf + 1) * 4, :], wsT_f, mask2.unsqueeze(1).to_broadcast([128, 4, 128]),
           ALU.mult, r=["tmpA0", "CF"], w=["wsTm"])
    dma("sp", bst, bs_tok, w=["bst"])
    dma("sp", cw_s, cw.rearrange("p (a b) -> p a b", a=16), w=["cw"])
    dma("sp", cb_s, cb, w=["cbs"])
    dma("sp", gb8, gate_b.partition_broadcast(128), w=["gb8"])
    dma("sp", lng, ln_g.partition_broadcast(128), w=["lng"])
    dma("sp", lnb, ln_b.partition_broadcast(128), w=["lnb"])
    dma("sp", hng, hn_g.partition_broadcast(128), w=["hng"])

    csT = tmpA[1][:, 0:16]
    cth = tmpA[1][:, 16:32]
    dma("sp", csT, cT, w=["csT"])
    act(cth, csT, AF.Tanh, scale=0.5, r=["csT"], w=["cth"])
    stt(cth, cth, 1.0, csT, ALU.add, ALU.mult, r=["cth", "csT"], w=["cth"])
    ts("dve", csT, cth, 0.5, None, ALU.mult, r=["cth"], w=["csT"])
    csT3 = csT.rearrange("p (a b) -> p a b", a=8)
    w_ada_v = w_ada.rearrange("(kc p) n -> p kc n", p=128)
    wst = [WIN[:, 6, 0:8192].bitcast(F32).rearrange("p (a b) -> p a b", a=8),
           WIN[:, 7, 0:8192].bitcast(F32).rearrange("p (a b) -> p a b", a=8)]
    wkeys = ["win6", "win7"]
    brow = [tmpB[1][0:2, 0:512], tmpB[1][0:2, 0:512]]
    mrow = [tmpA[0][0:2, 0:512], tmpB[0][0:2, 0:512]]
    for j in range(12):
        sb = j % 2
        dma("sp", wst[sb], w_ada_v[:, :, j * 512:(j + 1) * 512], w=[wkeys[sb]])
        dma("sp", brow[sb], b_ada[j * 512:(j + 1) * 512].partition_broadcast(2), w=["tmpB1"])
        for kc in range(8):
            mm(psf(1 + sb)[0:2, :], csT3[:, kc, :], wst[sb][:, kc, :], kc == 0, kc == 7,
               r=[wkeys[sb], "csT"], w=[f"ps{1 + sb}"])
        mk = "tmpA0" if sb == 0 else "tmpB0"
        tt("dve", mrow[sb], psf(1 + sb)[0:2, :], brow[sb], ALU.add, r=[f"ps{1 + sb}", "tmpB1"], w=[mk])
        dma("sp", mod_d[:, j * 512:(j + 1) * 512], mrow[sb], r=[mk], w=["mod_d"])
    modin = tmpB[1][0:96, 0:128]
    dma("sp", modin, mod_d.rearrange("b (c p) -> (b c) p", p=128), r=["mod_d"], w=["tmpB1"])
    tr(psf(1)[:, 0:96], modin, identf[0:96, 0:96], r=["tmpB1", "CF"], w=["ps1"])
    P.add("dve", lambda E: E.tensor_copy(out=modT, in_=psf(1)[:, 0:96]), r=["ps1"], w=["modT"])
    for b in range(2):
        o = b * 48
        stt(Gsh[:, 0, b * 8:(b + 1) * 8], modT[:, o + 8:o + 16], 1.0, n1g_s, ALU.add, ALU.mult,
            r=["modT", "n1g"], w=["Gsh"])
        P.add("dve", lambda E, b=b, o=o: E.tensor_copy(out=Gsh[:, 1, b * 8:(b + 1) * 8], in_=modT[:, o:o + 8]),
              r=["modT"], w=["Gsh"])
        stt(Gsh[:, 2, b * 8:(b + 1) * 8], modT[:, o + 32:o + 40], 1.0, n2g_s, ALU.add, ALU.mult,
            r=["modT", "n2g"], w=["Gsh"])
        P.add("dve", lambda E, b=b, o=o: E.tensor_copy(out=Gsh[:, 3, b * 8:(b + 1) * 8], in_=modT[:, o + 24:o + 32]),
              r=["modT"], w=["Gsh"])

    w_in_v = w_in.rearrange("(kc p) n -> p kc n", p=128)
    WGRP = [(OFF_Q, OFF_MV), (OFF_MV, OFF_GA), (0, OFF_Q), (OFF_GA, IN_COLS)]

    def wkey(c0):
        for gi, (lo, hi) in enumerate(WGRP):
            if lo <= c0 < hi:
                return f"winG{gi}"
        raise ValueError(c0)

    for gi, (lo, hi) in enumerate(WGRP):
        dma("pool", WIN[:, 0:3, lo:hi], w_in_v[:, 0:3, lo:hi], w=[f"winG{gi}"])
        dma("pool", WIN[:, 3:6, lo:hi], w_in_v[:, 3:6, lo:hi], w=[f"winG{gi}"])
    for gi, (lo, hi) in enumerate(WGRP):
        for kc in (6, 7):
            dma("pool", WIN[:, kc, lo:hi], w_in_v[:, kc, lo:hi], w=[f"winG{gi}", f"win{kc}"])

    tokbank = [0]

    def xn(i):
        return xnT2[i % 2], f"xnT{i % 2}"

    def tokblk(i, c0):
        bank = nextbank()
        X, xk = xn(i)
        for kc in range(8):
            mm(psf(bank), X[:, kc, :], WIN[:, kc, c0:c0 + 512], kc == 0, kc == 7,
               r=[xk, wkey(c0)], w=[f"ps{bank}"])
        return bank

    def eo(i):
        e = eouts[i % 2]
        return e, e[:, 0:4], e[:, 4:8], e[:, 8:12], f"eout{i % 2}"

    rot = [1, 2]

    def nextbank():
        tokbank[0] += 1
        return rot[tokbank[0] % len(rot)]

    def front(i):
        b = i // NT_SEQ
        X, xk = xn(i)
        dma("sp", xt, x[i * 128:(i + 1) * 128, :], w=["xt"])
        sumsq(xt, xhat, ssq, ["xt"], "xhat", "ssq")
        ts("dve", mse, ssq, 1.0 / D, EPS, ALU.mult, ALU.add, r=["ssq"], w=["mse"])
        tt("pool", rstd, mse, mhalf[:, 0:1], ALU.pow, r=["mse", "mhalf"], w=["rstd"])
        act(xhat, xt, AF.Identity, scale=rstd, r=["xt", "rstd"], w=["xhat"])
        yield

    def front_b(i):
        b = i // NT_SEQ
        X, xk = xn(i)
        for kc in range(8):
            tr(psb(0, kc * 128, 128), xhat[:, kc * 128:(kc + 1) * 128], identb, r=["xhat", "CB"], w=["ps0"])
        for kc in range(8):
            act(X[:, kc, :], psb(0, kc * 128, 128), AF.Identity, scale=Gsh[:, 0, b * 8 + kc:b * 8 + kc + 1],
                bias=Gsh[:, 1, b * 8 + kc:b * 8 + kc + 1], r=["ps0", "Gsh"], w=[xk])
        yield

    def qkconv(i):
        X, xk = xn(i)
        qkT = qkT2[i % 2]
        qk = f"qkT{i % 2}"
        if i % NT_SEQ == 0:
            memset("pool", hal, 0.0, w=["hal"])

        def names(g4):
            return (3 + g4 % 2, pq4[g4 % 2], f"pq4_{g4 % 2}", acc4s[g4 % 2], th4s[g4 % 2], f"a{g4 % 2}_",
                    f"th4{'' if g4 % 2 == 0 else 'b'}")

        def stA(g4):
            bank, buf, bk, acc4, th4c, ak, tk = names(g4)
            for j in range(4):
                blk = g4 * 4 + j
                for kc in range(8):
                    mm(psf(bank, j * 128, 128), WIN[:, kc, OFF_Q + blk * 128:OFF_Q + (blk + 1) * 128], X[:, kc, :],
                       kc == 0, kc == 7, r=[xk, "winG0"], w=[f"ps{bank}"])
            P.add("pool", lambda E, buf=buf, g4=g4: E.tensor_copy(out=buf[:, :, 0:3], in_=hal[:, g4 * 4:(g4 + 1) * 4, :]),
                  r=["hal"], w=[bk + "h"])
            act(buf[:, :, 3:131], psf(bank).rearrange("p (a b) -> p a b", a=4), AF.Copy, r=[f"ps{bank}"], w=[bk])
            P.add("pool", lambda E, buf=buf, g4=g4: E.tensor_copy(out=hal[:, g4 * 4:(g4 + 1) * 4, :], in_=buf[:, :, 128:131]),
                  r=[bk], w=["hal"])
            for j in range(4):
                blk = g4 * 4 + j
                act(acc4[:, j, :], buf[:, j, 3:131], AF.Identity, scale=cw_s[:, blk, 3:4], bias=cb_s[:, blk:blk + 1],
                    r=[bk, "cw", "cbs"], w=[f"{ak}{j}"])

        def stB(g4):
            bank, buf, bk, acc4, th4c, ak, tk = names(g4)
            for tap in range(3):
                for j in range(4):
                    blk = g4 * 4 + j
                    stt(acc4[:, j, :], buf[:, j, tap:tap + 128], cw_s[:, blk, tap:tap + 1], acc4[:, j, :],
                        ALU.mult, ALU.add, r=[bk, bk + "h", "cw", f"{ak}{j}"], w=[f"{ak}{j}"])

        def stC(g4):
            bank, buf, bk, acc4, th4c, ak, tk = names(g4)
            act(th4c, acc4, AF.Tanh, scale=0.5, r=[f"{ak}{j}" for j in range(4)], w=[tk])
            stt(qkT[:, g4 * 4:(g4 + 1) * 4, :], th4c, 1.0, acc4, ALU.add, ALU.mult,
                r=[tk] + [f"{ak}{j}" for j in range(4)], w=[qk])

        for st, g in ((stA, 0), (stA, 1), (stB, 0), (stC, 0), (stB, 1), (stA, 2), (stC, 1), (stB, 2), (stA, 3),
                      (stC, 2), (stB, 3), (stC, 3)):
            st(g)
            yield

    def vblocks(i):
        for half in range(2):
            bank = tokblk(i, OFF_MV + half * 512)
            act(Vb[:, half * 512:(half + 1) * 512], psf(bank), AF.Copy, r=[f"ps{bank}"], w=["Vb"])
            yield

    def gates(i):
        X, xk = xn(i)
        eout, wv, flv, decv, ek = eo(i)
        if i % NT_SEQ == 0:
            memset("pool", carry, 0.0, w=["carry"])
            memset("pool", Rprev, 0.0, w=["Rprev"])
        for kc in range(8):
            mm(psf(5, 300, 8), X[:, kc, :], WIN[:, kc, OFF_I:OFF_I + 8], kc == 0, kc == 7,
               r=[xk, "winG1"], w=["ps5"])
        tt("dve", g8, psf(5, 300, 8), gb8, ALU.add, r=["ps5", "gb8"], w=["g8"])
        yield
        stt(af, fp, -1.0, fp, ALU.mult, ALU.max, r=["g8"], w=["af"])
        act(ee, af, AF.Exp, scale=-1.0, r=["af"], w=["ee"])
        ts("dve", mn, fp, 0.0, None, ALU.min, r=["g8"], w=["mn"])
        yield
        ts("dve", dd, ee, 2.0, None, ALU.add, r=["ee"], w=["dd"])
        P.add("dve", lambda E: E.reciprocal(out=uu2, in_=dd), r=["dd"], w=["uu2"])
        tt("dve", uu, ee, uu2, ALU.mult, r=["ee", "uu2"], w=["uu"])
        tt("dve", s2, uu, uu, ALU.mult, r=["uu"], w=["s2"])
        yield
        ts("dve", tq, s2, 1.0 / 13.0, None, ALU.mult, r=["s2"], w=["tq"])
        for cc in (1.0 / 11, 1.0 / 9, 1.0 / 7, 1.0 / 5, 1.0 / 3):
            stt(tq, tq, cc, s2, ALU.add, ALU.mult, r=["tq", "s2"], w=["tq"])
            yield
        stt(tq, tq, 1.0, uu, ALU.add, ALU.mult, r=["tq", "uu"], w=["tq"])
        stt(lf, tq, -2.0, mn, ALU.mult, ALU.add, r=["tq", "mn"], w=["lf"])
        yield
        yield
        mm(psf(5, 308, 4), Umat, lf, True, True, r=["lf", "CF"], w=["ps5"])
        mm(psf(5, 312, 4), onesf, lf, True, True, r=["lf", "CF"], w=["ps5"])
        yield
        tt("dve", Bt, psf(5, 308, 4), carry, ALU.add, r=["ps5", "carry"], w=["Bt"])
        tt("dve", carry, psf(5, 312, 4), carry, ALU.add, r=["ps5", "carry"], w=["carry"])
        tt("dve", ah, li, Bt, ALU.subtract, r=["g8", "Bt"], w=["ah"])
        Dmb = tmpA[0].rearrange("p (a b) -> p a b", a=4)
        tt("dve", Dmb, identf.unsqueeze(1).to_broadcast([128, 4, 128]),
           ah.unsqueeze(2).to_broadcast([128, 4, 128]), ALU.mult, r=["CF", "ah"], w=["tmpA0"])
        yield
        yield
        bank = nextbank()
        mm(psf(bank), onesf, tmpA[0], True, True, r=["tmpA0", "CF"], w=[f"ps{bank}"])
        P.add("dve", lambda E, bank=bank: E.tensor_reduce(out=amax, in_=psf(bank).rearrange("p (a b) -> p a b", a=4),
                                                          axis=AX.X, op=ALU.max), r=[f"ps{bank}"], w=["amax"])
        yield
        tt("dve", Rnew, amax, Rprev, ALU.max, r=["amax", "Rprev"], w=["Rnew"])
        tt("dve", eargs[:, 0:4], ah, Rnew, ALU.subtract, r=["ah", "Rnew"], w=["ea0"])
        stt(eargs[:, 4:8], Bt, -1.0, Rnew, ALU.mult, ALU.subtract, r=["Bt", "Rnew"], w=["ea1"])
        ts("dve", eargs[:, 4:8], eargs[:, 4:8], math.log(64.0), None, ALU.add, r=["ea1"], w=["ea1"])
        tt("dve", eargs[:, 8:12], Rprev, Rnew, ALU.subtract, r=["Rprev", "Rnew"], w=["ea2"])
        act(eout, eargs, AF.Exp, r=["ea0", "ea1", "ea2"], w=[ek])
        P.add("pool", lambda E: E.tensor_copy(out=Rprev, in_=Rnew), r=["Rnew", "ea2"], w=["Rprev"])
        yield

    def gmlp(i):
        for half in range(2):
            bank = tokblk(i, OFF_V + half * 512)
            act(vg[:, half * 512:(half + 1) * 512], psf(bank), AF.Gelu_apprx_tanh, r=[f"ps{bank}"], w=["Cdec"])
            yield
        P.add("dve", lambda E: E.bn_stats(out=vst[:, 0:6], in_=vg[:, 0:512]), r=["Cdec"], w=["vst0"])
        P.add("dve", lambda E: E.bn_stats(out=vst[:, 6:12], in_=vg[:, 512:1024]), r=["Cdec"], w=["vst1"])
        P.add("dve", lambda E: E.bn_aggr(out=vmv, in_=vst), r=["vst0", "vst1"], w=["vmv"])
        ts("dve", vrs, vmv[:, 1:2], EPS, None, ALU.add, r=["vmv"], w=["vrs"])
        tt("pool", vrs, vrs, mhalf[:, 0:1], ALU.pow, r=["vrs", "mhalf"], w=["vrs"])
        yield
        stt(vg, vg, vmv[:, 0:1], lng, ALU.subtract, ALU.mult, r=["Cdec", "vmv", "lng"], w=["Cdec"])
        yield
        stt(vn, vg, vrs, lnb, ALU.mult, ALU.add, r=["Cdec", "vrs", "lnb"], w=["mg"])
        yield
        yield
        for g in range(8):
            mm(psf(3 + g // 4, (g % 4) * 128, 128), wsTm[:, g, :], vn[:, g * 128:(g + 1) * 128], True, True,
               r=["wsTm", "mg"], w=[f"ps{3 + g // 4}"])
        yield
        for half in range(2):
            bank = tokblk(i, OFF_U + half * 512)
            act(tmpA[half], psf(bank), AF.Gelu_apprx_tanh, r=[f"ps{bank}"], w=[f"tmpA{half}"])
            for gg in range(4):
                g = half * 4 + gg
                stt(tmpB[half][:, gg * 128:(gg + 1) * 128], psf(3 + half, gg * 128, 128), bst[:, g:g + 1],
                    tmpA[half][:, gg * 128:(gg + 1) * 128], ALU.add, ALU.mult,
                    r=[f"ps{3 + half}", "bst", f"tmpA{half}"], w=[f"tmpB{half}"])
            yield
            bank = tokblk(i, OFF_GA + half * 512)
            act(tmpA[half], psf(bank), AF.Tanh, scale=0.5, r=[f"ps{bank}"], w=[f"tmpA{half}"])
            stt(tmpB[half], tmpA[half], 1.0, tmpB[half], ALU.add, ALU.mult, r=[f"tmpA{half}", f"tmpB{half}"],
                w=[f"tmpB{half}"])
            yield

    def mlstm_core(i):
        eout, wv, flv, decv, ek = eo(i)
        qkT = qkT2[i % 2]
        qk = f"qkT{i % 2}"
        if i % NT_SEQ == 0:
            memset("pool", Cst, 0.0, w=["C"])
            memset("pool", nst, 0.0, w=["n"])
        for h in range(4):
            act(Cdec[:, h, :], Cst[:, h, :], AF.Identity, scale=decv[:, h:h + 1], r=["C", ek], w=["Cdec"])
        tt("dve", ndec, nst, decv.unsqueeze(2).to_broadcast([128, 4, 2]), ALU.mult, r=["n", ek], w=["ndec"])
        yield
        for kb in range(8):
            tr(psb(0, kb * 128, 128), qkT[:, 8 + kb, :], identb, r=[qk, "CB"], w=["ps0"])
        for h in range(4):
            act(wk[:, h * 256:(h + 1) * 256], psb(0, h * 256, 256), AF.Identity, scale=wv[:, h:h + 1],
                r=["ps0", ek], w=["wk"])
        yield
        for hp in range(2):
            for hh in range(2):
                h = 2 * hp + hh
                for kc in range(2):
                    mm(psf(5, hh * 128, 128), qkT[:, 8 + 2 * h + kc, :], qkT[:, 2 * h + kc, :], kc == 0, kc == 1,
                       r=[qk], w=["ps5"])
            for hh in range(2):
                h = 2 * hp + hh
                stt(PT[:, h, :], psf(5, hh * 128, 128), wv[:, h:h + 1], maskT, ALU.mult, ALU.mult,
                    r=["ps5", ek, "CB"], w=["PT"])
            yield
            for hh in range(2):
                h = 2 * hp + hh
                nb = 6 + hp
                Vh = Vb[:, h * 256:(h + 1) * 256]
                mm(psf(nb, hh * 256, 256), PT[:, h, :], Vh, True, False, r=["PT", "Vb"], w=[f"ps{nb}"])
                mm(psf(nb, hh * 256, 256), qkT[:, 2 * h, :], Cdec[:, h, 0:256], False, False, r=[qk, "Cdec"], w=[f"ps{nb}"])
                mm(psf(nb, hh * 256, 256), qkT[:, 2 * h + 1, :], Cdec[:, h, 256:512], False, True, r=[qk, "Cdec"], w=[f"ps{nb}"])
                mm(psf(5, 256 + h, 1), PT[:, h, :], onesb[:, 0:1], True, False, r=["PT", "onesb"], w=["ps5"])
                mm(psf(5, 256 + h, 1), qkT[:, 2 * h, :], ndec[:, h, 0:1], False, False, r=[qk, "ndec"], w=["ps5"])
                mm(psf(5, 256 + h, 1), qkT[:, 2 * h + 1, :], ndec[:, h, 1:2], False, True, r=[qk, "ndec"], w=["ps5"])
                bc = nextbank()
                for kc in range(2):
                    mm(psf(bc, kc * 256, 256), wk[:, h * 256 + kc * 128:h * 256 + (kc + 1) * 128], Vh, True, True,
                       r=["wk", "Vb"], w=[f"ps{bc}"])
                for kc in range(2):
                    mm(psf(5, 260 + 2 * h + kc, 1), wk[:, h * 256 + kc * 128:h * 256 + (kc + 1) * 128], onesb[:, 0:1],
                       True, True, r=["wk", "onesb"], w=["ps5"])
                stt(Cst[:, h, :], Cst[:, h, :], decv[:, h:h + 1], psf(bc), ALU.mult, ALU.add,
                    r=["C", ek, f"ps{bc}", "Cdec"], w=["C"])
                stt(nst[:, h, :], nst[:, h, :], decv[:, h:h + 1], psf(5, 260 + 2 * h, 2), ALU.mult, ALU.add,
                    r=["n", ek, "ps5", "ndec"], w=["n"])
                P.add("dve", lambda E, h=h, nb=nb, hh=hh: E.bn_stats(out=bst6[:, h, :], in_=psf(nb, hh * 256, 256)),
                      r=[f"ps{nb}"], w=[f"bst{h}"])
                yield

    def out_thread(i):
        eout, wv, flv, decv, ek = eo(i)
        P.add("dve", lambda E: E.tensor_copy(out=den4, in_=psf(5, 256, 4)), r=["ps5"], w=["den4"])
        stt(den4, den4, -1.0, den4, ALU.mult, ALU.max, r=["den4"], w=["den4"])
        tt("dve", den4, den4, flv, ALU.max, r=["den4", ek], w=["den4"])
        for h in range(4):
            P.add("dve", lambda E, h=h: E.bn_aggr(out=mv4[:, h, :], in_=bst6[:, h, :]), r=[f"bst{h}"], w=[f"mv{h}"])
        yield
        P.add("dve", lambda E: E.reciprocal(out=rr, in_=den4), r=["den4"], w=["rr"])
        tt("dve", rs4, rr, rr, ALU.mult, r=["rr"], w=["rs4"])
        tt("dve", rs4, rs4, mv4[:, :, 1], ALU.mult, r=["rs4"] + [f"mv{h}" for h in range(4)], w=["rs4"])
        ts("dve", rs4, rs4, EPS, None, ALU.add, r=["rs4"], w=["rs4"])
        tt("pool", rs4, rs4, mhalf, ALU.pow, r=["rs4", "mhalf"], w=["rs4"])
        yield
        stt(rs4, rs4, 0.25, rr, ALU.mult, ALU.mult, r=["rs4", "rr"], w=["rs4"])
        stt(nb4, mv4[:, :, 0], -1.0, rs4, ALU.mult, ALU.mult, r=["rs4"] + [f"mv{h}" for h in range(4)], w=["nb4"])
        for h in range(4):
            nb = 6 + h // 2
            act(yb[:, h * 256:(h + 1) * 256], psf(nb, (h % 2) * 256, 256), AF.Identity, scale=rs4[:, h:h + 1],
                bias=nb4[:, h:h + 1], r=[f"ps{nb}", "nb4", "rs4"], w=["yb"])
        tt("dve", yb, yb, hng, ALU.mult, r=["yb", "hng"], w=["yb"])
        rot[:] = [1, 2, 0, 6, 7]
        yield
        for c0 in (OFF_O, OFF_GB):
            for half in range(2):
                bank = tokblk(i, c0 + half * 512)
                act(tmpA[half], psf(bank), AF.Tanh, scale=0.5, r=[f"ps{bank}"], w=[f"tmpA{half}"])
                stt(yb[:, half * 512:(half + 1) * 512], tmpA[half], 1.0, yb[:, half * 512:(half + 1) * 512],
                    ALU.add, ALU.mult, r=[f"tmpA{half}", "yb"], w=["yb"])
                yield

    def merge(i):
        for half in range(2):
            hs = slice(half * 512, (half + 1) * 512)
            stt(mg[:, hs], tmpB[half], 0.5, yb[:, hs], ALU.mult, ALU.add, r=[f"tmpB{half}", "yb"], w=["mg"])
        dma("sp", merged_d[i * 128:(i + 1) * 128, :], mg, r=["mg"], w=[f"mgd{i}"])
        yield

    def chain(*gens):
        for g in gens:
            yield from g

    def interleave(*gens):
        gens = list(gens)
        while gens:
            for g in list(gens):
                try:
                    next(g)
                except StopIteration:
                    gens.remove(g)

    def zipg(*gens):
        gens = list(gens)
        while gens:
            for g in list(gens):
                try:
                    next(g)
                    yield
                except StopIteration:
                    gens.remove(g)

    interleave(chain(front(0), front_b(0), zipg(qkconv(0), gates(0)), vblocks(0)))
    if nt > 1:
        interleave(front(1))
    for i in range(nt):
        rot[:] = [1, 2]
        if i + 1 < nt:
            interleave(mlstm_core(i), chain(front_b(i + 1), gates(i + 1)), qkconv(i + 1))
            rot[:] = [1, 2, 0]
            if i + 2 < nt:
                interleave(out_thread(i), gmlp(i), vblocks(i + 1), front(i + 2))
            else:
                interleave(out_thread(i), gmlp(i), vblocks(i + 1))
        else:
            interleave(mlstm_core(i))
            rot[:] = [1, 2, 0]
            interleave(out_thread(i), gmlp(i))
        interleave(merge(i))

    P.add("act", lambda E: E.activation(out=scr["act"][:, 0:1], in_=scr["act"][:, 1:2], func=AF.Copy), w=["end_act", "scr_act"])
    memset("dve", scr["dve"], 0.0, w=["end_dve"])
    memset("pool", scr["pool"], 0.0, w=["end_pool"])
    tr(psb(0, 0, 128), identb, identb, r=["CB"], w=["ps0", "end_pe"])
    ENDK = ["end_act", "end_dve", "end_pool", "end_pe"] + [f"mgd{i}" for i in range(nt)]

    if phase2:
        A2 = Alloc(base1)
        WOUT = A2.bf([128, 8, D])
        FF1 = A2.bf([128, 8, 4 * D])
        FF2 = A2.bf([128, 32, D])
        g1b = A2.f32([128, D])
        g2b = A2.f32([128, D])
        fgb = A2.f32([128, D])
        xt2 = [A2.f32([128, D]) for _ in range(2)]
        mg2 = [A2.bf([128, D]) for _ in range(2)]
        mgT = A2.bf([128, 8, 128])
        xhat2 = A2.bf([128, D])
        xn2Ts = [A2.bf([128, 8, 128]) for _ in range(2)]
        hidT = A2.bf([128, 32, 128])
        tmpRA = [A2.f32([128, 512]) for _ in range(2)]
        tmpRB = [A2.f32([128, 512]) for _ in range(2)]
        tmpH = [A2.f32([128, 512]) for _ in range(2)]
        junkB = tmpH[0].bitcast(BF16)
        outb = A2.f32([128, D])
        sq = A2.f32([128, 16])
        ssq2, mse2, rstd2, ssq3, mse3, rstd3 = (sq[:, k:k + 1] for k in range(6))
        P.add("act", lambda E: E.activation(out=scr["act"][:, 0:1], in_=scr["act"][:, 1:2], func=AF.Copy), r=ENDK, w=["scr_act"])
        memset("dve", scr["dve"], 0.0, r=ENDK, w=["bar_dve"])
        memset("pool", scr["pool"], 0.0, r=ENDK, w=["bar_pool"])
        tr(psb(0, 0, 128), identb, identb, r=ENDK + ["CB"], w=["ps0"])
        w_out_v = w_out.rearrange("(kc p) n -> p kc n", p=128)
        w_ff1_v = w_ff1.rearrange("(kc p) n -> p kc n", p=128)
        w_ff2_v = w_ff2.rearrange("(fc p) n -> p fc n", p=128)

        def load_gb(b):
            dma("sp", g1b, mod_d[b, 2 * D:3 * D].partition_broadcast(128), r=ENDK + ["mod_d"], w=["g1b"])
            dma("sp", g2b, mod_d[b, 5 * D:6 * D].partition_broadcast(128), r=ENDK + ["mod_d"], w=["g2b"])

        load_gb(0)
        dma("sp", fgb, final_g.partition_broadcast(128), r=ENDK, w=["fgb"])
        for kc in range(8):
            dma("pool", WOUT[:, kc, :], w_out_v[:, kc, :], r=ENDK, w=[f"wout{kc}"])
        for q4 in range(4):
            for kh in range(2):
                dma("pool", FF1[:, kh * 4:(kh + 1) * 4, q4 * 1024:(q4 + 1) * 1024],
                    w_ff1_v[:, kh * 4:(kh + 1) * 4, q4 * 1024:(q4 + 1) * 1024], r=ENDK, w=[f"ff1_{q4}"])
        for f4 in range(8):
            dma("pool", FF2[:, f4 * 4:(f4 + 1) * 4, :], w_ff2_v[:, f4 * 4:(f4 + 1) * 4, :], r=ENDK, w=[f"ff2_{f4}"])

        def load_x(i):
            s = i % 2
            dma("sp", xt2[s], x[i * 128:(i + 1) * 128, :], r=ENDK, w=[f"xt2_{s}"])

        def load_mg(i):
            s = i % 2
            dma("sp", mg2[s], merged_d[i * 128:(i + 1) * 128, :], r=ENDK + [f"mgd{i}"], w=[f"mg2_{s}"])

        def A2t(i):
            b = i // NT_SEQ
            s = i % 2
            X = xt2[s]
            xk = f"xt2_{s}"
            xn2T = xn2Ts[s]
            nk = f"xn2T{s}"
            for kc in range(8):
                tr(psb(0, kc * 128, 128), mg2[s][:, kc * 128:(kc + 1) * 128], identb, r=[f"mg2_{s}", "CB"], w=["ps0"])
            act(mgT.rearrange("p a b -> p (a b)"), psb(0), AF.Copy, r=["ps0"], w=["mgT"])
            yield
            for half in range(2):
                hs = slice(half * 512, (half + 1) * 512)
                for kc in range(8):
                    mm(psf(1 + half), mgT[:, kc, :], WOUT[:, kc, hs], kc == 0, kc == 7, r=["mgT", f"wout{kc}"],
                       w=[f"ps{1 + half}"])
                tt("dve", tmpRA[half], psf(1 + half), g1b[:, hs], ALU.mult, r=[f"ps{1 + half}", "g1b"],
                   w=[f"tmpRA{half}"])
                tt("pool", X[:, hs], X[:, hs], tmpRA[half], ALU.add, r=[xk, f"tmpRA{half}"], w=[xk])
                yield
            sumsq(X, xhat2, ssq2, [xk], "xhat2", "ssq2")
            ts("dve", mse2, ssq2, 1.0 / D, EPS, ALU.mult, ALU.add, r=["ssq2"], w=["mse2"])
            tt("pool", rstd2, mse2, mhalf[:, 0:1], ALU.pow, r=["mse2", "mhalf"], w=["rstd2"])
            act(xhat2, X, AF.Identity, scale=rstd2, r=[xk, "rstd2"], w=["xhat2"])
            yield
            for kc in range(8):
                tr(psb(0, kc * 128, 128), xhat2[:, kc * 128:(kc + 1) * 128], identb, r=["xhat2", "CB"], w=["ps0"])
            for kc in range(8):
                act(xn2T[:, kc, :], psb(0, kc * 128, 128), AF.Identity, scale=Gsh[:, 2, b * 8 + kc:b * 8 + kc + 1],
                    bias=Gsh[:, 3, b * 8 + kc:b * 8 + kc + 1], r=["ps0", "Gsh"], w=[nk])
            yield

        def B2t(i):
            b = i // NT_SEQ
            s = i % 2
            X = xt2[s]
            xk = f"xt2_{s}"
            xn2T = xn2Ts[s]
            nk = f"xn2T{s}"
            for f4 in range(8):
                bank = 3 + f4 % 2
                for j in range(4):
                    fb = f4 * 4 + j
                    for kc in range(8):
                        mm(psf(bank, j * 128, 128), FF1[:, kc, fb * 128:(fb + 1) * 128], xn2T[:, kc, :], kc == 0, kc == 7,
                           r=[nk, f"ff1_{f4 // 2}"], w=[f"ps{bank}"])
                act(tmpH[f4 % 2], psf(bank), AF.Relu, r=[f"ps{bank}"], w=[f"tmpH{f4 % 2}"])
                tt("pool" if f4 % 2 else "dve", hidT[:, f4 * 4:(f4 + 1) * 4, :].rearrange("p a b -> p (a b)"),
                   tmpH[f4 % 2], tmpH[f4 % 2], ALU.mult, r=[f"tmpH{f4 % 2}"], w=[f"hid{f4}"])
                yield
            for half in range(2):
                hs = slice(half * 512, (half + 1) * 512)
                for q4 in range(4):
                    for fc in range(q4 * 8, (q4 + 1) * 8):
                        mm(psf(5 + half), hidT[:, fc, :], FF2[:, fc, hs], fc == 0, fc == 31,
                           r=[f"hid{fc // 4}", f"ff2_{fc // 4}"], w=[f"ps{5 + half}"])
                    if q4 < 3:
                        yield
                tt("dve", tmpRB[half], psf(5 + half), g2b[:, hs], ALU.mult, r=[f"ps{5 + half}", "g2b"],
                   w=[f"tmpRB{half}"])
                tt("pool", X[:, hs], X[:, hs], tmpRB[half], ALU.add, r=[xk, f"tmpRB{half}"], w=[xk])
                yield
            sumsq(X, junkB, ssq3, [xk], "tmpH0", "ssq3")
            ts("dve", mse3, ssq3, 1.0 / D, EPS, ALU.mult, ALU.add, r=["ssq3"], w=["mse3"])
            tt("pool", rstd3, mse3, mhalf[:, 0:1], ALU.pow, r=["mse3", "mhalf"], w=["rstd3"])
            stt(outb, X, rstd3, fgb, ALU.mult, ALU.mult, r=[xk, "rstd3", "fgb"], w=["outb"])
            dma("sp", out[i * 128:(i + 1) * 128, :], outb, r=["outb"], w=[f"out{i}"])
            if i + 2 < nt:
                load_x(i + 2)
            if i + 1 < nt and (i + 1) % NT_SEQ == 0:
                load_gb((i + 1) // NT_SEQ)
            yield

        load_x(0)
        load_mg(0)
        if nt > 1:
            load_x(1)
            load_mg(1)
        interleave(A2t(0))
        for i in range(nt):
            if i + 2 < nt:
                load_mg(i + 2)
            if i + 1 < nt and (i + 1) % NT_SEQ != 0:
                interleave(B2t(i), A2t(i + 1))
            elif i + 1 < nt:
                interleave(B2t(i))
                interleave(A2t(i + 1))
            else:
                interleave(B2t(i))

    P.finalize()
    sems = {e: [es.enter_context(nc.semaphore(f"s_{e}{k}")) for k in range(SEM_K)] for e in ("pe", "act", "dve", "pool")}
    dsems = {"sp": [es.enter_context(nc.semaphore(f"d_sp{k}")) for k in range(N_DMA_SP)],
             "pool": [es.enter_context(nc.semaphore(f"d_pl{k}")) for k in range(N_DMA_POOL)]}
    with nc.Block() as block:
        @block.sync
        def _(E):
            P.emit("sp", E, sems, dsems)

        @block.gpsimd
        def _(E):
            P.emit("pool", E, sems, dsems)

        @block.scalar
        def _(E):
            P.emit("act", E, sems, dsems)

        @block.vector
        def _(E):
            P.emit("dve", E, sems, dsems)

        @block.tensor
        def _(E):
            P.emit("pe", E, sems, dsems)
    es.close()
    return nc, P


def prep_inputs(inputs):
    f = lambda a: np.ascontiguousarray(np.asarray(a, dtype=np.float32))
    x = f(inputs["x"])
    c = f(inputs["c"])
    idn = np.eye(128, dtype=np.float32)
    U = np.triu(np.ones((128, 128), np.float32))
    ones = np.ones((128, 128), np.float32)
    cid = np.arange(128) // 64
    mask2 = (cid[:, None] <= cid[None, :]).astype(np.float32)
    cfc = np.ascontiguousarray(np.concatenate([idn, U, ones, mask2], axis=1))
    cb16 = np.ascontiguousarray(np.concatenate([idn, U], axis=1).astype(ml_dtypes.bfloat16))
    conv_w = f(inputs["conv_w"])[0]
    shared = {
        "w_ada": f(inputs["w_ada"])[0], "b_ada": f(inputs["b_ada"])[0],
        "n1g": np.ascontiguousarray(f(inputs["norm1_g"])[0].reshape(8, 128).T),
        "n2g": np.ascontiguousarray(f(inputs["norm2_g"])[0].reshape(8, 128).T),
        "w_in": f(inputs["w_in"])[0],
        "cw": np.ascontiguousarray(conv_w.reshape(4, 16, 128).transpose(2, 1, 0).reshape(128, 64)),
        "cb": np.ascontiguousarray(f(inputs["conv_b"])[0].reshape(16, 128).T),
        "gate_b": np.ascontiguousarray(f(inputs["mlstm_gate_b"])[0].reshape(8)),
        "ln_g": f(inputs["gmlp_ln_g"])[0], "ln_b": f(inputs["gmlp_ln_b"])[0],
        "hn_g": f(inputs["mlstm_hn_g"])[0], "final_g": f(inputs["final_g"]),
        "wsT": np.ascontiguousarray(f(inputs["gmlp_ws"])[0].transpose(2, 0, 1).reshape(128, 1024)),
        "bs_tok": np.ascontiguousarray(f(inputs["gmlp_bs"])[0].T),
        "w_out": f(inputs["w_out"])[0], "w_ff1": f(inputs["w_ff1"])[0], "w_ff2": f(inputs["w_ff2"])[0],
        "cf": cfc, "cb16": cb16,
    }
    maps = []
    for core in range(8):
        xc = np.ascontiguousarray(x[2 * core:2 * core + 2].reshape(NT * 128, D))
        cc = c[2 * core:2 * core + 2]
        cTc = np.ascontiguousarray(cc.reshape(2, 8, 128).transpose(2, 1, 0).reshape(128, 16))
        m = dict(shared)
        m["x"] = xc
        m["cT"] = cTc
        maps.append(m)
    return maps


def kernel(**inputs):
    maps = prep_inputs(inputs)
    nc, _ = build_program()
    res = run_bass_kernel_spmd(nc, maps, core_ids=list(range(8)))
    outs = [np.asarray(r["out"], dtype=np.float32).reshape(2, SEQ, D) for r in res.results]
    return np.concatenate(outs, axis=0)
```

```python
import math
import numpy as np
import ml_dtypes
from contextlib import ExitStack
import concourse.bass as bass
import concourse.mybir as mybir
from concourse.bass_utils import run_bass_kernel_spmd

F32 = mybir.dt.float32
BF16 = mybir.dt.bfloat16
AF = mybir.ActivationFunctionType
ALU = mybir.AluOpType
AX = mybir.AxisListType

D = 1024
SEQ = 2048
NT_SEQ = SEQ // 128
NSEQ = 2
NT = NT_SEQ * NSEQ
OFF_U, OFF_V, OFF_Q, OFF_K, OFF_MV, OFF_O, OFF_I, OFF_F, OFF_GA, OFF_GB, IN_COLS = (
    0, 1024, 2048, 3072, 4096, 5120, 6144, 6148, 6152, 7176, 8200)
EPS = 1e-6
SEM_M = 8000
SEM_K = 6
N_DMA_SP = 12
N_DMA_POOL = 6
STRICT = True


class Op:
    __slots__ = ("eng", "fn", "dma", "gid", "deps", "pos", "ms", "msn", "sem", "val")


class Prog:
    ENG = ("pe", "act", "dve", "pool", "sp")

    def __init__(self):
        self.q = {e: [] for e in self.ENG}
        self.lastw = {}
        self.readers = {}
        self.n = 0

    def add(self, eng, fn, r=(), w=(), dma=False):
        op = Op()
        op.eng, op.fn, op.dma, op.gid = eng, fn, dma, self.n
        self.n += 1
        op.ms = False
        r = list(r)
        w = list(w)
        for k in list(r):
            if k.startswith("ps"):
                r.remove(k)
                if k not in w:
                    w.append(k)
        deps = {}
        for k in r:
            o = self.lastw.get(k)
            if o is not None:
                deps[o.gid] = o
        for k in w:
            o = self.lastw.get(k)
            if o is not None:
                deps[o.gid] = o
            for o in self.readers.get(k, ()):
                deps[o.gid] = o
        for k in w:
            self.lastw[k] = op
            self.readers[k] = []
        for k in r:
            self.readers.setdefault(k, []).append(op)
        deps.pop(op.gid, None)
        op.deps = list(deps.values())
        op.pos = len(self.q[eng])
        self.q[eng].append(op)
        return op

    def _needs_wait(self, op, d):
        if d.dma:
            return True
        if d.eng == op.eng:
            if op.eng == "pe":
                return False
            if op.dma:
                return True
            return STRICT or (op.pos - d.pos <= 2)
        return True

    def finalize(self):
        for e in self.ENG:
            for op in self.q[e]:
                for d in op.deps:
                    if (not d.dma) and self._needs_wait(op, d):
                        d.ms = True
        self.dma_count = {}
        for e in self.ENG:
            c = 0
            j = 0
            npool = N_DMA_SP if e == "sp" else N_DMA_POOL
            for op in self.q[e]:
                if op.dma:
                    op.sem = j % npool
                    op.val = 16 * (j // npool + 1)
                    j += 1
                elif op.ms:
                    c += 1
                    op.msn = c
            self.dma_count[e] = j
            assert c <= SEM_M * SEM_K, (e, c)

    def emit(self, e, E, sems, dsems):
        waited = {}

        def wait(sem, val):
            key = id(sem)
            if waited.get(key, 0) >= val:
                return
            E.wait_ge(sem, val)
            waited[key] = val

        def semfor(eng, n):
            return sems[eng][(n - 1) // SEM_M], (n - 1) % SEM_M + 1

        for op in self.q[e]:
            for d in op.deps:
                if not self._needs_wait(op, d):
                    continue
                if d.dma:
                    wait(dsems[d.eng][d.sem], d.val)
                else:
                    s, v = semfor(d.eng, d.msn)
                    wait(s, v)
            if op.dma:
                if op.val > 16:
                    wait(dsems[e][op.sem], op.val - 16)
                ins = op.fn(E)
                ins.then_inc(dsems[e][op.sem], 16)
            else:
                ins = op.fn(E)
                if op.ms:
                    s, v = semfor(e, op.msn)
                    ins.then_inc(s, 1)
        j = self.dma_count.get(e, 0)
        npool = N_DMA_SP if e == "sp" else N_DMA_POOL
        for si in range(min(j, npool)):
            last = (j - 1 - si) // npool * npool + si
            if last < 0:
                continue
            wait(dsems[e][si], 16 * (last // npool + 1))


def build_program(nt=NT, phase2=True, dbg=False):
    nc = bass.Bass("TRN2", target_bir_lowering=False)

    def din(name, shape, dt=F32):
        return nc.dram_tensor(name, list(shape), dt, kind="ExternalInput").ap()

    x = din("x", [NT * 128, D])
    cT = din("cT", [128, 16])
    w_ada = din("w_ada", [D, 6 * D])
    b_ada = din("b_ada", [6 * D])
    n1g = din("n1g", [128, 8])
    n2g = din("n2g", [128, 8])
    w_in = din("w_in", [D, IN_COLS])
    cw = din("cw", [128, 64])
    cb = din("cb", [128, 16])
    gate_b = din("gate_b", [8])
    ln_g = din("ln_g", [D])
    ln_b = din("ln_b", [D])
    hn_g = din("hn_g", [D])
    final_g = din("final_g", [D])
    wsT = din("wsT", [128, 8 * 128])
    bs_tok = din("bs_tok", [128, 8])
    w_out = din("w_out", [D, D])
    w_ff1 = din("w_ff1", [D, 4 * D])
    w_ff2 = din("w_ff2", [4 * D, D])
    cf = din("cf", [128, 4 * 128])
    cb16 = din("cb16", [128, 2 * 128], BF16)
    out = nc.dram_tensor("out", [NT * 128, D], F32, kind="ExternalOutput").ap()
    if dbg:
        merged_d = nc.dram_tensor("merged_d", [NT * 128, D], BF16, kind="ExternalOutput").ap()
        mod_d = nc.dram_tensor("mod_d", [2, 6 * D], F32, kind="ExternalOutput").ap()
    else:
        merged_d = nc.dram_tensor("merged_d", [NT * 128, D], BF16, kind="Internal").ap()
        mod_d = nc.dram_tensor("mod_d", [2, 6 * D], F32, kind="Internal").ap()

    P = Prog()
    es = ExitStack()
    ARENA_BYTES = 212800
    arena = es.enter_context(nc.sbuf_tensor("arena", [128, ARENA_BYTES // 2], BF16))
    psum = [es.enter_context(nc.psum_tensor(f"psb{i}", [128, 512], F32)) for i in range(8)]

    class Alloc:
        def __init__(self, base=0):
            self.off = base

        def take(self, nbytes):
            o = (self.off + 7) // 8 * 8
            self.off = o + nbytes
            assert self.off <= ARENA_BYTES, self.off
            return o

        def bf(self, shape):
            n = int(np.prod(shape[1:]))
            o = self.take(n * 2)
            return self._shape(arena[:, o // 2:o // 2 + n], shape)

        def f32(self, shape):
            n = int(np.prod(shape[1:]))
            o = self.take(n * 4)
            return self._shape(arena[:, o // 2:o // 2 + 2 * n].bitcast(F32), shape)

        @staticmethod
        def _shape(v, shape):
            if len(shape) == 3:
                return v.rearrange("p (a b) -> p a b", a=shape[1])
            if len(shape) == 4:
                return v.rearrange("p (a b c) -> p a b c", a=shape[1], b=shape[2])
            return v

    def psf(i, lo=0, n=512):
        return psum[i][:, lo:lo + n]

    def psb(i, lo=0, n=1024):
        return psum[i][:].bitcast(BF16)[:, lo:lo + n]

    A0 = Alloc(0)
    CF = A0.f32([128, 4, 128])
    CB = A0.bf([128, 2, 128])
    identf, Umat, onesf, mask2 = CF[:, 0, :], CF[:, 1, :], CF[:, 2, :], CF[:, 3, :]
    identb, maskT = CB[:, 0, :], CB[:, 1, :]
    modT = A0.f32([128, 96])
    n1g_s = A0.f32([128, 8])
    n2g_s = A0.f32([128, 8])
    Gsh = A0.f32([128, 4, 16])
    mhalf = A0.f32([128, 4])
    onesb = A0.bf([128, 4])
    scr = {e: A0.f32([128, 4]) for e in ("act", "dve", "pool")}
    base1 = A0.off

    def mm(outap, lhsT, rhs, start, stop, r=(), w=()):
        return P.add("pe", lambda E: E.matmul(outap, lhsT=lhsT, rhs=rhs, start=start, stop=stop), r=r, w=w)

    def tr(outap, inap, ident, r=(), w=()):
        return P.add("pe", lambda E: E.transpose(outap, inap, ident), r=r, w=w)

    def act(outap, inap, func, r=(), w=(), scale=1.0, bias=None):
        def fn(E):
            if bias is not None:
                return E.activation(out=outap, in_=inap, func=func, scale=scale, bias=bias)
            return E.activation(out=outap, in_=inap, func=func, scale=scale)
        return P.add("act", fn, r=r, w=w)

    def ts(eng, outap, in0, s1, s2, op0, op1=None, r=(), w=()):
        def fn(E):
            if op1 is None:
                return E.tensor_scalar(out=outap, in0=in0, scalar1=s1, scalar2=None, op0=op0)
            return E.tensor_scalar(out=outap, in0=in0, scalar1=s1, scalar2=s2, op0=op0, op1=op1)
        return P.add(eng, fn, r=r, w=w)

    def tt(eng, outap, in0, in1, op, r=(), w=()):
        return P.add(eng, lambda E: E.tensor_tensor(out=outap, in0=in0, in1=in1, op=op), r=r, w=w)

    def stt(outap, in0, scalar, in1, op0, op1, r=(), w=()):
        return P.add("dve", lambda E: E.scalar_tensor_tensor(out=outap, in0=in0, scalar=scalar, in1=in1,
                                                             op0=op0, op1=op1), r=r, w=w)

    def dma(eng, outap, inap, r=(), w=()):
        return P.add(eng, lambda E: E.dma_start(out=outap, in_=inap), r=r, w=w, dma=True)

    def memset(eng, ap, val, r=(), w=()):
        return P.add(eng, lambda E: E.memset(ap, val), r=r, w=w)

    def sumsq(src, junk, dst, rk, jk, dk):
        def fn(E):
            return E.activation(out=junk, in_=src, func=AF.Square, accum_out=dst)
        P.add("act", fn, r=rk, w=[jk, dk + "_raw"])
        P.add("act", lambda E: E.activation(out=scr["act"][:, 0:1], in_=scr["act"][:, 1:2], func=AF.Copy),
              r=[dk + "_raw"], w=[dk, "scr_act"])

    dma("sp", CF, cf.rearrange("p (a b) -> p a b", a=4), w=["CF"])
    dma("sp", CB, cb16.rearrange("p (a b) -> p a b", a=2), w=["CB"])
    dma("sp", n1g_s, n1g, w=["n1g"])
    dma("sp", n2g_s, n2g, w=["n2g"])
    memset("pool", mhalf, -0.5, w=["mhalf"])
    memset("pool", onesb, 1.0, w=["onesb"])
    memset("pool", scr["act"], 0.0, w=["scr_act"])

    A1 = Alloc(base1)
    WIN = A1.bf([128, 8, IN_COLS])
    wsTm = A1.bf([128, 8, 128])
    bst = A1.f32([128, 8])
    cw_s = A1.f32([128, 16, 4])
    cb_s = A1.f32([128, 16])
    gb8 = A1.f32([128, 8])
    lng = A1.f32([128, D])
    lnb = A1.f32([128, D])
    hng = A1.f32([128, D])
    Cst = A1.f32([128, 4, 512])
    nst = A1.f32([128, 4, 2])
    Cdec = A1.bf([128, 4, 512])
    vg = Cdec.rearrange("p a b -> p (a b)").bitcast(F32)
    ndec = A1.bf([128, 4, 2])
    carry = A1.f32([128, 4])
    Rprev = A1.f32([128, 4])
    hal = A1.f32([128, 16, 3])
    pq4 = [A1.f32([128, 4, 131]) for _ in range(2)]
    xt = A1.f32([128, D])
    xhat = A1.bf([128, D])
    xnT2 = [A1.bf([128, 8, 128]) for _ in range(2)]
    Vb = A1.bf([128, D])
    qkT2 = [A1.bf([128, 16, 128]) for _ in range(2)]
    acc4s = [A1.f32([128, 4, 128]) for _ in range(2)]
    th4s = [A1.f32([128, 4, 128]) for _ in range(2)]
    wk = A1.bf([128, D])
    PT = A1.bf([128, 4, 128])
    yb = A1.f32([128, D])
    tmpA = [A1.f32([128, 512]) for _ in range(2)]
    tmpB = [A1.f32([128, 512]) for _ in range(2)]
    mg = A1.bf([128, D])
    vn = mg
    sm = A1.f32([128, 160])
    ssq, mse, rstd = sm[:, 0:1], sm[:, 1:2], sm[:, 2:3]
    g8 = sm[:, 8:16]
    li, fp = g8[:, 0:4], g8[:, 4:8]
    af, ee, dd, uu = sm[:, 16:20], sm[:, 20:24], sm[:, 24:28], sm[:, 28:32]
    s2, tq, mn, lf = sm[:, 32:36], sm[:, 36:40], sm[:, 40:44], sm[:, 44:48]
    Bt, ah, amax, Rnew = sm[:, 48:52], sm[:, 52:56], sm[:, 56:60], sm[:, 60:64]
    eargs, eout = sm[:, 64:76], sm[:, 76:88]
    wv, flv, decv = eout[:, 0:4], eout[:, 4:8], eout[:, 8:12]
    den4, rr = sm[:, 88:92], sm[:, 92:96]
    bst6 = sm[:, 96:120].rearrange("p (a b) -> p a b", a=4)
    mv4 = sm[:, 120:128].rearrange("p (a b) -> p a b", a=4)
    rs4 = sm[:, 128:132]
    vst, vmv, vrs = sm[:, 132:144], sm[:, 144:146], sm[:, 146:147]
    sm2 = A1.f32([128, 16])
    uu2 = sm2[:, 0:4]
    eouts = [sm[:, 76:88], sm2[:, 4:16]]
    nb4 = sm[:, 148:152]
    nmean = sm[:, 152:153]
    end1 = A1.off

    wsT_f = tmpA[0].rearrange("p (a b) -> p a b", a=4)
    wsT_v = wsT.rearrange("p (a b) -> p a b", a=8)
    for hlf in range(2):
        dma("sp", wsT_f, wsT_v[:, hlf * 4:(hlf + 1) * 4, :], w=["tmpA0"])
        tt("dve", wsTm[:, hlf * 4:(hlf + 1) * 4, :], wsT_f, mask2.unsqueeze(1).to_broadcast([128, 4, 128]),
           ALU.mult, r=["tmpA0", "CF"], w=["wsTm"])
    dma("sp", bst, bs_tok, w=["bst"])
    dma("sp", cw_s, cw.rearrange("p (a b) -> p a b", a=16), w=["cw"])
    dma("sp", cb_s, cb, w=["cbs"])
    dma("sp", gb8, gate_b.partition_broadcast(128), w=["gb8"])
    dma("sp", lng, ln_g.partition_broadcast(128), w=["lng"])
    dma("sp", lnb, ln_b.partition_broadcast(128), w=["lnb"])
    dma("sp", hng, hn_g.partition_broadcast(128), w=["hng"])

    csT = tmpA[1][:, 0:16]
    cth = tmpA[1][:, 16:32]
    dma("sp", csT, cT, w=["csT"])
    act(cth, csT, AF.Tanh, scale=0.5, r=["csT"], w=["cth"])
    stt(cth, cth, 1.0, csT, ALU.add, ALU.mult, r=["cth", "csT"], w=["cth"])
    ts("dve", csT, cth, 0.5, None, ALU.mult, r=["cth"], w=["csT"])
    csT3 = csT.rearrange("p (a b) -> p a b", a=8)
    w_ada_v = w_ada.rearrange("(kc p) n -> p kc n", p=128)
    wst = [WIN[:, 6, 0:8192].bitcast(F32).rearrange("p (a b) -> p a b", a=8),
           WIN[:, 7, 0:8192].bitcast(F32).rearrange("p (a b) -> p a b", a=8)]
    wkeys = ["win6", "win7"]
    brow = [tmpB[1][0:2, 0:512], tmpB[1][0:2, 0:512]]
    mrow = [tmpA[0][0:2, 0:512], tmpB[0][0:2, 0:512]]
    for j in range(12):
        sb = j % 2
        dma("sp", wst[sb], w_ada_v[:, :, j * 512:(j + 1) * 512], w=[wkeys[sb]] + (["wadone"] if j == 11 else []))
        dma("sp", brow[sb], b_ada[j * 512:(j + 1) * 512].partition_broadcast(2), w=["tmpB1"])
        for kc in range(8):
            mm(psf(1 + sb)[0:2, :], csT3[:, kc, :], wst[sb][:, kc, :], kc == 0, kc == 7,
               r=[wkeys[sb], "csT"], w=[f"ps{1 + sb}"])
        mk = "tmpA0" if sb == 0 else "tmpB0"
        tt("dve", mrow[sb], psf(1 + sb)[0:2, :], brow[sb], ALU.add, r=[f"ps{1 + sb}", "tmpB1"], w=[mk])
        dma("sp", mod_d[:, j * 512:(j + 1) * 512], mrow[sb], r=[mk], w=["mod_d"])
    modin = tmpB[1][0:96, 0:128]
    dma("sp", modin, mod_d.rearrange("b (c p) -> (b c) p", p=128), r=["mod_d"], w=["tmpB1"])
    tr(psf(1)[:, 0:96], modin, identf[0:96, 0:96], r=["tmpB1", "CF"], w=["ps1"])
    P.add("dve", lambda E: E.tensor_copy(out=modT, in_=psf(1)[:, 0:96]), r=["ps1"], w=["modT"])
    for b in range(2):
        o = b * 48
        stt(Gsh[:, 0, b * 8:(b + 1) * 8], modT[:, o + 8:o + 16], 1.0, n1g_s, ALU.add, ALU.mult,
            r=["modT", "n1g"], w=["Gsh"])
        P.add("dve", lambda E, b=b, o=o: E.tensor_copy(out=Gsh[:, 1, b * 8:(b + 1) * 8], in_=modT[:, o:o + 8]),
              r=["modT"], w=["Gsh"])
        stt(Gsh[:, 2, b * 8:(b + 1) * 8], modT[:, o + 32:o + 40], 1.0, n2g_s, ALU.add, ALU.mult,
            r=["modT", "n2g"], w=["Gsh"])
        P.add("dve", lambda E, b=b, o=o: E.tensor_copy(out=Gsh[:, 3, b * 8:(b + 1) * 8], in_=modT[:, o + 24:o + 32]),
              r=["modT"], w=["Gsh"])

    w_in_v = w_in.rearrange("(kc p) n -> p kc n", p=128)
    WGRP = [(OFF_Q, OFF_MV), (OFF_MV, OFF_GA), (0, OFF_Q), (OFF_GA, IN_COLS)]

    def wkey(c0):
        for gi, (lo, hi) in enumerate(WGRP):
            if lo <= c0 < hi:
                return f"winG{gi}"
        raise ValueError(c0)

    for gi, (lo, hi) in enumerate(WGRP):
        pr = ["wadone"] if gi >= 1 else []
        dma("pool", WIN[:, 0:3, lo:hi], w_in_v[:, 0:3, lo:hi], r=pr, w=[f"winG{gi}"])
        dma("pool", WIN[:, 3:6, lo:hi], w_in_v[:, 3:6, lo:hi], r=pr, w=[f"winG{gi}"])
        for kc in (6, 7):
            dma("pool", WIN[:, kc, lo:hi], w_in_v[:, kc, lo:hi], w=[f"winG{gi}", f"win{kc}"])

    tokbank = [0]

    def xn(i):
        return xnT2[i % 2], f"xnT{i % 2}"

    def tokblk(i, c0):
        bank = nextbank()
        X, xk = xn(i)
        for kc in range(8):
            mm(psf(bank), X[:, kc, :], WIN[:, kc, c0:c0 + 512], kc == 0, kc == 7,
               r=[xk, wkey(c0)], w=[f"ps{bank}"])
        return bank

    def eo(i):
        e = eouts[i % 2]
        return e, e[:, 0:4], e[:, 4:8], e[:, 8:12], f"eout{i % 2}"

    rot = [1, 2]

    def nextbank():
        tokbank[0] += 1
        return rot[tokbank[0] % len(rot)]

    def front(i):
        b = i // NT_SEQ
        X, xk = xn(i)
        dma("sp", xt, x[i * 128:(i + 1) * 128, :], w=["xt"])
        sumsq(xt, xhat, ssq, ["xt"], "xhat", "ssq")
        ts("dve", mse, ssq, 1.0 / D, EPS, ALU.mult, ALU.add, r=["ssq"], w=["mse"])
        tt("pool", rstd, mse, mhalf[:, 0:1], ALU.pow, r=["mse", "mhalf"], w=["rstd"])
        act(xhat, xt, AF.Identity, scale=rstd, r=["xt", "rstd"], w=["xhat"])
        yield

    def front_b(i):
        b = i // NT_SEQ
        X, xk = xn(i)
        for kc in range(8):
            tr(psb(0, kc * 128, 128), xhat[:, kc * 128:(kc + 1) * 128], identb, r=["xhat", "CB"], w=["ps0"])
        for kc in range(8):
            act(X[:, kc, :], psb(0, kc * 128, 128), AF.Identity, scale=Gsh[:, 0, b * 8 + kc:b * 8 + kc + 1],
                bias=Gsh[:, 1, b * 8 + kc:b * 8 + kc + 1], r=["ps0", "Gsh"], w=[xk])
        yield

    def qkconv(i):
        X, xk = xn(i)
        qkT = qkT2[i % 2]
        qk = f"qkT{i % 2}"
        if i % NT_SEQ == 0:
            memset("pool", hal, 0.0, w=["hal"])

        def names(g4):
            return (3 + g4 % 2, pq4[g4 % 2], f"pq4_{g4 % 2}", acc4s[g4 % 2], th4s[g4 % 2], f"a{g4 % 2}_",
                    f"th4{'' if g4 % 2 == 0 else 'b'}")

        def stA(g4):
            bank, buf, bk, acc4, th4c, ak, tk = names(g4)
            for j in range(4):
                blk = g4 * 4 + j
                for kc in range(8):
                    mm(psf(bank, j * 128, 128), WIN[:, kc, OFF_Q + blk * 128:OFF_Q + (blk + 1) * 128], X[:, kc, :],
                       kc == 0, kc == 7, r=[xk, "winG0"], w=[f"ps{bank}"])
            P.add("pool", lambda E, buf=buf, g4=g4: E.tensor_copy(out=buf[:, :, 0:3], in_=hal[:, g4 * 4:(g4 + 1) * 4, :]),
                  r=["hal"], w=[bk + "h"])
            act(buf[:, :, 3:131], psf(bank).rearrange("p (a b) -> p a b", a=4), AF.Copy, r=[f"ps{bank}"], w=[bk])
            P.add("pool", lambda E, buf=buf, g4=g4: E.tensor_copy(out=hal[:, g4 * 4:(g4 + 1) * 4, :], in_=buf[:, :, 128:131]),
                  r=[bk], w=["hal"])
            for j in range(4):
                blk = g4 * 4 + j
                act(acc4[:, j, :], buf[:, j, 3:131], AF.Identity, scale=cw_s[:, blk, 3:4], bias=cb_s[:, blk:blk + 1],
                    r=[bk, "cw", "cbs"], w=[f"{ak}{j}"])

        def stB(g4):
            bank, buf, bk, acc4, th4c, ak, tk = names(g4)
            for tap in range(3):
                for j in range(4):
                    blk = g4 * 4 + j
                    stt(acc4[:, j, :], buf[:, j, tap:tap + 128], cw_s[:, blk, tap:tap + 1], acc4[:, j, :],
                        ALU.mult, ALU.add, r=[bk, bk + "h", "cw", f"{ak}{j}"], w=[f"{ak}{j}"])

        def stC(g4):
            bank, buf, bk, acc4, th4c, ak, tk = names(g4)
            act(th4c, acc4, AF.Tanh, scale=0.5, r=[f"{ak}{j}" for j in range(4)], w=[tk])
            stt(qkT[:, g4 * 4:(g4 + 1) * 4, :], th4c, 1.0, acc4, ALU.add, ALU.mult,
                r=[tk] + [f"{ak}{j}" for j in range(4)], w=[qk])

        for st, g in ((stA, 0), (stA, 1), (stB, 0), (stC, 0), (stB, 1), (stA, 2), (stC, 1), (stB, 2), (stA, 3),
                      (stC, 2), (stB, 3), (stC, 3)):
            st(g)
            yield

    def vblocks(i):
        for half in range(2):
            bank = tokblk(i, OFF_MV + half * 512)
            act(Vb[:, half * 512:(half + 1) * 512], psf(bank), AF.Copy, r=[f"ps{bank}"], w=["Vb"])
            yield

    def gates(i):
        X, xk = xn(i)
        eout, wv, flv, decv, ek = eo(i)
        if i % NT_SEQ == 0:
            memset("pool", carry, 0.0, w=["carry"])
            memset("pool", Rprev, 0.0, w=["Rprev"])
        for kc in range(8):
            mm(psf(5, 300, 8), X[:, kc, :], WIN[:, kc, OFF_I:OFF_I + 8], kc == 0, kc == 7,
               r=[xk, "winG1"], w=["ps5"])
        tt("dve", g8, psf(5, 300, 8), gb8, ALU.add, r=["ps5", "gb8"], w=["g8"])
        yield
        stt(af, fp, -1.0, fp, ALU.mult, ALU.max, r=["g8"], w=["af"])
        act(ee, af, AF.Exp, scale=-1.0, r=["af"], w=["ee"])
        ts("dve", mn, fp, 0.0, None, ALU.min, r=["g8"], w=["mn"])
        yield
        ts("dve", dd, ee, 2.0, None, ALU.add, r=["ee"], w=["dd"])
        P.add("dve", lambda E: E.reciprocal(out=uu2, in_=dd), r=["dd"], w=["uu2"])
        tt("dve", uu, ee, uu2, ALU.mult, r=["ee", "uu2"], w=["uu"])
        tt("dve", s2, uu, uu, ALU.mult, r=["uu"], w=["s2"])
        yield
        ts("dve", tq, s2, 1.0 / 13.0, None, ALU.mult, r=["s2"], w=["tq"])
        for cc in (1.0 / 11, 1.0 / 9, 1.0 / 7, 1.0 / 5, 1.0 / 3):
            stt(tq, tq, cc, s2, ALU.add, ALU.mult, r=["tq", "s2"], w=["tq"])
            yield
        stt(tq, tq, 1.0, uu, ALU.add, ALU.mult, r=["tq", "uu"], w=["tq"])
        stt(lf, tq, -2.0, mn, ALU.mult, ALU.add, r=["tq", "mn"], w=["lf"])
        yield
        yield
        mm(psf(5, 308, 4), Umat, lf, True, True, r=["lf", "CF"], w=["ps5"])
        mm(psf(5, 312, 4), onesf, lf, True, True, r=["lf", "CF"], w=["ps5"])
        yield
        tt("dve", Bt, psf(5, 308, 4), carry, ALU.add, r=["ps5", "carry"], w=["Bt"])
        tt("dve", carry, psf(5, 312, 4), carry, ALU.add, r=["ps5", "carry"], w=["carry"])
        tt("dve", ah, li, Bt, ALU.subtract, r=["g8", "Bt"], w=["ah"])
        Dmb = tmpA[0].rearrange("p (a b) -> p a b", a=4)
        tt("dve", Dmb, identf.unsqueeze(1).to_broadcast([128, 4, 128]),
           ah.unsqueeze(2).to_broadcast([128, 4, 128]), ALU.mult, r=["CF", "ah"], w=["tmpA0"])
        yield
        yield
        bank = nextbank()
        mm(psf(bank), onesf, tmpA[0], True, True, r=["tmpA0", "CF"], w=[f"ps{bank}"])
        P.add("dve", lambda E, bank=bank: E.tensor_reduce(out=amax, in_=psf(bank).rearrange("p (a b) -> p a b", a=4),
                                                          axis=AX.X, op=ALU.max), r=[f"ps{bank}"], w=["amax"])
        yield
        tt("dve", Rnew, amax, Rprev, ALU.max, r=["amax", "Rprev"], w=["Rnew"])
        tt("dve", eargs[:, 0:4], ah, Rnew, ALU.subtract, r=["ah", "Rnew"], w=["ea0"])
        stt(eargs[:, 4:8], Bt, -1.0, Rnew, ALU.mult, ALU.subtract, r=["Bt", "Rnew"], w=["ea1"])
        ts("dve", eargs[:, 4:8], eargs[:, 4:8], math.log(64.0), None, ALU.add, r=["ea1"], w=["ea1"])
        tt("dve", eargs[:, 8:12], Rprev, Rnew, ALU.subtract, r=["Rprev", "Rnew"], w=["ea2"])
        act(eout, eargs, AF.Exp, r=["ea0", "ea1", "ea2"], w=[ek])
        P.add("pool", lambda E: E.tensor_copy(out=Rprev, in_=Rnew), r=["Rnew", "ea2"], w=["Rprev"])
        yield

    def gmlp(i):
        for half in range(2):
            bank = tokblk(i, OFF_V + half * 512)
            act(vg[:, half * 512:(half + 1) * 512], psf(bank), AF.Gelu_apprx_tanh, r=[f"ps{bank}"], w=["Cdec"])
            yield
        P.add("dve", lambda E: E.bn_stats(out=vst[:, 0:6], in_=vg[:, 0:512]), r=["Cdec"], w=["vst0"])
        P.add("dve", lambda E: E.bn_stats(out=vst[:, 6:12], in_=vg[:, 512:1024]), r=["Cdec"], w=["vst1"])
        P.add("dve", lambda E: E.bn_aggr(out=vmv, in_=vst), r=["vst0", "vst1"], w=["vmv"])
        ts("dve", vrs, vmv[:, 1:2], EPS, None, ALU.add, r=["vmv"], w=["vrs"])
        tt("pool", vrs, vrs, mhalf[:, 0:1], ALU.pow, r=["vrs", "mhalf"], w=["vrs"])
        yield
        stt(vg, vg, vmv[:, 0:1], lng, ALU.subtract, ALU.mult, r=["Cdec", "vmv", "lng"], w=["Cdec"])
        yield
        stt(vn, vg, vrs, lnb, ALU.mult, ALU.add, r=["Cdec", "vrs", "lnb"], w=["mg"])
        yield
        yield
        for g in range(8):
            mm(psf(3 + g // 4, (g % 4) * 128, 128), wsTm[:, g, :], vn[:, g * 128:(g + 1) * 128], True, True,
               r=["wsTm", "mg"], w=[f"ps{3 + g // 4}"])
        yield
        for half in range(2):
            bank = tokblk(i, OFF_U + half * 512)
            act(tmpA[half], psf(bank), AF.Gelu_apprx_tanh, r=[f"ps{bank}"], w=[f"tmpA{half}"])
            for gg in range(4):
                g = half * 4 + gg
                stt(tmpB[half][:, gg * 128:(gg + 1) * 128], psf(3 + half, gg * 128, 128), bst[:, g:g + 1],
                    tmpA[half][:, gg * 128:(gg + 1) * 128], ALU.add, ALU.mult,
                    r=[f"ps{3 + half}", "bst", f"tmpA{half}"], w=[f"tmpB{half}"])
            yield
            bank = tokblk(i, OFF_GA + half * 512)
            act(tmpA[half], psf(bank), AF.Tanh, scale=0.5, r=[f"ps{bank}"], w=[f"tmpA{half}"])
            stt(tmpB[half], tmpA[half], 1.0, tmpB[half], ALU.add, ALU.mult, r=[f"tmpA{half}", f"tmpB{half}"],
                w=[f"tmpB{half}"])
            yield

    def mlstm_core(i):
        eout, wv, flv, decv, ek = eo(i)
        qkT = qkT2[i % 2]
        qk = f"qkT{i % 2}"
        if i % NT_SEQ == 0:
            memset("pool", Cst, 0.0, w=["C"])
            memset("pool", nst, 0.0, w=["n"])
        for h in range(4):
            act(Cdec[:, h, :], Cst[:, h, :], AF.Identity, scale=decv[:, h:h + 1], r=["C", ek], w=["Cdec"])
        tt("dve", ndec, nst, decv.unsqueeze(2).to_broadcast([128, 4, 2]), ALU.mult, r=["n", ek], w=["ndec"])
        yield
        for kb in range(8):
            tr(psb(0, kb * 128, 128), qkT[:, 8 + kb, :], identb, r=[qk, "CB"], w=["ps0"])
        for h in range(4):
            act(wk[:, h * 256:(h + 1) * 256], psb(0, h * 256, 256), AF.Identity, scale=wv[:, h:h + 1],
                r=["ps0", ek], w=["wk"])
        yield
        for hp in range(2):
            for hh in range(2):
                h = 2 * hp + hh
                for kc in range(2):
                    mm(psf(5, hh * 128, 128), qkT[:, 8 + 2 * h + kc, :], qkT[:, 2 * h + kc, :], kc == 0, kc == 1,
                       r=[qk], w=["ps5"])
            for hh in range(2):
                h = 2 * hp + hh
                stt(PT[:, h, :], psf(5, hh * 128, 128), wv[:, h:h + 1], maskT, ALU.mult, ALU.mult,
                    r=["ps5", ek, "CB"], w=["PT"])
            yield
            for hh in range(2):
                h = 2 * hp + hh
                nb = 6 + hp
                Vh = Vb[:, h * 256:(h + 1) * 256]
                mm(psf(nb, hh * 256, 256), PT[:, h, :], Vh, True, False, r=["PT", "Vb"], w=[f"ps{nb}"])
                mm(psf(nb, hh * 256, 256), qkT[:, 2 * h, :], Cdec[:, h, 0:256], False, False, r=[qk, "Cdec"], w=[f"ps{nb}"])
                mm(psf(nb, hh * 256, 256), qkT[:, 2 * h + 1, :], Cdec[:, h, 256:512], False, True, r=[qk, "Cdec"], w=[f"ps{nb}"])
                mm(psf(5, 256 + h, 1), PT[:, h, :], onesb[:, 0:1], True, False, r=["PT", "onesb"], w=["ps5"])
                mm(psf(5, 256 + h, 1), qkT[:, 2 * h, :], ndec[:, h, 0:1], False, False, r=[qk, "ndec"], w=["ps5"])
                mm(psf(5, 256 + h, 1), qkT[:, 2 * h + 1, :], ndec[:, h, 1:2], False, True, r=[qk, "ndec"], w=["ps5"])
                bc = nextbank()
                for kc in range(2):
                    mm(psf(bc, kc * 256, 256), wk[:, h * 256 + kc * 128:h * 256 + (kc + 1) * 128], Vh, True, True,
                       r=["wk", "Vb"], w=[f"ps{bc}"])
                for kc in range(2):
                    mm(psf(5, 260 + 2 * h + kc, 1), wk[:, h * 256 + kc * 128:h * 256 + (kc + 1) * 128], onesb[:, 0:1],
                       True, True, r=["wk", "onesb"], w=["ps5"])
                stt(Cst[:, h, :], Cst[:, h, :], decv[:, h:h + 1], psf(bc), ALU.mult, ALU.add,
                    r=["C", ek, f"ps{bc}", "Cdec"], w=["C"])
                stt(nst[:, h, :], nst[:, h, :], decv[:, h:h + 1], psf(5, 260 + 2 * h, 2), ALU.mult, ALU.add,
                    r=["n", ek, "ps5", "ndec"], w=["n"])
                P.add("dve", lambda E, h=h, nb=nb, hh=hh: E.bn_stats(out=bst6[:, h, :], in_=psf(nb, hh * 256, 256)),
                      r=[f"ps{nb}"], w=[f"bst{h}"])
                yield

    def out_thread(i):
        eout, wv, flv, decv, ek = eo(i)
        P.add("dve", lambda E: E.tensor_copy(out=den4, in_=psf(5, 256, 4)), r=["ps5"], w=["den4"])
        stt(den4, den4, -1.0, den4, ALU.mult, ALU.max, r=["den4"], w=["den4"])
        tt("dve", den4, den4, flv, ALU.max, r=["den4", ek], w=["den4"])
        for h in range(4):
            P.add("dve", lambda E, h=h: E.bn_aggr(out=mv4[:, h, :], in_=bst6[:, h, :]), r=[f"bst{h}"], w=[f"mv{h}"])
        yield
        P.add("dve", lambda E: E.reciprocal(out=rr, in_=den4), r=["den4"], w=["rr"])
        tt("dve", rs4, rr, rr, ALU.mult, r=["rr"], w=["rs4"])
        tt("dve", rs4, rs4, mv4[:, :, 1], ALU.mult, r=["rs4"] + [f"mv{h}" for h in range(4)], w=["rs4"])
        ts("dve", rs4, rs4, EPS, None, ALU.add, r=["rs4"], w=["rs4"])
        tt("pool", rs4, rs4, mhalf, ALU.pow, r=["rs4", "mhalf"], w=["rs4"])
        yield
        stt(rs4, rs4, 0.25, rr, ALU.mult, ALU.mult, r=["rs4", "rr"], w=["rs4"])
        stt(nb4, mv4[:, :, 0], -1.0, rs4, ALU.mult, ALU.mult, r=["rs4"] + [f"mv{h}" for h in range(4)], w=["nb4"])
        for h in range(4):
            nb = 6 + h // 2
            act(yb[:, h * 256:(h + 1) * 256], psf(nb, (h % 2) * 256, 256), AF.Identity, scale=rs4[:, h:h + 1],
                bias=nb4[:, h:h + 1], r=[f"ps{nb}", "nb4", "rs4"], w=["yb"])
        tt("dve", yb, yb, hng, ALU.mult, r=["yb", "hng"], w=["yb"])
        rot[:] = [1, 2, 0, 6, 7]
        yield
        for c0 in (OFF_O, OFF_GB):
            for half in range(2):
                bank = tokblk(i, c0 + half * 512)
                act(tmpA[half], psf(bank), AF.Tanh, scale=0.5, r=[f"ps{bank}"], w=[f"tmpA{half}"])
                stt(yb[:, half * 512:(half + 1) * 512], tmpA[half], 1.0, yb[:, half * 512:(half + 1) * 512],
                    ALU.add, ALU.mult, r=[f"tmpA{half}", "yb"], w=["yb"])
                yield

    def merge(i):
        for half in range(2):
            hs = slice(half * 512, (half + 1) * 512)
            stt(mg[:, hs], tmpB[half], 0.5, yb[:, hs], ALU.mult, ALU.add, r=[f"tmpB{half}", "yb"], w=["mg"])
        dma("sp", merged_d[i * 128:(i + 1) * 128, :], mg, r=["mg"], w=[f"mgd{i}"])
        yield

    def chain(*gens):
        for g in gens:
            yield from g

    def interleave(*gens):
        gens = list(gens)
        while gens:
            for g in list(gens):
                try:
                    next(g)
                except StopIteration:
                    gens.remove(g)

    def zipg(*gens):
        gens = list(gens)
        while gens:
            for g in list(gens):
                try:
                    next(g)
                    yield
                except StopIteration:
                    gens.remove(g)

    interleave(chain(front(0), front_b(0), zipg(qkconv(0), gates(0)), vblocks(0)))
    if nt > 1:
        interleave(front(1))
    for i in range(nt):
        rot[:] = [1, 2]
        if i + 1 < nt:
            interleave(mlstm_core(i), chain(front_b(i + 1), gates(i + 1)), qkconv(i + 1))
            rot[:] = [1, 2, 0]
            if i + 2 < nt:
                interleave(out_thread(i), gmlp(i), vblocks(i + 1), front(i + 2))
            else:
                interleave(out_thread(i), gmlp(i), vblocks(i + 1))
        else:
            interleave(mlstm_core(i))
            rot[:] = [1, 2, 0]
            interleave(out_thread(i), gmlp(i))
        interleave(merge(i))

    P.add("act", lambda E: E.activation(out=scr["act"][:, 0:1], in_=scr["act"][:, 1:2], func=AF.Copy), w=["end_act", "scr_act"])
    memset("dve", scr["dve"], 0.0, w=["end_dve"])
    memset("pool", scr["pool"], 0.0, w=["end_pool"])
    tr(psb(0, 0, 128), identb, identb, r=["CB"], w=["ps0", "end_pe"])
    ENDK = ["end_act", "end_dve", "end_pool", "end_pe"] + [f"mgd{i}" for i in range(nt)]

    if phase2:
        A2 = Alloc(base1)
        WOUT = A2.bf([128, 8, D])
        FF1 = A2.bf([128, 8, 4 * D])
        FF2 = A2.bf([128, 32, D])
        g1b = A2.f32([128, D])
        g2b = A2.f32([128, D])
        fgb = A2.f32([128, D])
        xt2 = [A2.f32([128, D]) for _ in range(2)]
        mg2 = [A2.bf([128, D]) for _ in range(2)]
        mgT = A2.bf([128, 8, 128])
        xhat2 = A2.bf([128, D])
        xn2Ts = [A2.bf([128, 8, 128]) for _ in range(2)]
        hidT = A2.bf([128, 32, 128])
        tmpRA = [A2.f32([128, 512]) for _ in range(2)]
        tmpRB = [A2.f32([128, 512]) for _ in range(2)]
        tmpH = [A2.f32([128, 512]) for _ in range(2)]
        junkB = tmpH[0].bitcast(BF16)
        outb = A2.f32([128, D])
        sq = A2.f32([128, 16])
        ssq2, mse2, rstd2, ssq3, mse3, rstd3 = (sq[:, k:k + 1] for k in range(6))
        P.add("act", lambda E: E.activation(out=scr["act"][:, 0:1], in_=scr["act"][:, 1:2], func=AF.Copy), r=ENDK, w=["scr_act"])
        memset("dve", scr["dve"], 0.0, r=ENDK, w=["bar_dve"])
        memset("pool", scr["pool"], 0.0, r=ENDK, w=["bar_pool"])
        tr(psb(0, 0, 128), identb, identb, r=ENDK + ["CB"], w=["ps0"])
        w_out_v = w_out.rearrange("(kc p) n -> p kc n", p=128)
        w_ff1_v = w_ff1.rearrange("(kc p) n -> p kc n", p=128)
        w_ff2_v = w_ff2.rearrange("(fc p) n -> p fc n", p=128)

        def load_gb(b):
            dma("sp", g1b, mod_d[b, 2 * D:3 * D].partition_broadcast(128), r=ENDK + ["mod_d"], w=["g1b"])
            dma("sp", g2b, mod_d[b, 5 * D:6 * D].partition_broadcast(128), r=ENDK + ["mod_d"], w=["g2b"])

        load_gb(0)
        dma("sp", fgb, final_g.partition_broadcast(128), r=ENDK, w=["fgb"])
        for kc in range(8):
            dma("pool", WOUT[:, kc, :], w_out_v[:, kc, :], r=ENDK, w=[f"wout{kc}"])
        for q4 in range(4):
            for kh in range(2):
                dma("pool", FF1[:, kh * 4:(kh + 1) * 4, q4 * 1024:(q4 + 1) * 1024],
                    w_ff1_v[:, kh * 4:(kh + 1) * 4, q4 * 1024:(q4 + 1) * 1024], r=ENDK, w=[f"ff1_{q4}"])
        for f4 in range(8):
            dma("pool", FF2[:, f4 * 4:(f4 + 1) * 4, :], w_ff2_v[:, f4 * 4:(f4 + 1) * 4, :], r=ENDK, w=[f"ff2_{f4}"])

        def load_x(i):
            s = i % 2
            dma("sp", xt2[s], x[i * 128:(i + 1) * 128, :], r=ENDK, w=[f"xt2_{s}"])

        def load_mg(i):
            s = i % 2
            dma("sp", mg2[s], merged_d[i * 128:(i + 1) * 128, :], r=ENDK + [f"mgd{i}"], w=[f"mg2_{s}"])

        def A2t(i):
            b = i // NT_SEQ
            s = i % 2
            X = xt2[s]
            xk = f"xt2_{s}"
            xn2T = xn2Ts[s]
            nk = f"xn2T{s}"
            for kc in range(8):
                tr(psb(0, kc * 128, 128), mg2[s][:, kc * 128:(kc + 1) * 128], identb, r=[f"mg2_{s}", "CB"], w=["ps0"])
            act(mgT.rearrange("p a b -> p (a b)"), psb(0), AF.Copy, r=["ps0"], w=["mgT"])
            yield
            for half in range(2):
                hs = slice(half * 512, (half + 1) * 512)
                for kc in range(8):
                    mm(psf(1 + half), mgT[:, kc, :], WOUT[:, kc, hs], kc == 0, kc == 7, r=["mgT", f"wout{kc}"],
                       w=[f"ps{1 + half}"])
                tt("dve", tmpRA[half], psf(1 + half), g1b[:, hs], ALU.mult, r=[f"ps{1 + half}", "g1b"],
                   w=[f"tmpRA{half}"])
                tt("pool", X[:, hs], X[:, hs], tmpRA[half], ALU.add, r=[xk, f"tmpRA{half}"], w=[xk])
                yield
            sumsq(X, xhat2, ssq2, [xk], "xhat2", "ssq2")
            ts("dve", mse2, ssq2, 1.0 / D, EPS, ALU.mult, ALU.add, r=["ssq2"], w=["mse2"])
            tt("pool", rstd2, mse2, mhalf[:, 0:1], ALU.pow, r=["mse2", "mhalf"], w=["rstd2"])
            act(xhat2, X, AF.Identity, scale=rstd2, r=[xk, "rstd2"], w=["xhat2"])
            yield
            for kc in range(8):
                tr(psb(0, kc * 128, 128), xhat2[:, kc * 128:(kc + 1) * 128], identb, r=["xhat2", "CB"], w=["ps0"])
            for kc in range(8):
                act(xn2T[:, kc, :], psb(0, kc * 128, 128), AF.Identity, scale=Gsh[:, 2, b * 8 + kc:b * 8 + kc + 1],
                    bias=Gsh[:, 3, b * 8 + kc:b * 8 + kc + 1], r=["ps0", "Gsh"], w=[nk])
            yield

        def B2t(i):
            b = i // NT_SEQ
            s = i % 2
            X = xt2[s]
            xk = f"xt2_{s}"
            xn2T = xn2Ts[s]
            nk = f"xn2T{s}"
            for f4 in range(8):
                bank = 3 + f4 % 2
                for j in range(4):
                    fb = f4 * 4 + j
                    for kc in range(8):
                        mm(psf(bank, j * 128, 128), FF1[:, kc, fb * 128:(fb + 1) * 128], xn2T[:, kc, :], kc == 0, kc == 7,
                           r=[nk, f"ff1_{f4 // 2}"], w=[f"ps{bank}"])
                act(tmpH[f4 % 2], psf(bank), AF.Relu, r=[f"ps{bank}"], w=[f"tmpH{f4 % 2}"])
                tt("pool" if f4 % 2 else "dve", hidT[:, f4 * 4:(f4 + 1) * 4, :].rearrange("p a b -> p (a b)"),
                   tmpH[f4 % 2], tmpH[f4 % 2], ALU.mult, r=[f"tmpH{f4 % 2}"], w=[f"hid{f4}"])
                yield
            for half in range(2):
                hs = slice(half * 512, (half + 1) * 512)
                for q4 in range(4):
                    for fc in range(q4 * 8, (q4 + 1) * 8):
                        mm(psf(5 + half), hidT[:, fc, :], FF2[:, fc, hs], fc == 0, fc == 31,
                           r=[f"hid{fc // 4}", f"ff2_{fc // 4}"], w=[f"ps{5 + half}"])
                    if q4 < 3:
                        yield
                tt("dve", tmpRB[half], psf(5 + half), g2b[:, hs], ALU.mult, r=[f"ps{5 + half}", "g2b"],
                   w=[f"tmpRB{half}"])
                tt("pool", X[:, hs], X[:, hs], tmpRB[half], ALU.add, r=[xk, f"tmpRB{half}"], w=[xk])
                yield
            sumsq(X, junkB, ssq3, [xk], "tmpH0", "ssq3")
            ts("dve", mse3, ssq3, 1.0 / D, EPS, ALU.mult, ALU.add, r=["ssq3"], w=["mse3"])
            tt("pool", rstd3, mse3, mhalf[:, 0:1], ALU.pow, r=["mse3", "mhalf"], w=["rstd3"])
            stt(outb, X, rstd3, fgb, ALU.mult, ALU.mult, r=[xk, "rstd3", "fgb"], w=["outb"])
            dma("sp", out[i * 128:(i + 1) * 128, :], outb, r=["outb"], w=[f"out{i}"])
            if i + 2 < nt:
                load_x(i + 2)
            if i + 1 < nt and (i + 1) % NT_SEQ == 0:
                load_gb((i + 1) // NT_SEQ)
            yield

        load_x(0)
        load_mg(0)
        if nt > 1:
            load_x(1)
            load_mg(1)
        interleave(A2t(0))
        for i in range(nt):
            if i + 2 < nt:
                load_mg(i + 2)
            if i + 1 < nt and (i + 1) % NT_SEQ != 0:
                interleave(B2t(i), A2t(i + 1))
            elif i + 1 < nt:
                interleave(B2t(i))
                interleave(A2t(i + 1))
            else:
                interleave(B2t(i))

    P.finalize()
    sems = {e: [es.enter_context(nc.semaphore(f"s_{e}{k}")) for k in range(SEM_K)] for e in ("pe", "act", "dve", "pool")}
    dsems = {"sp": [es.enter_context(nc.semaphore(f"d_sp{k}")) for k in range(N_DMA_SP)],
             "pool": [es.enter_context(nc.semaphore(f"d_pl{k}")) for k in range(N_DMA_POOL)]}
    with nc.Block() as block:
        @block.sync
        def _(E):
            P.emit("sp", E, sems, dsems)

        @block.gpsimd
        def _(E):
            P.emit("pool", E, sems, dsems)

        @block.scalar
        def _(E):
            P.emit("act", E, sems, dsems)

        @block.vector
        def _(E):
            P.emit("dve", E, sems, dsems)

        @block.tensor
        def _(E):
            P.emit("pe", E, sems, dsems)
    es.close()
    return nc, P


def prep_inputs(inputs):
    f = lambda a: np.ascontiguousarray(np.asarray(a, dtype=np.float32))
    x = f(inputs["x"])
    c = f(inputs["c"])
    idn = np.eye(128, dtype=np.float32)
    U = np.triu(np.ones((128, 128), np.float32))
    ones = np.ones((128, 128), np.float32)
    cid = np.arange(128) // 64
    mask2 = (cid[:, None] <= cid[None, :]).astype(np.float32)
    cfc = np.ascontiguousarray(np.concatenate([idn, U, ones, mask2], axis=1))
    cb16 = np.ascontiguousarray(np.concatenate([idn, U], axis=1).astype(ml_dtypes.bfloat16))
    conv_w = f(inputs["conv_w"])[0]
    shared = {
        "w_ada": f(inputs["w_ada"])[0], "b_ada": f(inputs["b_ada"])[0],
        "n1g": np.ascontiguousarray(f(inputs["norm1_g"])[0].reshape(8, 128).T),
        "n2g": np.ascontiguousarray(f(inputs["norm2_g"])[0].reshape(8, 128).T),
        "w_in": f(inputs["w_in"])[0],
        "cw": np.ascontiguousarray(conv_w.reshape(4, 16, 128).transpose(2, 1, 0).reshape(128, 64)),
        "cb": np.ascontiguousarray(f(inputs["conv_b"])[0].reshape(16, 128).T),
        "gate_b": np.ascontiguousarray(f(inputs["mlstm_gate_b"])[0].reshape(8)),
        "ln_g": f(inputs["gmlp_ln_g"])[0], "ln_b": f(inputs["gmlp_ln_b"])[0],
        "hn_g": f(inputs["mlstm_hn_g"])[0], "final_g": f(inputs["final_g"]),
        "wsT": np.ascontiguousarray(f(inputs["gmlp_ws"])[0].transpose(2, 0, 1).reshape(128, 1024)),
        "bs_tok": np.ascontiguousarray(f(inputs["gmlp_bs"])[0].T),
        "w_out": f(inputs["w_out"])[0], "w_ff1": f(inputs["w_ff1"])[0], "w_ff2": f(inputs["w_ff2"])[0],
        "cf": cfc, "cb16": cb16,
    }
    maps = []
    for core in range(8):
        xc = np.ascontiguousarray(x[2 * core:2 * core + 2].reshape(NT * 128, D))
        cc = c[2 * core:2 * core + 2]
        cTc = np.ascontiguousarray(cc.reshape(2, 8, 128).transpose(2, 1, 0).reshape(128, 16))
        m = dict(shared)
        m["x"] = xc
        m["cT"] = cTc
        maps.append(m)
    return maps


def kernel(**inputs):
    maps = prep_inputs(inputs)
    nc, _ = build_program()
    res = run_bass_kernel_spmd(nc, maps, core_ids=list(range(8)))
    outs = [np.asarray(r["out"], dtype=np.float32).reshape(2, SEQ, D) for r in res.results]
    return np.concatenate(outs, axis=0)
```

```python
import math
import numpy as np
import ml_dtypes
from contextlib import ExitStack
import concourse.bass as bass
import concourse.mybir as mybir
from concourse.bass_utils import run_bass_kernel_spmd

F32 = mybir.dt.float32
BF16 = mybir.dt.bfloat16
AF = mybir.ActivationFunctionType
ALU = mybir.AluOpType
AX = mybir.AxisListType

D = 1024
SEQ = 2048
NT_SEQ = SEQ // 128
NSEQ = 2
NT = NT_SEQ * NSEQ
OFF_U, OFF_V, OFF_Q, OFF_K, OFF_MV, OFF_O, OFF_I, OFF_F, OFF_GA, OFF_GB, IN_COLS = (
    0, 1024, 2048, 3072, 4096, 5120, 6144, 6148, 6152, 7176, 8200)
EPS = 1e-6
SEM_M = 8000
SEM_K = 6
N_DMA_SP = 12
N_DMA_POOL = 6
STRICT = True


class Op:
    __slots__ = ("eng", "fn", "dma", "gid", "deps", "pos", "ms", "msn", "sem", "val")


class Prog:
    ENG = ("pe", "act", "dve", "pool", "sp")

    def __init__(self):
        self.q = {e: [] for e in self.ENG}
        self.lastw = {}
        self.readers = {}
        self.n = 0

    def add(self, eng, fn, r=(), w=(), dma=False):
        op = Op()
        op.eng, op.fn, op.dma, op.gid = eng, fn, dma, self.n
        self.n += 1
        op.ms = False
        r = list(r)
        w = list(w)
        for k in list(r):
            if k.startswith("ps"):
                r.remove(k)
                if k not in w:
                    w.append(k)
        deps = {}
        for k in r:
            o = self.lastw.get(k)
            if o is not None:
                deps[o.gid] = o
        for k in w:
            o = self.lastw.get(k)
            if o is not None:
                deps[o.gid] = o
            for o in self.readers.get(k, ()):
                deps[o.gid] = o
        for k in w:
            self.lastw[k] = op
            self.readers[k] = []
        for k in r:
            self.readers.setdefault(k, []).append(op)
        deps.pop(op.gid, None)
        op.deps = list(deps.values())
        op.pos = len(self.q[eng])
        self.q[eng].append(op)
        return op

    def _needs_wait(self, op, d):
        if d.dma:
            return True
        if d.eng == op.eng:
            if op.eng == "pe":
                return False
            if op.dma:
                return True
            return STRICT or (op.pos - d.pos <= 2)
        return True

    def finalize(self):
        for e in self.ENG:
            for op in self.q[e]:
                for d in op.deps:
                    if (not d.dma) and self._needs_wait(op, d):
                        d.ms = True
        self.dma_count = {}
        for e in self.ENG:
            c = 0
            j = 0
            npool = N_DMA_SP if e == "sp" else N_DMA_POOL
            for op in self.q[e]:
                if op.dma:
                    op.sem = j % npool
                    op.val = 16 * (j // npool + 1)
                    j += 1
                elif op.ms:
                    c += 1
                    op.msn = c
            self.dma_count[e] = j
            assert c <= SEM_M * SEM_K, (e, c)

    def emit(self, e, E, sems, dsems):
        waited = {}

        def wait(sem, val):
            key = id(sem)
            if waited.get(key, 0) >= val:
                return
            E.wait_ge(sem, val)
            waited[key] = val

        def semfor(eng, n):
            return sems[eng][(n - 1) // SEM_M], (n - 1) % SEM_M + 1

        for op in self.q[e]:
            for d in op.deps:
                if not self._needs_wait(op, d):
                    continue
                if d.dma:
                    wait(dsems[d.eng][d.sem], d.val)
                else:
                    s, v = semfor(d.eng, d.msn)
                    wait(s, v)
            if op.dma:
                if op.val > 16:
                    wait(dsems[e][op.sem], op.val - 16)
                ins = op.fn(E)
                ins.then_inc(dsems[e][op.sem], 16)
            else:
                ins = op.fn(E)
                if op.ms:
                    s, v = semfor(e, op.msn)
                    ins.then_inc(s, 1)
        j = self.dma_count.get(e, 0)
        npool = N_DMA_SP if e == "sp" else N_DMA_POOL
        for si in range(min(j, npool)):
            last = (j - 1 - si) // npool * npool + si
            if last < 0:
                continue
            wait(dsems[e][si], 16 * (last // npool + 1))


def build_program(nt=NT, phase2=True, dbg=False):
    nc = bass.Bass("TRN2", target_bir_lowering=False)

    def din(name, shape, dt=F32):
        return nc.dram_tensor(name, list(shape), dt, kind="ExternalInput").ap()

    x = din("x", [NT * 128, D])
    cT = din("cT", [128, 16])
    w_ada = din("w_ada", [D, 6 * D])
    b_ada = din("b_ada", [6 * D])
    n1g = din("n1g", [128, 8])
    n2g = din("n2g", [128, 8])
    w_in = din("w_in", [D, IN_COLS])
    cw = din("cw", [128, 64])
    cb = din("cb", [128, 16])
    gate_b = din("gate_b", [8])
    ln_g = din("ln_g", [D])
    ln_b = din("ln_b", [D])
    hn_g = din("hn_g", [D])
    final_g = din("final_g", [D])
    wsT = din("wsT", [128, 8 * 128])
    bs_tok = din("bs_tok", [128, 8])
    w_out = din("w_out", [D, D])
    w_ff1 = din("w_ff1", [D, 4 * D])
    w_ff2 = din("w_ff2", [4 * D, D])
    cf = din("cf", [128, 4 * 128])
    cb16 = din("cb16", [128, 2 * 128], BF16)
    out = nc.dram_tensor("out", [NT * 128, D], F32, kind="ExternalOutput").ap()
    if dbg:
        merged_d = nc.dram_tensor("merged_d", [NT * 128, D], BF16, kind="ExternalOutput").ap()
        mod_d = nc.dram_tensor("mod_d", [2, 6 * D], F32, kind="ExternalOutput").ap()
    else:
        merged_d = nc.dram_tensor("merged_d", [NT * 128, D], BF16, kind="Internal").ap()
        mod_d = nc.dram_tensor("mod_d", [2, 6 * D], F32, kind="Internal").ap()

    P = Prog()
    es = ExitStack()
    ARENA_BYTES = 212800
    arena = es.enter_context(nc.sbuf_tensor("arena", [128, ARENA_BYTES // 2], BF16))
    psum = [es.enter_context(nc.psum_tensor(f"psb{i}", [128, 512], F32)) for i in range(8)]

    class Alloc:
        def __init__(self, base=0):
            self.off = base

        def take(self, nbytes):
            o = (self.off + 7) // 8 * 8
            self.off = o + nbytes
            assert self.off <= ARENA_BYTES, self.off
            return o

        def bf(self, shape):
            n = int(np.prod(shape[1:]))
            o = self.take(n * 2)
            return self._shape(arena[:, o // 2:o // 2 + n], shape)

        def f32(self, shape):
            n = int(np.prod(shape[1:]))
            o = self.take(n * 4)
            return self._shape(arena[:, o // 2:o // 2 + 2 * n].bitcast(F32), shape)

        @staticmethod
        def _shape(v, shape):
            if len(shape) == 3:
                return v.rearrange("p (a b) -> p a b", a=shape[1])
            if len(shape) == 4:
                return v.rearrange("p (a b c) -> p a b c", a=shape[1], b=shape[2])
            return v

    def psf(i, lo=0, n=512):
        return psum[i][:, lo:lo + n]

    def psb(i, lo=0, n=1024):
        return psum[i][:].bitcast(BF16)[:, lo:lo + n]

    A0 = Alloc(0)
    CF = A0.f32([128, 4, 128])
    CB = A0.bf([128, 2, 128])
    identf, Umat, onesf, mask2 = CF[:, 0, :], CF[:, 1, :], CF[:, 2, :], CF[:, 3, :]
    identb, maskT = CB[:, 0, :], CB[:, 1, :]
    modT = A0.f32([128, 96])
    n1g_s = A0.f32([128, 8])
    n2g_s = A0.f32([128, 8])
    Gsh = A0.f32([128, 4, 16])
    mhalf = A0.f32([128, 4])
    onesb = A0.bf([128, 4])
    scr = {e: A0.f32([128, 4]) for e in ("act", "dve", "pool")}
    base1 = A0.off

    def mm(outap, lhsT, rhs, start, stop, r=(), w=()):
        return P.add("pe", lambda E: E.matmul(outap, lhsT=lhsT, rhs=rhs, start=start, stop=stop), r=r, w=w)

    def tr(outap, inap, ident, r=(), w=()):
        return P.add("pe", lambda E: E.transpose(outap, inap, ident), r=r, w=w)

    def act(outap, inap, func, r=(), w=(), scale=1.0, bias=None):
        def fn(E):
            if bias is not None:
                return E.activation(out=outap, in_=inap, func=func, scale=scale, bias=bias)
            return E.activation(out=outap, in_=inap, func=func, scale=scale)
        return P.add("act", fn, r=r, w=w)

    def ts(eng, outap, in0, s1, s2, op0, op1=None, r=(), w=()):
        def fn(E):
            if op1 is None:
                return E.tensor_scalar(out=outap, in0=in0, scalar1=s1, scalar2=None, op0=op0)
            return E.tensor_scalar(out=outap, in0=in0, scalar1=s1, scalar2=s2, op0=op0, op1=op1)
        return P.add(eng, fn, r=r, w=w)

    def tt(eng, outap, in0, in1, op, r=(), w=()):
        return P.add(eng, lambda E: E.tensor_tensor(out=outap, in0=in0, in1=in1, op=op), r=r, w=w)

    def stt(outap, in0, scalar, in1, op0, op1, r=(), w=()):
        return P.add("dve", lambda E: E.scalar_tensor_tensor(out=outap, in0=in0, scalar=scalar, in1=in1,
                                                             op0=op0, op1=op1), r=r, w=w)

    def dma(eng, outap, inap, r=(), w=()):
        return P.add(eng, lambda E: E.dma_start(out=outap, in_=inap), r=r, w=w, dma=True)

    def memset(eng, ap, val, r=(), w=()):
        return P.add(eng, lambda E: E.memset(ap, val), r=r, w=w)

    def sumsq(src, junk, dst, rk, jk, dk):
        def fn(E):
            return E.activation(out=junk, in_=src, func=AF.Square, accum_out=dst)
        P.add("act", fn, r=rk, w=[jk, dk + "_raw"])
        P.add("act", lambda E: E.activation(out=scr["act"][:, 0:1], in_=scr["act"][:, 1:2], func=AF.Copy),
              r=[dk + "_raw"], w=[dk, "scr_act"])

    dma("sp", CF, cf.rearrange("p (a b) -> p a b", a=4), w=["CF"])
    dma("sp", CB, cb16.rearrange("p (a b) -> p a b", a=2), w=["CB"])
    dma("sp", n1g_s, n1g, w=["n1g"])
    dma("sp", n2g_s, n2g, w=["n2g"])
    memset("pool", mhalf, -0.5, w=["mhalf"])
    memset("pool", onesb, 1.0, w=["onesb"])
    memset("pool", scr["act"], 0.0, w=["scr_act"])

    A1 = Alloc(base1)
    WIN = A1.bf([128, 8, IN_COLS])
    wsTm = A1.bf([128, 8, 128])
    bst = A1.f32([128, 8])
    cw_s = A1.f32([128, 16, 4])
    cb_s = A1.f32([128, 16])
    gb8 = A1.f32([128, 8])
    lng = A1.f32([128, D])
    lnb = A1.f32([128, D])
    hng = A1.f32([128, D])
    Cst = A1.f32([128, 4, 512])
    nst = A1.f32([128, 4, 2])
    Cdec = A1.bf([128, 4, 512])
    vg = Cdec.rearrange("p a b -> p (a b)").bitcast(F32)
    ndec = A1.bf([128, 4, 2])
    carry = A1.f32([128, 4])
    Rprev = A1.f32([128, 4])
    hal = A1.f32([128, 16, 3])
    pq4 = [A1.f32([128, 4, 131]) for _ in range(2)]
    xt = A1.f32([128, D])
    xhat = A1.bf([128, D])
    xnT2 = [A1.bf([128, 8, 128]) for _ in range(2)]
    Vb2 = [A1.bf([128, D]) for _ in range(2)]
    qkT2 = [A1.bf([128, 16, 128]) for _ in range(2)]
    acc4s = [A1.f32([128, 4, 128]) for _ in range(2)]
    th4_one = A1.f32([128, 4, 128])
    th4s = [th4_one, th4_one]
    wk = A1.bf([128, D])
    PT = A1.bf([128, 4, 128])
    yb = A1.f32([128, D])
    tmpA = [A1.f32([128, 512]) for _ in range(2)]
    tmpB = [A1.f32([128, 512]) for _ in range(2)]
    mg = A1.bf([128, D])
    vn = mg
    sm = A1.f32([128, 160])
    ssq, mse, rstd = sm[:, 0:1], sm[:, 1:2], sm[:, 2:3]
    g8 = sm[:, 8:16]
    li, fp = g8[:, 0:4], g8[:, 4:8]
    af, ee, dd, uu = sm[:, 16:20], sm[:, 20:24], sm[:, 24:28], sm[:, 28:32]
    s2, tq, mn, lf = sm[:, 32:36], sm[:, 36:40], sm[:, 40:44], sm[:, 44:48]
    Bt, ah, amax, Rnew = sm[:, 48:52], sm[:, 52:56], sm[:, 56:60], sm[:, 60:64]
    eargs, eout = sm[:, 64:76], sm[:, 76:88]
    wv, flv, decv = eout[:, 0:4], eout[:, 4:8], eout[:, 8:12]
    den4, rr = sm[:, 88:92], sm[:, 92:96]
    bst6 = sm[:, 96:120].rearrange("p (a b) -> p a b", a=4)
    mv4 = sm[:, 120:128].rearrange("p (a b) -> p a b", a=4)
    rs4 = sm[:, 128:132]
    vst, vmv, vrs = sm[:, 132:144], sm[:, 144:146], sm[:, 146:147]
    sm2 = A1.f32([128, 16])
    uu2 = sm2[:, 0:4]
    eouts = [sm[:, 76:88], sm2[:, 4:16]]
    nb4 = sm[:, 148:152]
    nmean = sm[:, 152:153]
    end1 = A1.off

    wsT_f = tmpA[0].rearrange("p (a b) -> p a b", a=4)
    wsT_v = wsT.rearrange("p (a b) -> p a b", a=8)
    for hlf in range(2):
        dma("sp", wsT_f, wsT_v[:, hlf * 4:(hlf + 1) * 4, :], w=["tmpA0"])
        tt("dve", wsTm[:, hlf * 4:(hlf + 1) * 4, :], wsT_f, mask2.unsqueeze(1).to_broadcast([128, 4, 128]),
           ALU.mult, r=["tmpA0", "CF"], w=["wsTm"])
    dma("sp", bst, bs_tok, w=["bst"])
    dma("sp", cw_s, cw.rearrange("p (a b) -> p a b", a=16), w=["cw"])
    dma("sp", cb_s, cb, w=["cbs"])
    dma("sp", gb8, gate_b.partition_broadcast(128), w=["gb8"])
    dma("sp", lng, ln_g.partition_broadcast(128), w=["lng"])
    dma("sp", lnb, ln_b.partition_broadcast(128), w=["lnb"])
    dma("sp", hng, hn_g.partition_broadcast(128), w=["hng"])

    csT = tmpA[1][:, 0:16]
    cth = tmpA[1][:, 16:32]
    dma("sp", csT, cT, w=["csT"])
    act(cth, csT, AF.Tanh, scale=0.5, r=["csT"], w=["cth"])
    stt(cth, cth, 1.0, csT, ALU.add, ALU.mult, r=["cth", "csT"], w=["cth"])
    ts("dve", csT, cth, 0.5, None, ALU.mult, r=["cth"], w=["csT"])
    csT3 = csT.rearrange("p (a b) -> p a b", a=8)
    w_ada_v = w_ada.rearrange("(kc p) n -> p kc n", p=128)
    wst = [WIN[:, 6, 0:8192].bitcast(F32).rearrange("p (a b) -> p a b", a=8),
           WIN[:, 7, 0:8192].bitcast(F32).rearrange("p (a b) -> p a b", a=8)]
    wkeys = ["win6", "win7"]
    brow = [tmpB[1][0:2, 0:512], tmpB[1][0:2, 0:512]]
    mrow = [tmpA[0][0:2, 0:512], tmpB[0][0:2, 0:512]]
    for j in range(12):
        sb = j % 2
        dma("sp", wst[sb], w_ada_v[:, :, j * 512:(j + 1) * 512], w=[wkeys[sb]])
        dma("sp", brow[sb], b_ada[j * 512:(j + 1) * 512].partition_broadcast(2), w=["tmpB1"])
        for kc in range(8):
            mm(psf(1 + sb)[0:2, :], csT3[:, kc, :], wst[sb][:, kc, :], kc == 0, kc == 7,
               r=[wkeys[sb], "csT"], w=[f"ps{1 + sb}"])
        mk = "tmpA0" if sb == 0 else "tmpB0"
        tt("dve", mrow[sb], psf(1 + sb)[0:2, :], brow[sb], ALU.add, r=[f"ps{1 + sb}", "tmpB1"], w=[mk])
        dma("sp", mod_d[:, j * 512:(j + 1) * 512], mrow[sb], r=[mk], w=["mod_d"])
    modin = tmpB[1][0:96, 0:128]
    dma("sp", modin, mod_d.rearrange("b (c p) -> (b c) p", p=128), r=["mod_d"], w=["tmpB1"])
    tr(psf(1)[:, 0:96], modin, identf[0:96, 0:96], r=["tmpB1", "CF"], w=["ps1"])
    P.add("dve", lambda E: E.tensor_copy(out=modT, in_=psf(1)[:, 0:96]), r=["ps1"], w=["modT"])
    for b in range(2):
        o = b * 48
        stt(Gsh[:, 0, b * 8:(b + 1) * 8], modT[:, o + 8:o + 16], 1.0, n1g_s, ALU.add, ALU.mult,
            r=["modT", "n1g"], w=["Gsh"])
        P.add("dve", lambda E, b=b, o=o: E.tensor_copy(out=Gsh[:, 1, b * 8:(b + 1) * 8], in_=modT[:, o:o + 8]),
              r=["modT"], w=["Gsh"])
        stt(Gsh[:, 2, b * 8:(b + 1) * 8], modT[:, o + 32:o + 40], 1.0, n2g_s, ALU.add, ALU.mult,
            r=["modT", "n2g"], w=["Gsh"])
        P.add("dve", lambda E, b=b, o=o: E.tensor_copy(out=Gsh[:, 3, b * 8:(b + 1) * 8], in_=modT[:, o + 24:o + 32]),
              r=["modT"], w=["Gsh"])

    w_in_v = w_in.rearrange("(kc p) n -> p kc n", p=128)
    WGRP = [(OFF_Q, OFF_MV), (OFF_MV, OFF_GA), (0, OFF_Q), (OFF_GA, IN_COLS)]

    def wkey(c0):
        for gi, (lo, hi) in enumerate(WGRP):
            if lo <= c0 < hi:
                return f"winG{gi}"
        raise ValueError(c0)

    for gi, (lo, hi) in enumerate(WGRP):
        dma("pool", WIN[:, 0:3, lo:hi], w_in_v[:, 0:3, lo:hi], w=[f"winG{gi}"])
        dma("pool", WIN[:, 3:6, lo:hi], w_in_v[:, 3:6, lo:hi], w=[f"winG{gi}"])
    for gi, (lo, hi) in enumerate(WGRP):
        for kc in (6, 7):
            dma("pool", WIN[:, kc, lo:hi], w_in_v[:, kc, lo:hi], w=[f"winG{gi}", f"win{kc}"])

    tokbank = [0]

    def xn(i):
        return xnT2[i % 2], f"xnT{i % 2}"

    def tokblk(i, c0):
        bank = nextbank()
        X, xk = xn(i)
        for kc in range(8):
            mm(psf(bank), X[:, kc, :], WIN[:, kc, c0:c0 + 512], kc == 0, kc == 7,
               r=[xk, wkey(c0)], w=[f"ps{bank}"])
        return bank

    def eo(i):
        e = eouts[i % 2]
        return e, e[:, 0:4], e[:, 4:8], e[:, 8:12], f"eout{i % 2}"

    rot = [1, 2]

    def nextbank():
        tokbank[0] += 1
        return rot[tokbank[0] % len(rot)]

    def front(i):
        b = i // NT_SEQ
        X, xk = xn(i)
        dma("sp", xt, x[i * 128:(i + 1) * 128, :], w=["xt"])
        sumsq(xt, xhat, ssq, ["xt"], "xhat", "ssq")
        ts("dve", mse, ssq, 1.0 / D, EPS, ALU.mult, ALU.add, r=["ssq"], w=["mse"])
        tt("pool", rstd, mse, mhalf[:, 0:1], ALU.pow, r=["mse", "mhalf"], w=["rstd"])
        act(xhat, xt, AF.Identity, scale=rstd, r=["xt", "rstd"], w=["xhat"])
        yield

    def front_b(i):
        b = i // NT_SEQ
        X, xk = xn(i)
        for kc in range(8):
            tr(psb(0, kc * 128, 128), xhat[:, kc * 128:(kc + 1) * 128], identb, r=["xhat", "CB"], w=["ps0"])
        for kc in range(8):
            act(X[:, kc, :], psb(0, kc * 128, 128), AF.Identity, scale=Gsh[:, 0, b * 8 + kc:b * 8 + kc + 1],
                bias=Gsh[:, 1, b * 8 + kc:b * 8 + kc + 1], r=["ps0", "Gsh"], w=[xk])
        yield

    def qkconv(i):
        X, xk = xn(i)
        qkT = qkT2[i % 2]
        qk = f"qkT{i % 2}"
        if i % NT_SEQ == 0:
            memset("pool", hal, 0.0, w=["hal"])

        def names(g4):
            return (3 + g4 % 2, pq4[g4 % 2], f"pq4_{g4 % 2}", acc4s[g4 % 2], th4s[g4 % 2], f"a{g4 % 2}_",
                    "th4")

        def stA(g4):
            bank, buf, bk, acc4, th4c, ak, tk = names(g4)
            for j in range(4):
                blk = g4 * 4 + j
                for kc in range(8):
                    mm(psf(bank, j * 128, 128), WIN[:, kc, OFF_Q + blk * 128:OFF_Q + (blk + 1) * 128], X[:, kc, :],
                       kc == 0, kc == 7, r=[xk, "winG0"], w=[f"ps{bank}"])
            P.add("pool", lambda E, buf=buf, g4=g4: E.tensor_copy(out=buf[:, :, 0:3], in_=hal[:, g4 * 4:(g4 + 1) * 4, :]),
                  r=["hal"], w=[bk + "h"])
            act(buf[:, :, 3:131], psf(bank).rearrange("p (a b) -> p a b", a=4), AF.Copy, r=[f"ps{bank}"], w=[bk])
            P.add("pool", lambda E, buf=buf, g4=g4: E.tensor_copy(out=hal[:, g4 * 4:(g4 + 1) * 4, :], in_=buf[:, :, 128:131]),
                  r=[bk], w=["hal"])
            for j in range(4):
                blk = g4 * 4 + j
                act(acc4[:, j, :], buf[:, j, 3:131], AF.Identity, scale=cw_s[:, blk, 3:4], bias=cb_s[:, blk:blk + 1],
                    r=[bk, "cw", "cbs"], w=[f"{ak}{j}"])

        def stB(g4):
            bank, buf, bk, acc4, th4c, ak, tk = names(g4)
            for tap in range(3):
                for j in range(4):
                    blk = g4 * 4 + j
                    stt(acc4[:, j, :], buf[:, j, tap:tap + 128], cw_s[:, blk, tap:tap + 1], acc4[:, j, :],
                        ALU.mult, ALU.add, r=[bk, bk + "h", "cw", f"{ak}{j}"], w=[f"{ak}{j}"])

        def stC(g4):
            bank, buf, bk, acc4, th4c, ak, tk = names(g4)
            act(th4c, acc4, AF.Tanh, scale=0.5, r=[f"{ak}{j}" for j in range(4)], w=[tk])
            stt(qkT[:, g4 * 4:(g4 + 1) * 4, :], th4c, 1.0, acc4, ALU.add, ALU.mult,
                r=[tk] + [f"{ak}{j}" for j in range(4)], w=[qk])

        for st, g in ((stA, 0), (stA, 1), (stB, 0), (stC, 0), (stB, 1), (stA, 2), (stC, 1), (stB, 2), (stA, 3),
                      (stC, 2), (stB, 3), (stC, 3)):
            st(g)
            yield

    def vblocks(i):
        Vb = Vb2[i % 2]
        for half in range(2):
            bank = tokblk(i, OFF_MV + half * 512)
            act(Vb[:, half * 512:(half + 1) * 512], psf(bank), AF.Copy, r=[f"ps{bank}"], w=[f"Vb{i % 2}"])
            yield

    def gates(i):
        X, xk = xn(i)
        eout, wv, flv, decv, ek = eo(i)
        if i % NT_SEQ == 0:
            memset("pool", carry, 0.0, w=["carry"])
            memset("pool", Rprev, 0.0, w=["Rprev"])
        for kc in range(8):
            mm(psf(5, 300, 8), X[:, kc, :], WIN[:, kc, OFF_I:OFF_I + 8], kc == 0, kc == 7,
               r=[xk, "winG1"], w=["ps5"])
        tt("dve", g8, psf(5, 300, 8), gb8, ALU.add, r=["ps5", "gb8"], w=["g8"])
        yield
        stt(af, fp, -1.0, fp, ALU.mult, ALU.max, r=["g8"], w=["af"])
        act(ee, af, AF.Exp, scale=-1.0, r=["af"], w=["ee"])
        ts("dve", mn, fp, 0.0, None, ALU.min, r=["g8"], w=["mn"])
        yield
        ts("dve", dd, ee, 2.0, None, ALU.add, r=["ee"], w=["dd"])
        P.add("dve", lambda E: E.reciprocal(out=uu2, in_=dd), r=["dd"], w=["uu2"])
        tt("dve", uu, ee, uu2, ALU.mult, r=["ee", "uu2"], w=["uu"])
        tt("dve", s2, uu, uu, ALU.mult, r=["uu"], w=["s2"])
        yield
        ts("dve", tq, s2, 1.0 / 13.0, None, ALU.mult, r=["s2"], w=["tq"])
        for cc in (1.0 / 11, 1.0 / 9, 1.0 / 7, 1.0 / 5, 1.0 / 3):
            stt(tq, tq, cc, s2, ALU.add, ALU.mult, r=["tq", "s2"], w=["tq"])
            yield
        stt(tq, tq, 1.0, uu, ALU.add, ALU.mult, r=["tq", "uu"], w=["tq"])
        stt(lf, tq, -2.0, mn, ALU.mult, ALU.add, r=["tq", "mn"], w=["lf"])
        yield
        yield
        mm(psf(5, 308, 4), Umat, lf, True, True, r=["lf", "CF"], w=["ps5"])
        mm(psf(5, 312, 4), onesf, lf, True, True, r=["lf", "CF"], w=["ps5"])
        yield
        tt("dve", Bt, psf(5, 308, 4), carry, ALU.add, r=["ps5", "carry"], w=["Bt"])
        tt("dve", carry, psf(5, 312, 4), carry, ALU.add, r=["ps5", "carry"], w=["carry"])
        tt("dve", ah, li, Bt, ALU.subtract, r=["g8", "Bt"], w=["ah"])
        Dmb = tmpA[0].rearrange("p (a b) -> p a b", a=4)
        tt("dve", Dmb, identf.unsqueeze(1).to_broadcast([128, 4, 128]),
           ah.unsqueeze(2).to_broadcast([128, 4, 128]), ALU.mult, r=["CF", "ah"], w=["tmpA0"])
        yield
        yield
        bank = nextbank()
        mm(psf(bank), onesf, tmpA[0], True, True, r=["tmpA0", "CF"], w=[f"ps{bank}"])
        P.add("dve", lambda E, bank=bank: E.tensor_reduce(out=amax, in_=psf(bank).rearrange("p (a b) -> p a b", a=4),
                                                          axis=AX.X, op=ALU.max), r=[f"ps{bank}"], w=["amax"])
        yield
        tt("dve", Rnew, amax, Rprev, ALU.max, r=["amax", "Rprev"], w=["Rnew"])
        tt("dve", eargs[:, 0:4], ah, Rnew, ALU.subtract, r=["ah", "Rnew"], w=["ea0"])
        stt(eargs[:, 4:8], Bt, -1.0, Rnew, ALU.mult, ALU.subtract, r=["Bt", "Rnew"], w=["ea1"])
        ts("dve", eargs[:, 4:8], eargs[:, 4:8], math.log(64.0), None, ALU.add, r=["ea1"], w=["ea1"])
        tt("dve", eargs[:, 8:12], Rprev, Rnew, ALU.subtract, r=["Rprev", "Rnew"], w=["ea2"])
        act(eout, eargs, AF.Exp, r=["ea0", "ea1", "ea2"], w=[ek])
        P.add("pool", lambda E: E.tensor_copy(out=Rprev, in_=Rnew), r=["Rnew", "ea2"], w=["Rprev"])
        yield

    def gmlp(i):
        for half in range(2):
            bank = tokblk(i, OFF_V + half * 512)
            act(vg[:, half * 512:(half + 1) * 512], psf(bank), AF.Gelu_apprx_tanh, r=[f"ps{bank}"], w=["Cdec"])
            yield
        P.add("dve", lambda E: E.bn_stats(out=vst[:, 0:6], in_=vg[:, 0:512]), r=["Cdec"], w=["vst0"])
        P.add("dve", lambda E: E.bn_stats(out=vst[:, 6:12], in_=vg[:, 512:1024]), r=["Cdec"], w=["vst1"])
        P.add("dve", lambda E: E.bn_aggr(out=vmv, in_=vst), r=["vst0", "vst1"], w=["vmv"])
        ts("dve", vrs, vmv[:, 1:2], EPS, None, ALU.add, r=["vmv"], w=["vrs"])
        tt("pool", vrs, vrs, mhalf[:, 0:1], ALU.pow, r=["vrs", "mhalf"], w=["vrs"])
        yield
        stt(vg, vg, vmv[:, 0:1], lng, ALU.subtract, ALU.mult, r=["Cdec", "vmv", "lng"], w=["Cdec"])
        yield
        stt(vn, vg, vrs, lnb, ALU.mult, ALU.add, r=["Cdec", "vrs", "lnb"], w=["mg"])
        yield
        yield
        for g in range(8):
            mm(psf(3 + g // 4, (g % 4) * 128, 128), wsTm[:, g, :], vn[:, g * 128:(g + 1) * 128], True, True,
               r=["wsTm", "mg"], w=[f"ps{3 + g // 4}"])
        yield
        for half in range(2):
            bank = tokblk(i, OFF_U + half * 512)
            act(tmpA[half], psf(bank), AF.Gelu_apprx_tanh, r=[f"ps{bank}"], w=[f"tmpA{half}"])
            for gg in range(4):
                g = half * 4 + gg
                stt(tmpB[half][:, gg * 128:(gg + 1) * 128], psf(3 + half, gg * 128, 128), bst[:, g:g + 1],
                    tmpA[half][:, gg * 128:(gg + 1) * 128], ALU.add, ALU.mult,
                    r=[f"ps{3 + half}", "bst", f"tmpA{half}"], w=[f"tmpB{half}"])
            yield
            bank = tokblk(i, OFF_GA + half * 512)
            act(tmpA[half], psf(bank), AF.Tanh, scale=0.5, r=[f"ps{bank}"], w=[f"tmpA{half}"])
            stt(tmpB[half], tmpA[half], 1.0, tmpB[half], ALU.add, ALU.mult, r=[f"tmpA{half}", f"tmpB{half}"],
                w=[f"tmpB{half}"])
            yield

    def mlstm_core(i):
        eout, wv, flv, decv, ek = eo(i)
        qkT = qkT2[i % 2]
        Vb = Vb2[i % 2]
        vk = f"Vb{i % 2}"
        qk = f"qkT{i % 2}"
        if i % NT_SEQ == 0:
            memset("pool", Cst, 0.0, w=["C"])
            memset("pool", nst, 0.0, w=["n"])
        for h in range(4):
            act(Cdec[:, h, :], Cst[:, h, :], AF.Identity, scale=decv[:, h:h + 1], r=["C", ek], w=["Cdec"])
        tt("dve", ndec, nst, decv.unsqueeze(2).to_broadcast([128, 4, 2]), ALU.mult, r=["n", ek], w=["ndec"])
        yield
        for kb in range(8):
            tr(psb(0, kb * 128, 128), qkT[:, 8 + kb, :], identb, r=[qk, "CB"], w=["ps0"])
        for h in range(4):
            act(wk[:, h * 256:(h + 1) * 256], psb(0, h * 256, 256), AF.Identity, scale=wv[:, h:h + 1],
                r=["ps0", ek], w=["wk"])
        yield
        for hp in range(2):
            for hh in range(2):
                h = 2 * hp + hh
                for kc in range(2):
                    mm(psf(5, hh * 128, 128), qkT[:, 8 + 2 * h + kc, :], qkT[:, 2 * h + kc, :], kc == 0, kc == 1,
                       r=[qk], w=["ps5"])
            for hh in range(2):
                h = 2 * hp + hh
                stt(PT[:, h, :], psf(5, hh * 128, 128), wv[:, h:h + 1], maskT, ALU.mult, ALU.mult,
                    r=["ps5", ek, "CB"], w=["PT"])
            yield
            for hh in range(2):
                h = 2 * hp + hh
                nb = 6 + hp
                Vh = Vb[:, h * 256:(h + 1) * 256]
                mm(psf(nb, hh * 256, 256), PT[:, h, :], Vh, True, False, r=["PT", vk], w=[f"ps{nb}"])
                mm(psf(nb, hh * 256, 256), qkT[:, 2 * h, :], Cdec[:, h, 0:256], False, False, r=[qk, "Cdec"], w=[f"ps{nb}"])
                mm(psf(nb, hh * 256, 256), qkT[:, 2 * h + 1, :], Cdec[:, h, 256:512], False, True, r=[qk, "Cdec"], w=[f"ps{nb}"])
                mm(psf(5, 256 + h, 1), PT[:, h, :], onesb[:, 0:1], True, False, r=["PT", "onesb"], w=["ps5"])
                mm(psf(5, 256 + h, 1), qkT[:, 2 * h, :], ndec[:, h, 0:1], False, False, r=[qk, "ndec"], w=["ps5"])
                mm(psf(5, 256 + h, 1), qkT[:, 2 * h + 1, :], ndec[:, h, 1:2], False, True, r=[qk, "ndec"], w=["ps5"])
                bc = nextbank()
                for kc in range(2):
                    mm(psf(bc, kc * 256, 256), wk[:, h * 256 + kc * 128:h * 256 + (kc + 1) * 128], Vh, True, True,
                       r=["wk", vk], w=[f"ps{bc}"])
                for kc in range(2):
                    mm(psf(5, 260 + 2 * h + kc, 1), wk[:, h * 256 + kc * 128:h * 256 + (kc + 1) * 128], onesb[:, 0:1],
                       True, True, r=["wk", "onesb"], w=["ps5"])
                stt(Cst[:, h, :], Cst[:, h, :], decv[:, h:h + 1], psf(bc), ALU.mult, ALU.add,
                    r=["C", ek, f"ps{bc}", "Cdec"], w=["C"])
                stt(nst[:, h, :], nst[:, h, :], decv[:, h:h + 1], psf(5, 260 + 2 * h, 2), ALU.mult, ALU.add,
                    r=["n", ek, "ps5", "ndec"], w=["n"])
                P.add("dve", lambda E, h=h, nb=nb, hh=hh: E.bn_stats(out=bst6[:, h, :], in_=psf(nb, hh * 256, 256)),
                      r=[f"ps{nb}"], w=[f"bst{h}"])
                yield

    def out_thread(i):
        eout, wv, flv, decv, ek = eo(i)
        P.add("dve", lambda E: E.tensor_copy(out=den4, in_=psf(5, 256, 4)), r=["ps5"], w=["den4"])
        stt(den4, den4, -1.0, den4, ALU.mult, ALU.max, r=["den4"], w=["den4"])
        tt("dve", den4, den4, flv, ALU.max, r=["den4", ek], w=["den4"])
        for h in range(4):
            P.add("dve", lambda E, h=h: E.bn_aggr(out=mv4[:, h, :], in_=bst6[:, h, :]), r=[f"bst{h}"], w=[f"mv{h}"])
        yield
        P.add("dve", lambda E: E.reciprocal(out=rr, in_=den4), r=["den4"], w=["rr"])
        tt("dve", rs4, rr, rr, ALU.mult, r=["rr"], w=["rs4"])
        tt("dve", rs4, rs4, mv4[:, :, 1], ALU.mult, r=["rs4"] + [f"mv{h}" for h in range(4)], w=["rs4"])
        ts("dve", rs4, rs4, EPS, None, ALU.add, r=["rs4"], w=["rs4"])
        tt("pool", rs4, rs4, mhalf, ALU.pow, r=["rs4", "mhalf"], w=["rs4"])
        yield
        stt(rs4, rs4, 0.25, rr, ALU.mult, ALU.mult, r=["rs4", "rr"], w=["rs4"])
        stt(nb4, mv4[:, :, 0], -1.0, rs4, ALU.mult, ALU.mult, r=["rs4"] + [f"mv{h}" for h in range(4)], w=["nb4"])
        for h in range(4):
            nb = 6 + h // 2
            act(yb[:, h * 256:(h + 1) * 256], psf(nb, (h % 2) * 256, 256), AF.Identity, scale=rs4[:, h:h + 1],
                bias=nb4[:, h:h + 1], r=[f"ps{nb}", "nb4", "rs4"], w=["yb"])
        tt("dve", yb, yb, hng, ALU.mult, r=["yb", "hng"], w=["yb"])
        rot[:] = [1, 2, 0, 6, 7]
        yield
        for c0 in (OFF_O, OFF_GB):
            for half in range(2):
                bank = tokblk(i, c0 + half * 512)
                act(tmpA[half], psf(bank), AF.Tanh, scale=0.5, r=[f"ps{bank}"], w=[f"tmpA{half}"])
                stt(yb[:, half * 512:(half + 1) * 512], tmpA[half], 1.0, yb[:, half * 512:(half + 1) * 512],
                    ALU.add, ALU.mult, r=[f"tmpA{half}", "yb"], w=["yb"])
                yield

    def merge(i):
        for half in range(2):
            hs = slice(half * 512, (half + 1) * 512)
            stt(mg[:, hs], tmpB[half], 0.5, yb[:, hs], ALU.mult, ALU.add, r=[f"tmpB{half}", "yb"], w=["mg"])
        dma("sp", merged_d[i * 128:(i + 1) * 128, :], mg, r=["mg"], w=[f"mgd{i}"])
        yield

    def chain(*gens):
        for g in gens:
            yield from g

    def interleave(*gens):
        gens = list(gens)
        while gens:
            for g in list(gens):
                try:
                    next(g)
                except StopIteration:
                    gens.remove(g)

    def zipg(*gens):
        gens = list(gens)
        while gens:
            for g in list(gens):
                try:
                    next(g)
                    yield
                except StopIteration:
                    gens.remove(g)

    interleave(chain(front(0), front_b(0), zipg(qkconv(0), gates(0)), vblocks(0)))
    if nt > 1:
        interleave(front(1))
    for i in range(nt):
        rot[:] = [1, 2]
        if i + 1 < nt:
            interleave(mlstm_core(i), chain(front_b(i + 1), gates(i + 1)), chain(qkconv(i + 1), vblocks(i + 1)))
            rot[:] = [1, 2, 0]
            if i + 2 < nt:
                interleave(out_thread(i), gmlp(i), front(i + 2))
            else:
                interleave(out_thread(i), gmlp(i))
        else:
            interleave(mlstm_core(i))
            rot[:] = [1, 2, 0]
            interleave(out_thread(i), gmlp(i))
        interleave(merge(i))

    P.add("act", lambda E: E.activation(out=scr["act"][:, 0:1], in_=scr["act"][:, 1:2], func=AF.Copy), w=["end_act", "scr_act"])
    memset("dve", scr["dve"], 0.0, w=["end_dve"])
    memset("pool", scr["pool"], 0.0, w=["end_pool"])
    tr(psb(0, 0, 128), identb, identb, r=["CB"], w=["ps0", "end_pe"])
    ENDK = ["end_act", "end_dve", "end_pool", "end_pe"] + [f"mgd{i}" for i in range(nt)]

    if phase2:
        A2 = Alloc(base1)
        WOUT = A2.bf([128, 8, D])
        FF1 = A2.bf([128, 8, 4 * D])
        FF2 = A2.bf([128, 32, D])
        g1b = A2.f32([128, D])
        g2b = A2.f32([128, D])
        fgb = A2.f32([128, D])
        xt2 = [A2.f32([128, D]) for _ in range(2)]
        mg2 = [A2.bf([128, D]) for _ in range(2)]
        mgT = A2.bf([128, 8, 128])
        xhat2 = A2.bf([128, D])
        xn2Ts = [A2.bf([128, 8, 128]) for _ in range(2)]
        hidT = A2.bf([128, 32, 128])
        tmpRA = [A2.f32([128, 512]) for _ in range(2)]
        tmpRB = [A2.f32([128, 512]) for _ in range(2)]
        tmpH = [A2.f32([128, 512]) for _ in range(2)]
        junkB = tmpH[0].bitcast(BF16)
        outb = A2.f32([128, D])
        sq = A2.f32([128, 16])
        ssq2, mse2, rstd2, ssq3, mse3, rstd3 = (sq[:, k:k + 1] for k in range(6))
        P.add("act", lambda E: E.activation(out=scr["act"][:, 0:1], in_=scr["act"][:, 1:2], func=AF.Copy), r=ENDK, w=["scr_act"])
        memset("dve", scr["dve"], 0.0, r=ENDK, w=["bar_dve"])
        memset("pool", scr["pool"], 0.0, r=ENDK, w=["bar_pool"])
        tr(psb(0, 0, 128), identb, identb, r=ENDK + ["CB"], w=["ps0"])
        w_out_v = w_out.rearrange("(kc p) n -> p kc n", p=128)
        w_ff1_v = w_ff1.rearrange("(kc p) n -> p kc n", p=128)
        w_ff2_v = w_ff2.rearrange("(fc p) n -> p fc n", p=128)

        def load_gb(b):
            dma("sp", g1b, mod_d[b, 2 * D:3 * D].partition_broadcast(128), r=ENDK + ["mod_d"], w=["g1b"])
            dma("sp", g2b, mod_d[b, 5 * D:6 * D].partition_broadcast(128), r=ENDK + ["mod_d"], w=["g2b"])

        load_gb(0)
        dma("sp", fgb, final_g.partition_broadcast(128), r=ENDK, w=["fgb"])
        for kc in range(8):
            dma("pool", WOUT[:, kc, :], w_out_v[:, kc, :], r=ENDK, w=[f"wout{kc}"])
        for q4 in range(4):
            for kh in range(2):
                dma("pool", FF1[:, kh * 4:(kh + 1) * 4, q4 * 1024:(q4 + 1) * 1024],
                    w_ff1_v[:, kh * 4:(kh + 1) * 4, q4 * 1024:(q4 + 1) * 1024], r=ENDK, w=[f"ff1_{q4}"])
        for f4 in range(8):
            dma("pool", FF2[:, f4 * 4:(f4 + 1) * 4, :], w_ff2_v[:, f4 * 4:(f4 + 1) * 4, :], r=ENDK, w=[f"ff2_{f4}"])

        def load_x(i):
            s = i % 2
            dma("sp", xt2[s], x[i * 128:(i + 1) * 128, :], r=ENDK, w=[f"xt2_{s}"])

        def load_mg(i):
            s = i % 2
            dma("sp", mg2[s], merged_d[i * 128:(i + 1) * 128, :], r=ENDK + [f"mgd{i}"], w=[f"mg2_{s}"])

        def A2t(i):
            b = i // NT_SEQ
            s = i % 2
            X = xt2[s]
            xk = f"xt2_{s}"
            xn2T = xn2Ts[s]
            nk = f"xn2T{s}"
            for kc in range(8):
                tr(psb(0, kc * 128, 128), mg2[s][:, kc * 128:(kc + 1) * 128], identb, r=[f"mg2_{s}", "CB"], w=["ps0"])
            act(mgT.rearrange("p a b -> p (a b)"), psb(0), AF.Copy, r=["ps0"], w=["mgT"])
            yield
            for half in range(2):
                hs = slice(half * 512, (half + 1) * 512)
                for kc in range(8):
                    mm(psf(1 + half), mgT[:, kc, :], WOUT[:, kc, hs], kc == 0, kc == 7, r=["mgT", f"wout{kc}"],
                       w=[f"ps{1 + half}"])
                tt("dve", tmpRA[half], psf(1 + half), g1b[:, hs], ALU.mult, r=[f"ps{1 + half}", "g1b"],
                   w=[f"tmpRA{half}"])
                tt("pool", X[:, hs], X[:, hs], tmpRA[half], ALU.add, r=[xk, f"tmpRA{half}"], w=[xk])
                yield
            sumsq(X, xhat2, ssq2, [xk], "xhat2", "ssq2")
            ts("dve", mse2, ssq2, 1.0 / D, EPS, ALU.mult, ALU.add, r=["ssq2"], w=["mse2"])
            tt("pool", rstd2, mse2, mhalf[:, 0:1], ALU.pow, r=["mse2", "mhalf"], w=["rstd2"])
            act(xhat2, X, AF.Identity, scale=rstd2, r=[xk, "rstd2"], w=["xhat2"])
            yield
            for kc in range(8):
                tr(psb(0, kc * 128, 128), xhat2[:, kc * 128:(kc + 1) * 128], identb, r=["xhat2", "CB"], w=["ps0"])
            for kc in range(8):
                act(xn2T[:, kc, :], psb(0, kc * 128, 128), AF.Identity, scale=Gsh[:, 2, b * 8 + kc:b * 8 + kc + 1],
                    bias=Gsh[:, 3, b * 8 + kc:b * 8 + kc + 1], r=["ps0", "Gsh"], w=[nk])
            yield

        def B2t(i):
            b = i // NT_SEQ
            s = i % 2
            X = xt2[s]
            xk = f"xt2_{s}"
            xn2T = xn2Ts[s]
            nk = f"xn2T{s}"
            for f4 in range(8):
                bank = 3 + f4 % 2
                for j in range(4):
                    fb = f4 * 4 + j
                    for kc in range(8):
                        mm(psf(bank, j * 128, 128), FF1[:, kc, fb * 128:(fb + 1) * 128], xn2T[:, kc, :], kc == 0, kc == 7,
                           r=[nk, f"ff1_{f4 // 2}"], w=[f"ps{bank}"])
                act(tmpH[f4 % 2], psf(bank), AF.Relu, r=[f"ps{bank}"], w=[f"tmpH{f4 % 2}"])
                tt("pool" if f4 % 2 else "dve", hidT[:, f4 * 4:(f4 + 1) * 4, :].rearrange("p a b -> p (a b)"),
                   tmpH[f4 % 2], tmpH[f4 % 2], ALU.mult, r=[f"tmpH{f4 % 2}"], w=[f"hid{f4}"])
                yield
            for half in range(2):
                hs = slice(half * 512, (half + 1) * 512)
                for q4 in range(4):
                    for fc in range(q4 * 8, (q4 + 1) * 8):
                        mm(psf(5 + half), hidT[:, fc, :], FF2[:, fc, hs], fc == 0, fc == 31,
                           r=[f"hid{fc // 4}", f"ff2_{fc // 4}"], w=[f"ps{5 + half}"])
                    if q4 < 3:
                        yield
                tt("dve", tmpRB[half], psf(5 + half), g2b[:, hs], ALU.mult, r=[f"ps{5 + half}", "g2b"],
                   w=[f"tmpRB{half}"])
                tt("pool", X[:, hs], X[:, hs], tmpRB[half], ALU.add, r=[xk, f"tmpRB{half}"], w=[xk])
                yield
            sumsq(X, junkB, ssq3, [xk], "tmpH0", "ssq3")
            ts("dve", mse3, ssq3, 1.0 / D, EPS, ALU.mult, ALU.add, r=["ssq3"], w=["mse3"])
            tt("pool", rstd3, mse3, mhalf[:, 0:1], ALU.pow, r=["mse3", "mhalf"], w=["rstd3"])
            stt(outb, X, rstd3, fgb, ALU.mult, ALU.mult, r=[xk, "rstd3", "fgb"], w=["outb"])
            dma("sp", out[i * 128:(i + 1) * 128, :], outb, r=["outb"], w=[f"out{i}"])
            if i + 2 < nt:
                load_x(i + 2)
            if i + 1 < nt and (i + 1) % NT_SEQ == 0:
                load_gb((i + 1) // NT_SEQ)
            yield

        load_x(0)
        load_mg(0)
        if nt > 1:
            load_x(1)
            load_mg(1)
        interleave(A2t(0))
        for i in range(nt):
            if i + 2 < nt:
                load_mg(i + 2)
            if i + 1 < nt and (i + 1) % NT_SEQ != 0:
                interleave(B2t(i), A2t(i + 1))
            elif i + 1 < nt:
                interleave(B2t(i))
                interleave(A2t(i + 1))
            else:
                interleave(B2t(i))

    P.finalize()
    sems = {e: [es.enter_context(nc.semaphore(f"s_{e}{k}")) for k in range(SEM_K)] for e in ("pe", "act", "dve", "pool")}
    dsems = {"sp": [es.enter_context(nc.semaphore(f"d_sp{k}")) for k in range(N_DMA_SP)],
             "pool": [es.enter_context(nc.semaphore(f"d_pl{k}")) for k in range(N_DMA_POOL)]}
    with nc.Block() as block:
        @block.sync
        def _(E):
            P.emit("sp", E, sems, dsems)

        @block.gpsimd
        def _(E):
            P.emit("pool", E, sems, dsems)

        @block.scalar
        def _(E):
            P.emit("act", E, sems, dsems)

        @block.vector
        def _(E):
            P.emit("dve", E, sems, dsems)

        @block.tensor
        def _(E):
            P.emit("pe", E, sems, dsems)
    es.close()
    return nc, P


def prep_inputs(inputs):
    f = lambda a: np.ascontiguousarray(np.asarray(a, dtype=np.float32))
    x = f(inputs["x"])
    c = f(inputs["c"])
    idn = np.eye(128, dtype=np.float32)
    U = np.triu(np.ones((128, 128), np.float32))
    ones = np.ones((128, 128), np.float32)
    cid = np.arange(128) // 64
    mask2 = (cid[:, None] <= cid[None, :]).astype(np.float32)
    cfc = np.ascontiguousarray(np.concatenate([idn, U, ones, mask2], axis=1))
    cb16 = np.ascontiguousarray(np.concatenate([idn, U], axis=1).astype(ml_dtypes.bfloat16))
    conv_w = f(inputs["conv_w"])[0]
    shared = {
        "w_ada": f(inputs["w_ada"])[0], "b_ada": f(inputs["b_ada"])[0],
        "n1g": np.ascontiguousarray(f(inputs["norm1_g"])[0].reshape(8, 128).T),
        "n2g": np.ascontiguousarray(f(inputs["norm2_g"])[0].reshape(8, 128).T),
        "w_in": f(inputs["w_in"])[0],
        "cw": np.ascontiguousarray(conv_w.reshape(4, 16, 128).transpose(2, 1, 0).reshape(128, 64)),
        "cb": np.ascontiguousarray(f(inputs["conv_b"])[0].reshape(16, 128).T),
        "gate_b": np.ascontiguousarray(f(inputs["mlstm_gate_b"])[0].reshape(8)),
        "ln_g": f(inputs["gmlp_ln_g"])[0], "ln_b": f(inputs["gmlp_ln_b"])[0],
        "hn_g": f(inputs["mlstm_hn_g"])[0], "final_g": f(inputs["final_g"]),
        "wsT": np.ascontiguousarray(f(inputs["gmlp_ws"])[0].transpose(2, 0, 1).reshape(128, 1024)),
        "bs_tok": np.ascontiguousarray(f(inputs["gmlp_bs"])[0].T),
        "w_out": f(inputs["w_out"])[0], "w_ff1": f(inputs["w_ff1"])[0], "w_ff2": f(inputs["w_ff2"])[0],
        "cf": cfc, "cb16": cb16,
    }
    maps = []
    for core in range(8):
        xc = np.ascontiguousarray(x[2 * core:2 * core + 2].reshape(NT * 128, D))
        cc = c[2 * core:2 * core + 2]
        cTc = np.ascontiguousarray(cc.reshape(2, 8, 128).transpose(2, 1, 0).reshape(128, 16))
        m = dict(shared)
        m["x"] = xc
        m["cT"] = cTc
        maps.append(m)
    return maps


def kernel(**inputs):
    maps = prep_inputs(inputs)
    nc, _ = build_program()
    res = run_bass_kernel_spmd(nc, maps, core_ids=list(range(8)))
    outs = [np.asarray(r["out"], dtype=np.float32).reshape(2, SEQ, D) for r in res.results]
    return np.concatenate(outs, axis=0)
```

```python
import math
import numpy as np
import ml_dtypes
from contextlib import ExitStack
import concourse.bass as bass
import concourse.mybir as mybir
from concourse.bass_utils import run_bass_kernel_spmd

F32 = mybir.dt.float32
BF16 = mybir.dt.bfloat16
AF = mybir.ActivationFunctionType
ALU = mybir.AluOpType
AX = mybir.AxisListType

D = 1024
SEQ = 2048
NT_SEQ = SEQ // 128
NSEQ = 2
NT = NT_SEQ * NSEQ
OFF_U, OFF_V, OFF_Q, OFF_K, OFF_MV, OFF_O, OFF_I, OFF_F, OFF_GA, OFF_GB, IN_COLS = (
    0, 1024, 2048, 3072, 4096, 5120, 6144, 6148, 6152, 7176, 8200)
EPS = 1e-6
SEM_M = 8000
SEM_K = 6
N_DMA_SP = 12
N_DMA_POOL = 6
STRICT = True


class Op:
    __slots__ = ("eng", "fn", "dma", "gid", "deps", "pos", "ms", "msn", "sem", "val")


class Prog:
    ENG = ("pe", "act", "dve", "pool", "sp")

    def __init__(self):
        self.q = {e: [] for e in self.ENG}
        self.lastw = {}
        self.readers = {}
        self.n = 0

    def add(self, eng, fn, r=(), w=(), dma=False):
        op = Op()
        op.eng, op.fn, op.dma, op.gid = eng, fn, dma, self.n
        self.n += 1
        op.ms = False
        r = list(r)
        w = list(w)
        for k in list(r):
            if k.startswith("ps"):
                r.remove(k)
                if k not in w:
                    w.append(k)
        deps = {}
        for k in r:
            o = self.lastw.get(k)
            if o is not None:
                deps[o.gid] = o
        for k in w:
            o = self.lastw.get(k)
            if o is not None:
                deps[o.gid] = o
            for o in self.readers.get(k, ()):
                deps[o.gid] = o
        for k in w:
            self.lastw[k] = op
            self.readers[k] = []
        for k in r:
            self.readers.setdefault(k, []).append(op)
        deps.pop(op.gid, None)
        op.deps = list(deps.values())
        op.pos = len(self.q[eng])
        self.q[eng].append(op)
        return op

    def _needs_wait(self, op, d):
        if d.dma:
            return True
        if d.eng == op.eng:
            if op.eng == "pe":
                return False
            if op.dma:
                return True
            return STRICT or (op.pos - d.pos <= 2)
        return True

    def finalize(self):
        for e in self.ENG:
            for op in self.q[e]:
                for d in op.deps:
                    if (not d.dma) and self._needs_wait(op, d):
                        d.ms = True
        self.dma_count = {}
        for e in self.ENG:
            c = 0
            j = 0
            npool = N_DMA_SP if e == "sp" else N_DMA_POOL
            for op in self.q[e]:
                if op.dma:
                    op.sem = j % npool
                    op.val = 16 * (j // npool + 1)
                    j += 1
                elif op.ms:
                    c += 1
                    op.msn = c
            self.dma_count[e] = j
            assert c <= SEM_M * SEM_K, (e, c)

    def emit(self, e, E, sems, dsems):
        waited = {}

        def wait(sem, val):
            key = id(sem)
            if waited.get(key, 0) >= val:
                return
            E.wait_ge(sem, val)
            waited[key] = val

        def semfor(eng, n):
            return sems[eng][(n - 1) // SEM_M], (n - 1) % SEM_M + 1

        for op in self.q[e]:
            for d in op.deps:
                if not self._needs_wait(op, d):
                    continue
                if d.dma:
                    wait(dsems[d.eng][d.sem], d.val)
                else:
                    s, v = semfor(d.eng, d.msn)
                    wait(s, v)
            if op.dma:
                if op.val > 16:
                    wait(dsems[e][op.sem], op.val - 16)
                ins = op.fn(E)
                ins.then_inc(dsems[e][op.sem], 16)
            else:
                ins = op.fn(E)
                if op.ms:
                    s, v = semfor(e, op.msn)
                    ins.then_inc(s, 1)
        j = self.dma_count.get(e, 0)
        npool = N_DMA_SP if e == "sp" else N_DMA_POOL
        for si in range(min(j, npool)):
            last = (j - 1 - si) // npool * npool + si
            if last < 0:
                continue
            wait(dsems[e][si], 16 * (last // npool + 1))


def build_program(nt=NT, phase2=True, dbg=False):
    nc = bass.Bass("TRN2", target_bir_lowering=False)

    def din(name, shape, dt=F32):
        return nc.dram_tensor(name, list(shape), dt, kind="ExternalInput").ap()

    x = din("x", [NT * 128, D])
    cT = din("cT", [128, 16])
    w_ada = din("w_ada", [D, 6 * D])
    b_ada = din("b_ada", [6 * D])
    n1g = din("n1g", [128, 8])
    n2g = din("n2g", [128, 8])
    w_in = din("w_in", [D, IN_COLS])
    cw = din("cw", [128, 64])
    cb = din("cb", [128, 16])
    gate_b = din("gate_b", [8])
    ln_g = din("ln_g", [D])
    ln_b = din("ln_b", [D])
    hn_g = din("hn_g", [D])
    final_g = din("final_g", [D])
    wsT = din("wsT", [128, 8 * 128])
    bs_tok = din("bs_tok", [128, 8])
    w_out = din("w_out", [D, D])
    w_ff1 = din("w_ff1", [D, 4 * D])
    w_ff2 = din("w_ff2", [4 * D, D])
    cf = din("cf", [128, 4 * 128])
    cb16 = din("cb16", [128, 2 * 128], BF16)
    out = nc.dram_tensor("out", [NT * 128, D], F32, kind="ExternalOutput").ap()
    if dbg:
        merged_d = nc.dram_tensor("merged_d", [NT * 128, D], BF16, kind="ExternalOutput").ap()
        mod_d = nc.dram_tensor("mod_d", [2, 6 * D], F32, kind="ExternalOutput").ap()
    else:
        merged_d = nc.dram_tensor("merged_d", [NT * 128, D], BF16, kind="Internal").ap()
        mod_d = nc.dram_tensor("mod_d", [2, 6 * D], F32, kind="Internal").ap()

    P = Prog()
    es = ExitStack()
    ARENA_BYTES = 212800
    arena = es.enter_context(nc.sbuf_tensor("arena", [128, ARENA_BYTES // 2], BF16))
    psum = [es.enter_context(nc.psum_tensor(f"psb{i}", [128, 512], F32)) for i in range(8)]

    class Alloc:
        def __init__(self, base=0):
            self.off = base

        def take(self, nbytes):
            o = (self.off + 7) // 8 * 8
            self.off = o + nbytes
            assert self.off <= ARENA_BYTES, self.off
            return o

        def bf(self, shape):
            n = int(np.prod(shape[1:]))
            o = self.take(n * 2)
            return self._shape(arena[:, o // 2:o // 2 + n], shape)

        def f32(self, shape):
            n = int(np.prod(shape[1:]))
            o = self.take(n * 4)
            return self._shape(arena[:, o // 2:o // 2 + 2 * n].bitcast(F32), shape)

        @staticmethod
        def _shape(v, shape):
            if len(shape) == 3:
                return v.rearrange("p (a b) -> p a b", a=shape[1])
            if len(shape) == 4:
                return v.rearrange("p (a b c) -> p a b c", a=shape[1], b=shape[2])
            return v

    def psf(i, lo=0, n=512):
        return psum[i][:, lo:lo + n]

    def psb(i, lo=0, n=1024):
        return psum[i][:].bitcast(BF16)[:, lo:lo + n]

    A0 = Alloc(0)
    CF = A0.f32([128, 4, 128])
    CB = A0.bf([128, 2, 128])
    identf, Umat, onesf, mask2 = CF[:, 0, :], CF[:, 1, :], CF[:, 2, :], CF[:, 3, :]
    identb, maskT = CB[:, 0, :], CB[:, 1, :]
    modT = A0.f32([128, 96])
    n1g_s = A0.f32([128, 8])
    n2g_s = A0.f32([128, 8])
    Gsh = A0.f32([128, 4, 16])
    mhalf = A0.f32([128, 4])
    onesb = A0.bf([128, 4])
    scr = {e: A0.f32([128, 4]) for e in ("act", "dve", "pool")}
    base1 = A0.off

    def mm(outap, lhsT, rhs, start, stop, r=(), w=()):
        return P.add("pe", lambda E: E.matmul(outap, lhsT=lhsT, rhs=rhs, start=start, stop=stop), r=r, w=w)

    def tr(outap, inap, ident, r=(), w=()):
        return P.add("pe", lambda E: E.transpose(outap, inap, ident), r=r, w=w)

    def act(outap, inap, func, r=(), w=(), scale=1.0, bias=None):
        def fn(E):
            if bias is not None:
                return E.activation(out=outap, in_=inap, func=func, scale=scale, bias=bias)
            return E.activation(out=outap, in_=inap, func=func, scale=scale)
        return P.add("act", fn, r=r, w=w)

    def ts(eng, outap, in0, s1, s2, op0, op1=None, r=(), w=()):
        def fn(E):
            if op1 is None:
                return E.tensor_scalar(out=outap, in0=in0, scalar1=s1, scalar2=None, op0=op0)
            return E.tensor_scalar(out=outap, in0=in0, scalar1=s1, scalar2=s2, op0=op0, op1=op1)
        return P.add(eng, fn, r=r, w=w)

    def tt(eng, outap, in0, in1, op, r=(), w=()):
        return P.add(eng, lambda E: E.tensor_tensor(out=outap, in0=in0, in1=in1, op=op), r=r, w=w)

    def stt(outap, in0, scalar, in1, op0, op1, r=(), w=()):
        return P.add("dve", lambda E: E.scalar_tensor_tensor(out=outap, in0=in0, scalar=scalar, in1=in1,
                                                             op0=op0, op1=op1), r=r, w=w)

    def dma(eng, outap, inap, r=(), w=()):
        return P.add(eng, lambda E: E.dma_start(out=outap, in_=inap), r=r, w=w, dma=True)

    def memset(eng, ap, val, r=(), w=()):
        return P.add(eng, lambda E: E.memset(ap, val), r=r, w=w)

    def sumsq(src, junk, dst, rk, jk, dk):
        def fn(E):
            return E.activation(out=junk, in_=src, func=AF.Square, accum_out=dst)
        P.add("act", fn, r=rk, w=[jk, dk + "_raw"])
        P.add("act", lambda E: E.activation(out=scr["act"][:, 0:1], in_=scr["act"][:, 1:2], func=AF.Copy),
              r=[dk + "_raw"], w=[dk, "scr_act"])

    dma("sp", CF, cf.rearrange("p (a b) -> p a b", a=4), w=["CF"])
    dma("sp", CB, cb16.rearrange("p (a b) -> p a b", a=2), w=["CB"])
    dma("sp", n1g_s, n1g, w=["n1g"])
    dma("sp", n2g_s, n2g, w=["n2g"])
    memset("pool", mhalf, -0.5, w=["mhalf"])
    memset("pool", onesb, 1.0, w=["onesb"])
    memset("pool", scr["act"], 0.0, w=["scr_act"])

    A1 = Alloc(base1)
    WIN = A1.bf([128, 8, IN_COLS])
    wsTm = A1.bf([128, 8, 128])
    bst = A1.f32([128, 8])
    cw_s = A1.f32([128, 16, 4])
    cb_s = A1.f32([128, 16])
    gb8 = A1.f32([128, 8])
    lng = A1.f32([128, D])
    lnb = A1.f32([128, D])
    hng = A1.f32([128, D])
    Cst = A1.f32([128, 4, 512])
    nst = A1.f32([128, 4, 2])
    Cdec = A1.bf([128, 4, 512])
    vg = Cdec.rearrange("p a b -> p (a b)").bitcast(F32)
    ndec = A1.bf([128, 4, 2])
    carry = A1.f32([128, 4])
    Rprev = A1.f32([128, 4])
    hal = A1.f32([128, 16, 3])
    pq4 = [A1.f32([128, 4, 131]) for _ in range(2)]
    xt = A1.f32([128, D])
    xhat = A1.bf([128, D])
    xnT2 = [A1.bf([128, 8, 128]) for _ in range(2)]
    Vb = A1.bf([128, D])
    qkT2 = [A1.bf([128, 16, 128]) for _ in range(2)]
    acc4s = [A1.f32([128, 4, 128]) for _ in range(2)]
    th4s = [A1.f32([128, 4, 128]) for _ in range(2)]
    wk = A1.bf([128, D])
    PT = A1.bf([128, 4, 128])
    yb = A1.f32([128, D])
    tmpA = [A1.f32([128, 512]) for _ in range(2)]
    tmpB = [A1.f32([128, 512]) for _ in range(2)]
    mg = A1.bf([128, D])
    vn = mg
    sm = A1.f32([128, 160])
    ssq, mse, rstd = sm[:, 0:1], sm[:, 1:2], sm[:, 2:3]
    g8 = sm[:, 8:16]
    li, fp = g8[:, 0:4], g8[:, 4:8]
    af, ee, dd, uu = sm[:, 16:20], sm[:, 20:24], sm[:, 24:28], sm[:, 28:32]
    s2, tq, mn, lf = sm[:, 32:36], sm[:, 36:40], sm[:, 40:44], sm[:, 44:48]
    Bt, ah, amax, Rnew = sm[:, 48:52], sm[:, 52:56], sm[:, 56:60], sm[:, 60:64]
    eargs, eout = sm[:, 64:76], sm[:, 76:88]
    wv, flv, decv = eout[:, 0:4], eout[:, 4:8], eout[:, 8:12]
    den4, rr = sm[:, 88:92], sm[:, 92:96]
    bst6 = sm[:, 96:120].rearrange("p (a b) -> p a b", a=4)
    mv4 = sm[:, 120:128].rearrange("p (a b) -> p a b", a=4)
    rs4 = sm[:, 128:132]
    vst, vmv, vrs = sm[:, 132:144], sm[:, 144:146], sm[:, 146:147]
    sm2 = A1.f32([128, 16])
    uu2 = sm2[:, 0:4]
    eouts = [sm[:, 76:88], sm2[:, 4:16]]
    nb4 = sm[:, 148:152]
    nmean = sm[:, 152:153]
    end1 = A1.off

    wsT_f = tmpA[0].rearrange("p (a b) -> p a b", a=4)
    wsT_v = wsT.rearrange("p (a b) -> p a b", a=8)
    for hlf in range(2):
        dma("sp", wsT_f, wsT_v[:, hlf * 4:(hlf + 1) * 4, :], w=["tmpA0"])
        tt("dve", wsTm[:, hlf * 4:(hlf + 1) * 4, :], wsT_f, mask2.unsqueeze(1).to_broadcast([128, 4, 128]),
           ALU.mult, r=["tmpA0", "CF"], w=["wsTm"])
    dma("sp", bst, bs_tok, w=["bst"])
    dma("sp", cw_s, cw.rearrange("p (a b) -> p a b", a=16), w=["cw"])
    dma("sp", cb_s, cb, w=["cbs"])
    dma("sp", gb8, gate_b.partition_broadcast(128), w=["gb8"])
    dma("sp", lng, ln_g.partition_broadcast(128), w=["lng"])
    dma("sp", lnb, ln_b.partition_broadcast(128), w=["lnb"])
    dma("sp", hng, hn_g.partition_broadcast(128), w=["hng"])

    csT = tmpA[1][:, 0:16]
    cth = tmpA[1][:, 16:32]
    dma("sp", csT, cT, w=["csT"])
    act(cth, csT, AF.Tanh, scale=0.5, r=["csT"], w=["cth"])
    stt(cth, cth, 1.0, csT, ALU.add, ALU.mult, r=["cth", "csT"], w=["cth"])
    ts("dve", csT, cth, 0.5, None, ALU.mult, r=["cth"], w=["csT"])
    csT3 = csT.rearrange("p (a b) -> p a b", a=8)
    w_ada_v = w_ada.rearrange("(kc p) n -> p kc n", p=128)
    wst = [WIN[:, 6, 0:8192].bitcast(F32).rearrange("p (a b) -> p a b", a=8),
           WIN[:, 7, 0:8192].bitcast(F32).rearrange("p (a b) -> p a b", a=8)]
    wkeys = ["win6", "win7"]
    brow = [tmpB[1][0:2, 0:512], tmpB[1][0:2, 0:512]]
    mrow = [tmpA[0][0:2, 0:512], tmpB[0][0:2, 0:512]]
    for j in range(12):
        sb = j % 2
        dma("sp", wst[sb], w_ada_v[:, :, j * 512:(j + 1) * 512], w=[wkeys[sb]])
        dma("sp", brow[sb], b_ada[j * 512:(j + 1) * 512].partition_broadcast(2), w=["tmpB1"])
        for kc in range(8):
            mm(psf(1 + sb)[0:2, :], csT3[:, kc, :], wst[sb][:, kc, :], kc == 0, kc == 7,
               r=[wkeys[sb], "csT"], w=[f"ps{1 + sb}"])
        mk = "tmpA0" if sb == 0 else "tmpB0"
        tt("dve", mrow[sb], psf(1 + sb)[0:2, :], brow[sb], ALU.add, r=[f"ps{1 + sb}", "tmpB1"], w=[mk])
        dma("sp", mod_d[:, j * 512:(j + 1) * 512], mrow[sb], r=[mk], w=["mod_d"])
    modin = tmpB[1][0:96, 0:128]
    dma("sp", modin, mod_d.rearrange("b (c p) -> (b c) p", p=128), r=["mod_d"], w=["tmpB1"])
    tr(psf(1)[:, 0:96], modin, identf[0:96, 0:96], r=["tmpB1", "CF"], w=["ps1"])
    P.add("dve", lambda E: E.tensor_copy(out=modT, in_=psf(1)[:, 0:96]), r=["ps1"], w=["modT"])
    for b in range(2):
        o = b * 48
        stt(Gsh[:, 0, b * 8:(b + 1) * 8], modT[:, o + 8:o + 16], 1.0, n1g_s, ALU.add, ALU.mult,
            r=["modT", "n1g"], w=["Gsh"])
        P.add("dve", lambda E, b=b, o=o: E.tensor_copy(out=Gsh[:, 1, b * 8:(b + 1) * 8], in_=modT[:, o:o + 8]),
              r=["modT"], w=["Gsh"])
        stt(Gsh[:, 2, b * 8:(b + 1) * 8], modT[:, o + 32:o + 40], 1.0, n2g_s, ALU.add, ALU.mult,
            r=["modT", "n2g"], w=["Gsh"])
        P.add("dve", lambda E, b=b, o=o: E.tensor_copy(out=Gsh[:, 3, b * 8:(b + 1) * 8], in_=modT[:, o + 24:o + 32]),
              r=["modT"], w=["Gsh"])

    w_in_v = w_in.rearrange("(kc p) n -> p kc n", p=128)
    WGRP = [(OFF_Q, OFF_MV), (OFF_MV, OFF_GA), (0, OFF_Q), (OFF_GA, IN_COLS)]

    def wkey(c0):
        for gi, (lo, hi) in enumerate(WGRP):
            if lo <= c0 < hi:
                return f"winG{gi}"
        raise ValueError(c0)

    for gi, (lo, hi) in enumerate(WGRP):
        dma("pool", WIN[:, 0:3, lo:hi], w_in_v[:, 0:3, lo:hi], w=[f"winG{gi}"])
        dma("pool", WIN[:, 3:6, lo:hi], w_in_v[:, 3:6, lo:hi], w=[f"winG{gi}"])
    for gi, (lo, hi) in enumerate(WGRP):
        for kc in (6, 7):
            dma("pool", WIN[:, kc, lo:hi], w_in_v[:, kc, lo:hi], w=[f"winG{gi}", f"win{kc}"])

    tokbank = [0]

    def xn(i):
        return xnT2[i % 2], f"xnT{i % 2}"

    def tokblk(i, c0):
        bank = nextbank()
        X, xk = xn(i)
        for kc in range(8):
            mm(psf(bank), X[:, kc, :], WIN[:, kc, c0:c0 + 512], kc == 0, kc == 7,
               r=[xk, wkey(c0)], w=[f"ps{bank}"])
        return bank

    def eo(i):
        e = eouts[i % 2]
        return e, e[:, 0:4], e[:, 4:8], e[:, 8:12], f"eout{i % 2}"

    rot = [1, 2]

    def nextbank():
        tokbank[0] += 1
        return rot[tokbank[0] % len(rot)]

    def front(i):
        b = i // NT_SEQ
        X, xk = xn(i)
        dma("sp", xt, x[i * 128:(i + 1) * 128, :], w=["xt"])
        sumsq(xt, xhat, ssq, ["xt"], "xhat", "ssq")
        ts("dve", mse, ssq, 1.0 / D, EPS, ALU.mult, ALU.add, r=["ssq"], w=["mse"])
        tt("pool", rstd, mse, mhalf[:, 0:1], ALU.pow, r=["mse", "mhalf"], w=["rstd"])
        act(xhat, xt, AF.Identity, scale=rstd, r=["xt", "rstd"], w=["xhat"])
        yield

    def front_b(i):
        b = i // NT_SEQ
        X, xk = xn(i)
        for kc in range(8):
            tr(psb(0, kc * 128, 128), xhat[:, kc * 128:(kc + 1) * 128], identb, r=["xhat", "CB"], w=["ps0"])
        for kc in range(8):
            act(X[:, kc, :], psb(0, kc * 128, 128), AF.Identity, scale=Gsh[:, 0, b * 8 + kc:b * 8 + kc + 1],
                bias=Gsh[:, 1, b * 8 + kc:b * 8 + kc + 1], r=["ps0", "Gsh"], w=[xk])
        yield

    def qkconv(i):
        X, xk = xn(i)
        qkT = qkT2[i % 2]
        qk = f"qkT{i % 2}"
        if i % NT_SEQ == 0:
            memset("pool", hal, 0.0, w=["hal"])

        def names(g4):
            return (3 + g4 % 2, pq4[g4 % 2], f"pq4_{g4 % 2}", acc4s[g4 % 2], th4s[g4 % 2], f"a{g4 % 2}_",
                    f"th4{'' if g4 % 2 == 0 else 'b'}")

        def stA(g4):
            bank, buf, bk, acc4, th4c, ak, tk = names(g4)
            for j in range(4):
                blk = g4 * 4 + j
                for kc in range(8):
                    mm(psf(bank, j * 128, 128), WIN[:, kc, OFF_Q + blk * 128:OFF_Q + (blk + 1) * 128], X[:, kc, :],
                       kc == 0, kc == 7, r=[xk, "winG0"], w=[f"ps{bank}"])
            P.add("pool", lambda E, buf=buf, g4=g4: E.tensor_copy(out=buf[:, :, 0:3], in_=hal[:, g4 * 4:(g4 + 1) * 4, :]),
                  r=["hal"], w=[bk + "h"])
            act(buf[:, :, 3:131], psf(bank).rearrange("p (a b) -> p a b", a=4), AF.Copy, r=[f"ps{bank}"], w=[bk])
            P.add("pool", lambda E, buf=buf, g4=g4: E.tensor_copy(out=hal[:, g4 * 4:(g4 + 1) * 4, :], in_=buf[:, :, 128:131]),
                  r=[bk], w=["hal"])
            for j in range(4):
                blk = g4 * 4 + j
                act(acc4[:, j, :], buf[:, j, 3:131], AF.Identity, scale=cw_s[:, blk, 3:4], bias=cb_s[:, blk:blk + 1],
                    r=[bk, "cw", "cbs"], w=[f"{ak}{j}"])

        def stB(g4):
            bank, buf, bk, acc4, th4c, ak, tk = names(g4)
            for tap in range(3):
                for j in range(4):
                    blk = g4 * 4 + j
                    stt(acc4[:, j, :], buf[:, j, tap:tap + 128], cw_s[:, blk, tap:tap + 1], acc4[:, j, :],
                        ALU.mult, ALU.add, r=[bk, bk + "h", "cw", f"{ak}{j}"], w=[f"{ak}{j}"])

        def stC(g4):
            bank, buf, bk, acc4, th4c, ak, tk = names(g4)
            act(th4c, acc4, AF.Tanh, scale=0.5, r=[f"{ak}{j}" for j in range(4)], w=[tk])
            stt(qkT[:, g4 * 4:(g4 + 1) * 4, :], th4c, 1.0, acc4, ALU.add, ALU.mult,
                r=[tk] + [f"{ak}{j}" for j in range(4)], w=[qk])

        for st, g in ((stA, 0), (stA, 1), (stB, 0), (stC, 0), (stB, 1), (stA, 2), (stC, 1), (stB, 2), (stA, 3),
                      (stC, 2), (stB, 3), (stC, 3)):
            st(g)
            yield

    def vblocks(i):
        for half in range(2):
            bank = tokblk(i, OFF_MV + half * 512)
            act(Vb[:, half * 512:(half + 1) * 512], psf(bank), AF.Copy, r=[f"ps{bank}"], w=["Vb"])
            yield

    def gates(i):
        X, xk = xn(i)
        eout, wv, flv, decv, ek = eo(i)
        if i % NT_SEQ == 0:
            memset("pool", carry, 0.0, w=["carry"])
            memset("pool", Rprev, 0.0, w=["Rprev"])
        for kc in range(8):
            mm(psf(5, 300, 8), X[:, kc, :], WIN[:, kc, OFF_I:OFF_I + 8], kc == 0, kc == 7,
               r=[xk, "winG1"], w=["ps5"])
        tt("dve", g8, psf(5, 300, 8), gb8, ALU.add, r=["ps5", "gb8"], w=["g8"])
        yield
        stt(af, fp, -1.0, fp, ALU.mult, ALU.max, r=["g8"], w=["af"])
        act(ee, af, AF.Exp, scale=-1.0, r=["af"], w=["ee"])
        ts("dve", mn, fp, 0.0, None, ALU.min, r=["g8"], w=["mn"])
        yield
        ts("dve", dd, ee, 2.0, None, ALU.add, r=["ee"], w=["dd"])
        P.add("dve", lambda E: E.reciprocal(out=uu2, in_=dd), r=["dd"], w=["uu2"])
        tt("dve", uu, ee, uu2, ALU.mult, r=["ee", "uu2"], w=["uu"])
        tt("dve", s2, uu, uu, ALU.mult, r=["uu"], w=["s2"])
        yield
        ts("dve", tq, s2, 1.0 / 13.0, None, ALU.mult, r=["s2"], w=["tq"])
        for cc in (1.0 / 11, 1.0 / 9, 1.0 / 7, 1.0 / 5, 1.0 / 3):
            stt(tq, tq, cc, s2, ALU.add, ALU.mult, r=["tq", "s2"], w=["tq"])
            yield
        stt(tq, tq, 1.0, uu, ALU.add, ALU.mult, r=["tq", "uu"], w=["tq"])
        stt(lf, tq, -2.0, mn, ALU.mult, ALU.add, r=["tq", "mn"], w=["lf"])
        yield
        yield
        mm(psf(5, 308, 4), Umat, lf, True, True, r=["lf", "CF"], w=["ps5"])
        mm(psf(5, 312, 4), onesf, lf, True, True, r=["lf", "CF"], w=["ps5"])
        yield
        tt("dve", Bt, psf(5, 308, 4), carry, ALU.add, r=["ps5", "carry"], w=["Bt"])
        tt("dve", carry, psf(5, 312, 4), carry, ALU.add, r=["ps5", "carry"], w=["carry"])
        tt("dve", ah, li, Bt, ALU.subtract, r=["g8", "Bt"], w=["ah"])
        Dmb = tmpA[0].rearrange("p (a b) -> p a b", a=4)
        tt("dve", Dmb, identf.unsqueeze(1).to_broadcast([128, 4, 128]),
           ah.unsqueeze(2).to_broadcast([128, 4, 128]), ALU.mult, r=["CF", "ah"], w=["tmpA0"])
        yield
        yield
        bank = nextbank()
        mm(psf(bank), onesf, tmpA[0], True, True, r=["tmpA0", "CF"], w=[f"ps{bank}"])
        P.add("dve", lambda E, bank=bank: E.tensor_reduce(out=amax, in_=psf(bank).rearrange("p (a b) -> p a b", a=4),
                                                          axis=AX.X, op=ALU.max), r=[f"ps{bank}"], w=["amax"])
        yield
        tt("dve", Rnew, amax, Rprev, ALU.max, r=["amax", "Rprev"], w=["Rnew"])
        tt("dve", eargs[:, 0:4], ah, Rnew, ALU.subtract, r=["ah", "Rnew"], w=["ea0"])
        stt(eargs[:, 4:8], Bt, -1.0, Rnew, ALU.mult, ALU.subtract, r=["Bt", "Rnew"], w=["ea1"])
        ts("dve", eargs[:, 4:8], eargs[:, 4:8], math.log(64.0), None, ALU.add, r=["ea1"], w=["ea1"])
        tt("dve", eargs[:, 8:12], Rprev, Rnew, ALU.subtract, r=["Rprev", "Rnew"], w=["ea2"])
        act(eout, eargs, AF.Exp, r=["ea0", "ea1", "ea2"], w=[ek])
        P.add("pool", lambda E: E.tensor_copy(out=Rprev, in_=Rnew), r=["Rnew", "ea2"], w=["Rprev"])
        yield

    def gmlp(i):
        for half in range(2):
            bank = tokblk(i, OFF_V + half * 512)
            act(vg[:, half * 512:(half + 1) * 512], psf(bank), AF.Gelu_apprx_tanh, r=[f"ps{bank}"], w=["Cdec"])
            yield
        P.add("dve", lambda E: E.bn_stats(out=vst[:, 0:6], in_=vg[:, 0:512]), r=["Cdec"], w=["vst0"])
        P.add("dve", lambda E: E.bn_stats(out=vst[:, 6:12], in_=vg[:, 512:1024]), r=["Cdec"], w=["vst1"])
        P.add("dve", lambda E: E.bn_aggr(out=vmv, in_=vst), r=["vst0", "vst1"], w=["vmv"])
        ts("dve", vrs, vmv[:, 1:2], EPS, None, ALU.add, r=["vmv"], w=["vrs"])
        tt("pool", vrs, vrs, mhalf[:, 0:1], ALU.pow, r=["vrs", "mhalf"], w=["vrs"])
        yield
        stt(vg, vg, vmv[:, 0:1], lng, ALU.subtract, ALU.mult, r=["Cdec", "vmv", "lng"], w=["Cdec"])
        yield
        stt(vn, vg, vrs, lnb, ALU.mult, ALU.add, r=["Cdec", "vrs", "lnb"], w=["mg"])
        yield
        yield
        for g in range(8):
            mm(psf(3 + g // 4, (g % 4) * 128, 128), wsTm[:, g, :], vn[:, g * 128:(g + 1) * 128], True, True,
               r=["wsTm", "mg"], w=[f"ps{3 + g // 4}"])
        yield
        for half in range(2):
            bank = tokblk(i, OFF_U + half * 512)
            act(tmpA[half], psf(bank), AF.Gelu_apprx_tanh, r=[f"ps{bank}"], w=[f"tmpA{half}"])
            for gg in range(4):
                g = half * 4 + gg
                stt(tmpB[half][:, gg * 128:(gg + 1) * 128], psf(3 + half, gg * 128, 128), bst[:, g:g + 1],
                    tmpA[half][:, gg * 128:(gg + 1) * 128], ALU.add, ALU.mult,
                    r=[f"ps{3 + half}", "bst", f"tmpA{half}"], w=[f"tmpB{half}"])
            yield
            bank = tokblk(i, OFF_GA + half * 512)
            act(tmpA[half], psf(bank), AF.Tanh, scale=0.5, r=[f"ps{bank}"], w=[f"tmpA{half}"])
            stt(tmpB[half], tmpA[half], 1.0, tmpB[half], ALU.add, ALU.mult, r=[f"tmpA{half}", f"tmpB{half}"],
                w=[f"tmpB{half}"])
            yield

    def mlstm_core(i):
        eout, wv, flv, decv, ek = eo(i)
        qkT = qkT2[i % 2]
        qk = f"qkT{i % 2}"
        if i % NT_SEQ == 0:
            memset("pool", Cst, 0.0, w=["C"])
            memset("pool", nst, 0.0, w=["n"])
        for h in range(4):
            ts("pool", Cdec[:, h, :], Cst[:, h, :], decv[:, h:h + 1], 1.0, ALU.mult, ALU.mult, r=["C", ek], w=["Cdec"])
        tt("dve", ndec, nst, decv.unsqueeze(2).to_broadcast([128, 4, 2]), ALU.mult, r=["n", ek], w=["ndec"])
        yield
        for kb in range(8):
            tr(psb(0, kb * 128, 128), qkT[:, 8 + kb, :], identb, r=[qk, "CB"], w=["ps0"])
        for h in range(4):
            act(wk[:, h * 256:(h + 1) * 256], psb(0, h * 256, 256), AF.Identity, scale=wv[:, h:h + 1],
                r=["ps0", ek], w=["wk"])
        yield
        for hp in range(2):
            for hh in range(2):
                h = 2 * hp + hh
                for kc in range(2):
                    mm(psf(5, hh * 128, 128), qkT[:, 8 + 2 * h + kc, :], qkT[:, 2 * h + kc, :], kc == 0, kc == 1,
                       r=[qk], w=["ps5"])
            for hh in range(2):
                h = 2 * hp + hh
                stt(PT[:, h, :], psf(5, hh * 128, 128), wv[:, h:h + 1], maskT, ALU.mult, ALU.mult,
                    r=["ps5", ek, "CB"], w=["PT"])
            yield
            for hh in range(2):
                h = 2 * hp + hh
                nb = 6 + hp
                Vh = Vb[:, h * 256:(h + 1) * 256]
                mm(psf(nb, hh * 256, 256), PT[:, h, :], Vh, True, False, r=["PT", "Vb"], w=[f"ps{nb}"])
                mm(psf(nb, hh * 256, 256), qkT[:, 2 * h, :], Cdec[:, h, 0:256], False, False, r=[qk, "Cdec"], w=[f"ps{nb}"])
                mm(psf(nb, hh * 256, 256), qkT[:, 2 * h + 1, :], Cdec[:, h, 256:512], False, True, r=[qk, "Cdec"], w=[f"ps{nb}"])
                mm(psf(5, 256 + h, 1), PT[:, h, :], onesb[:, 0:1], True, False, r=["PT", "onesb"], w=["ps5"])
                mm(psf(5, 256 + h, 1), qkT[:, 2 * h, :], ndec[:, h, 0:1], False, False, r=[qk, "ndec"], w=["ps5"])
                mm(psf(5, 256 + h, 1), qkT[:, 2 * h + 1, :], ndec[:, h, 1:2], False, True, r=[qk, "ndec"], w=["ps5"])
                bc = nextbank()
                for kc in range(2):
                    mm(psf(bc, kc * 256, 256), wk[:, h * 256 + kc * 128:h * 256 + (kc + 1) * 128], Vh, True, True,
                       r=["wk", "Vb"], w=[f"ps{bc}"])
                for kc in range(2):
                    mm(psf(5, 260 + 2 * h + kc, 1), wk[:, h * 256 + kc * 128:h * 256 + (kc + 1) * 128], onesb[:, 0:1],
                       True, True, r=["wk", "onesb"], w=["ps5"])
                stt(Cst[:, h, :], Cst[:, h, :], decv[:, h:h + 1], psf(bc), ALU.mult, ALU.add,
                    r=["C", ek, f"ps{bc}", "Cdec"], w=["C"])
                stt(nst[:, h, :], nst[:, h, :], decv[:, h:h + 1], psf(5, 260 + 2 * h, 2), ALU.mult, ALU.add,
                    r=["n", ek, "ps5", "ndec"], w=["n"])
                P.add("dve", lambda E, h=h, nb=nb, hh=hh: E.bn_stats(out=bst6[:, h, :], in_=psf(nb, hh * 256, 256)),
                      r=[f"ps{nb}"], w=[f"bst{h}"])
                yield

    def out_thread(i):
        eout, wv, flv, decv, ek = eo(i)
        P.add("dve", lambda E: E.tensor_copy(out=den4, in_=psf(5, 256, 4)), r=["ps5"], w=["den4"])
        stt(den4, den4, -1.0, den4, ALU.mult, ALU.max, r=["den4"], w=["den4"])
        tt("dve", den4, den4, flv, ALU.max, r=["den4", ek], w=["den4"])
        for h in range(4):
            P.add("dve", lambda E, h=h: E.bn_aggr(out=mv4[:, h, :], in_=bst6[:, h, :]), r=[f"bst{h}"], w=[f"mv{h}"])
        yield
        P.add("dve", lambda E: E.reciprocal(out=rr, in_=den4), r=["den4"], w=["rr"])
        tt("dve", rs4, rr, rr, ALU.mult, r=["rr"], w=["rs4"])
        tt("dve", rs4, rs4, mv4[:, :, 1], ALU.mult, r=["rs4"] + [f"mv{h}" for h in range(4)], w=["rs4"])
        ts("dve", rs4, rs4, EPS, None, ALU.add, r=["rs4"], w=["rs4"])
        tt("pool", rs4, rs4, mhalf, ALU.pow, r=["rs4", "mhalf"], w=["rs4"])
        yield
        stt(rs4, rs4, 0.25, rr, ALU.mult, ALU.mult, r=["rs4", "rr"], w=["rs4"])
        stt(nb4, mv4[:, :, 0], -1.0, rs4, ALU.mult, ALU.mult, r=["rs4"] + [f"mv{h}" for h in range(4)], w=["nb4"])
        for h in range(4):
            nb = 6 + h // 2
            act(yb[:, h * 256:(h + 1) * 256], psf(nb, (h % 2) * 256, 256), AF.Identity, scale=rs4[:, h:h + 1],
                bias=nb4[:, h:h + 1], r=[f"ps{nb}", "nb4", "rs4"], w=["yb"])
        tt("dve", yb, yb, hng, ALU.mult, r=["yb", "hng"], w=["yb"])
        rot[:] = [1, 2, 0, 6, 7]
        yield
        for c0 in (OFF_O, OFF_GB):
            for half in range(2):
                bank = tokblk(i, c0 + half * 512)
                act(tmpA[half], psf(bank), AF.Tanh, scale=0.5, r=[f"ps{bank}"], w=[f"tmpA{half}"])
                stt(yb[:, half * 512:(half + 1) * 512], tmpA[half], 1.0, yb[:, half * 512:(half + 1) * 512],
                    ALU.add, ALU.mult, r=[f"tmpA{half}", "yb"], w=["yb"])
                yield

    def merge(i):
        for half in range(2):
            hs = slice(half * 512, (half + 1) * 512)
            stt(mg[:, hs], tmpB[half], 0.5, yb[:, hs], ALU.mult, ALU.add, r=[f"tmpB{half}", "yb"], w=["mg"])
        dma("sp", merged_d[i * 128:(i + 1) * 128, :], mg, r=["mg"], w=[f"mgd{i}"])
        yield

    def chain(*gens):
        for g in gens:
            yield from g

    def interleave(*gens):
        gens = list(gens)
        while gens:
            for g in list(gens):
                try:
                    next(g)
                except StopIteration:
                    gens.remove(g)

    def zipg(*gens):
        gens = list(gens)
        while gens:
            for g in list(gens):
                try:
                    next(g)
                    yield
                except StopIteration:
                    gens.remove(g)

    interleave(chain(front(0), front_b(0), zipg(qkconv(0), gates(0)), vblocks(0)))
    if nt > 1:
        interleave(front(1))
    for i in range(nt):
        rot[:] = [1, 2]
        if i + 1 < nt:
            interleave(mlstm_core(i), chain(front_b(i + 1), gates(i + 1)), qkconv(i + 1))
            rot[:] = [1, 2, 0]
            if i + 2 < nt:
                interleave(out_thread(i), gmlp(i), vblocks(i + 1), front(i + 2))
            else:
                interleave(out_thread(i), gmlp(i), vblocks(i + 1))
        else:
            interleave(mlstm_core(i))
            rot[:] = [1, 2, 0]
            interleave(out_thread(i), gmlp(i))
        interleave(merge(i))

    P.add("act", lambda E: E.activation(out=scr["act"][:, 0:1], in_=scr["act"][:, 1:2], func=AF.Copy), w=["end_act", "scr_act"])
    memset("dve", scr["dve"], 0.0, w=["end_dve"])
    memset("pool", scr["pool"], 0.0, w=["end_pool"])
    tr(psb(0, 0, 128), identb, identb, r=["CB"], w=["ps0", "end_pe"])
    ENDK = ["end_act", "end_dve", "end_pool", "end_pe"] + [f"mgd{i}" for i in range(nt)]

    if phase2:
        A2 = Alloc(base1)
        WOUT = A2.bf([128, 8, D])
        FF1 = A2.bf([128, 8, 4 * D])
        FF2 = A2.bf([128, 32, D])
        g1b = A2.f32([128, D])
        g2b = A2.f32([128, D])
        fgb = A2.f32([128, D])
        xt2 = [A2.f32([128, D]) for _ in range(2)]
        mg2 = [A2.bf([128, D]) for _ in range(2)]
        mgT = A2.bf([128, 8, 128])
        xhat2 = A2.bf([128, D])
        xn2Ts = [A2.bf([128, 8, 128]) for _ in range(2)]
        hidT = A2.bf([128, 32, 128])
        tmpRA = [A2.f32([128, 512]) for _ in range(2)]
        tmpRB = [A2.f32([128, 512]) for _ in range(2)]
        tmpH = [A2.f32([128, 512]) for _ in range(2)]
        junkB = tmpH[0].bitcast(BF16)
        outb = A2.f32([128, D])
        sq = A2.f32([128, 16])
        ssq2, mse2, rstd2, ssq3, mse3, rstd3 = (sq[:, k:k + 1] for k in range(6))
        P.add("act", lambda E: E.activation(out=scr["act"][:, 0:1], in_=scr["act"][:, 1:2], func=AF.Copy), r=ENDK, w=["scr_act"])
        memset("dve", scr["dve"], 0.0, r=ENDK, w=["bar_dve"])
        memset("pool", scr["pool"], 0.0, r=ENDK, w=["bar_pool"])
        tr(psb(0, 0, 128), identb, identb, r=ENDK + ["CB"], w=["ps0"])
        w_out_v = w_out.rearrange("(kc p) n -> p kc n", p=128)
        w_ff1_v = w_ff1.rearrange("(kc p) n -> p kc n", p=128)
        w_ff2_v = w_ff2.rearrange("(fc p) n -> p fc n", p=128)

        def load_gb(b):
            dma("sp", g1b, mod_d[b, 2 * D:3 * D].partition_broadcast(128), r=ENDK + ["mod_d"], w=["g1b"])
            dma("sp", g2b, mod_d[b, 5 * D:6 * D].partition_broadcast(128), r=ENDK + ["mod_d"], w=["g2b"])

        load_gb(0)
        dma("sp", fgb, final_g.partition_broadcast(128), r=ENDK, w=["fgb"])
        for kc in range(8):
            dma("pool", WOUT[:, kc, :], w_out_v[:, kc, :], r=ENDK, w=[f"wout{kc}"])
        for q4 in range(4):
            for kh in range(2):
                dma("pool", FF1[:, kh * 4:(kh + 1) * 4, q4 * 1024:(q4 + 1) * 1024],
                    w_ff1_v[:, kh * 4:(kh + 1) * 4, q4 * 1024:(q4 + 1) * 1024], r=ENDK, w=[f"ff1_{q4}"])
        for f4 in range(8):
            dma("pool", FF2[:, f4 * 4:(f4 + 1) * 4, :], w_ff2_v[:, f4 * 4:(f4 + 1) * 4, :], r=ENDK, w=[f"ff2_{f4}"])

        def load_x(i):
            s = i % 2
            dma("sp", xt2[s], x[i * 128:(i + 1) * 128, :], r=ENDK, w=[f"xt2_{s}"])

        def load_mg(i):
            s = i % 2
            dma("sp", mg2[s], merged_d[i * 128:(i + 1) * 128, :], r=ENDK + [f"mgd{i}"], w=[f"mg2_{s}"])

        def A2t(i):
            b = i // NT_SEQ
            s = i % 2
            X = xt2[s]
            xk = f"xt2_{s}"
            xn2T = xn2Ts[s]
            nk = f"xn2T{s}"
            for kc in range(8):
                tr(psb(0, kc * 128, 128), mg2[s][:, kc * 128:(kc + 1) * 128], identb, r=[f"mg2_{s}", "CB"], w=["ps0"])
            act(mgT.rearrange("p a b -> p (a b)"), psb(0), AF.Copy, r=["ps0"], w=["mgT"])
            yield
            for half in range(2):
                hs = slice(half * 512, (half + 1) * 512)
                for kc in range(8):
                    mm(psf(1 + half), mgT[:, kc, :], WOUT[:, kc, hs], kc == 0, kc == 7, r=["mgT", f"wout{kc}"],
                       w=[f"ps{1 + half}"])
                tt("dve", tmpRA[half], psf(1 + half), g1b[:, hs], ALU.mult, r=[f"ps{1 + half}", "g1b"],
                   w=[f"tmpRA{half}"])
                tt("pool", X[:, hs], X[:, hs], tmpRA[half], ALU.add, r=[xk, f"tmpRA{half}"], w=[xk])
                yield
            sumsq(X, xhat2, ssq2, [xk], "xhat2", "ssq2")
            ts("dve", mse2, ssq2, 1.0 / D, EPS, ALU.mult, ALU.add, r=["ssq2"], w=["mse2"])
            tt("pool", rstd2, mse2, mhalf[:, 0:1], ALU.pow, r=["mse2", "mhalf"], w=["rstd2"])
            act(xhat2, X, AF.Identity, scale=rstd2, r=[xk, "rstd2"], w=["xhat2"])
            yield
            for kc in range(8):
                tr(psb(0, kc * 128, 128), xhat2[:, kc * 128:(kc + 1) * 128], identb, r=["xhat2", "CB"], w=["ps0"])
            for kc in range(8):
                act(xn2T[:, kc, :], psb(0, kc * 128, 128), AF.Identity, scale=Gsh[:, 2, b * 8 + kc:b * 8 + kc + 1],
                    bias=Gsh[:, 3, b * 8 + kc:b * 8 + kc + 1], r=["ps0", "Gsh"], w=[nk])
            yield

        def B2t(i):
            b = i // NT_SEQ
            s = i % 2
            X = xt2[s]
            xk = f"xt2_{s}"
            xn2T = xn2Ts[s]
            nk = f"xn2T{s}"
            for f4 in range(8):
                bank = 3 + f4 % 2
                for j in range(4):
                    fb = f4 * 4 + j
                    for kc in range(8):
                        mm(psf(bank, j * 128, 128), FF1[:, kc, fb * 128:(fb + 1) * 128], xn2T[:, kc, :], kc == 0, kc == 7,
                           r=[nk, f"ff1_{f4 // 2}"], w=[f"ps{bank}"])
                act(tmpH[f4 % 2], psf(bank), AF.Relu, r=[f"ps{bank}"], w=[f"tmpH{f4 % 2}"])
                tt("pool" if f4 % 2 else "dve", hidT[:, f4 * 4:(f4 + 1) * 4, :].rearrange("p a b -> p (a b)"),
                   tmpH[f4 % 2], tmpH[f4 % 2], ALU.mult, r=[f"tmpH{f4 % 2}"], w=[f"hid{f4}"])
                yield
            for half in range(2):
                hs = slice(half * 512, (half + 1) * 512)
                for q4 in range(4):
                    for fc in range(q4 * 8, (q4 + 1) * 8):
                        mm(psf(5 + half), hidT[:, fc, :], FF2[:, fc, hs], fc == 0, fc == 31,
                           r=[f"hid{fc // 4}", f"ff2_{fc // 4}"], w=[f"ps{5 + half}"])
                    if q4 < 3:
                        yield
                tt("dve", tmpRB[half], psf(5 + half), g2b[:, hs], ALU.mult, r=[f"ps{5 + half}", "g2b"],
                   w=[f"tmpRB{half}"])
                tt("pool", X[:, hs], X[:, hs], tmpRB[half], ALU.add, r=[xk, f"tmpRB{half}"], w=[xk])
                yield
            sumsq(X, junkB, ssq3, [xk], "tmpH0", "ssq3")
            ts("dve", mse3, ssq3, 1.0 / D, EPS, ALU.mult, ALU.add, r=["ssq3"], w=["mse3"])
            tt("pool", rstd3, mse3, mhalf[:, 0:1], ALU.pow, r=["mse3", "mhalf"], w=["rstd3"])
            stt(outb, X, rstd3, fgb, ALU.mult, ALU.mult, r=[xk, "rstd3", "fgb"], w=["outb"])
            dma("sp", out[i * 128:(i + 1) * 128, :], outb, r=["outb"], w=[f"out{i}"])
            if i + 2 < nt:
                load_x(i + 2)
            if i + 1 < nt and (i + 1) % NT_SEQ == 0:
                load_gb((i + 1) // NT_SEQ)
            yield

        load_x(0)
        load_mg(0)
        if nt > 1:
            load_x(1)
            load_mg(1)
        interleave(A2t(0))
        for i in range(nt):
            if i + 2 < nt:
                load_mg(i + 2)
            if i + 1 < nt and (i + 1) % NT_SEQ != 0:
                interleave(B2t(i), A2t(i + 1))
            elif i + 1 < nt:
                interleave(B2t(i))
                interleave(A2t(i + 1))
            else:
                interleave(B2t(i))

    P.finalize()
    sems = {e: [es.enter_context(nc.semaphore(f"s_{e}{k}")) for k in range(SEM_K)] for e in ("pe", "act", "dve", "pool")}
    dsems = {"sp": [es.enter_context(nc.semaphore(f"d_sp{k}")) for k in range(N_DMA_SP)],
             "pool": [es.enter_context(nc.semaphore(f"d_pl{k}")) for k in range(N_DMA_POOL)]}
    with nc.Block() as block:
        @block.sync
        def _(E):
            P.emit("sp", E, sems, dsems)

        @block.gpsimd
        def _(E):
            P.emit("pool", E, sems, dsems)

        @block.scalar
        def _(E):
            P.emit("act", E, sems, dsems)

        @block.vector
        def _(E):
            P.emit("dve", E, sems, dsems)

        @block.tensor
        def _(E):
            P.emit("pe", E, sems, dsems)
    es.close()
    return nc, P


def prep_inputs(inputs):
    f = lambda a: np.ascontiguousarray(np.asarray(a, dtype=np.float32))
    x = f(inputs["x"])
    c = f(inputs["c"])
    idn = np.eye(128, dtype=np.float32)
    U = np.triu(np.ones((128, 128), np.float32))
    ones = np.ones((128, 128), np.float32)
    cid = np.arange(128) // 64
    mask2 = (cid[:, None] <= cid[None, :]).astype(np.float32)
    cfc = np.ascontiguousarray(np.concatenate([idn, U, ones, mask2], axis=1))
    cb16 = np.ascontiguousarray(np.concatenate([idn, U], axis=1).astype(ml_dtypes.bfloat16))
    conv_w = f(inputs["conv_w"])[0]
    shared = {
        "w_ada": f(inputs["w_ada"])[0], "b_ada": f(inputs["b_ada"])[0],
        "n1g": np.ascontiguousarray(f(inputs["norm1_g"])[0].reshape(8, 128).T),
        "n2g": np.ascontiguousarray(f(inputs["norm2_g"])[0].reshape(8, 128).T),
        "w_in": f(inputs["w_in"])[0],
        "cw": np.ascontiguousarray(conv_w.reshape(4, 16, 128).transpose(2, 1, 0).reshape(128, 64)),
        "cb": np.ascontiguousarray(f(inputs["conv_b"])[0].reshape(16, 128).T),
        "gate_b": np.ascontiguousarray(f(inputs["mlstm_gate_b"])[0].reshape(8)),
        "ln_g": f(inputs["gmlp_ln_g"])[0], "ln_b": f(inputs["gmlp_ln_b"])[0],
        "hn_g": f(inputs["mlstm_hn_g"])[0], "final_g": f(inputs["final_g"]),
        "wsT": np.ascontiguousarray(f(inputs["gmlp_ws"])[0].transpose(2, 0, 1).reshape(128, 1024)),
        "bs_tok": np.ascontiguousarray(f(inputs["gmlp_bs"])[0].T),
        "w_out": f(inputs["w_out"])[0], "w_ff1": f(inputs["w_ff1"])[0], "w_ff2": f(inputs["w_ff2"])[0],
        "cf": cfc, "cb16": cb16,
    }
    maps = []
    for core in range(8):
        xc = np.ascontiguousarray(x[2 * core:2 * core + 2].reshape(NT * 128, D))
        cc = c[2 * core:2 * core + 2]
        cTc = np.ascontiguousarray(cc.reshape(2, 8, 128).transpose(2, 1, 0).reshape(128, 16))
        m = dict(shared)
        m["x"] = xc
        m["cT"] = cTc
        maps.append(m)
    return maps


def kernel(**inputs):
    maps = prep_inputs(inputs)
    nc, _ = build_program()
    res = run_bass_kernel_spmd(nc, maps, core_ids=list(range(8)))
    outs = [np.asarray(r["out"], dtype=np.float32).reshape(2, SEQ, D) for r in res.results]
    return np.concatenate(outs, axis=0)
```

```python
import math
import numpy as np
import ml_dtypes
from contextlib import ExitStack
import concourse.bass as bass
import concourse.mybir as mybir
from concourse.bass_utils import run_bass_kernel_spmd

F32 = mybir.dt.float32
BF16 = mybir.dt.bfloat16
AF = mybir.ActivationFunctionType
ALU = mybir.AluOpType
AX = mybir.AxisListType

D = 1024
SEQ = 2048
NT_SEQ = SEQ // 128
NSEQ = 2
NT = NT_SEQ * NSEQ
OFF_U, OFF_V, OFF_Q, OFF_K, OFF_MV, OFF_O, OFF_I, OFF_F, OFF_GA, OFF_GB, IN_COLS = (
    0, 1024, 2048, 3072, 4096, 5120, 6144, 6148, 6152, 7176, 8200)
EPS = 1e-6
SEM_M = 8000
SEM_K = 6
N_DMA_SP = 12
N_DMA_POOL = 6
STRICT = True


class Op:
    __slots__ = ("eng", "fn", "dma", "gid", "deps", "pos", "ms", "msn", "sem", "val")


class Prog:
    ENG = ("pe", "act", "dve", "pool", "sp")

    def __init__(self):
        self.q = {e: [] for e in self.ENG}
        self.lastw = {}
        self.readers = {}
        self.n = 0

    def add(self, eng, fn, r=(), w=(), dma=False):
        op = Op()
        op.eng, op.fn, op.dma, op.gid = eng, fn, dma, self.n
        self.n += 1
        op.ms = False
        r = list(r)
        w = list(w)
        for k in list(r):
            if k.startswith("ps"):
                r.remove(k)
                if k not in w:
                    w.append(k)
        deps = {}
        for k in r:
            o = self.lastw.get(k)
            if o is not None:
                deps[o.gid] = o
        for k in w:
            o = self.lastw.get(k)
            if o is not None:
                deps[o.gid] = o
            for o in self.readers.get(k, ()):
                deps[o.gid] = o
        for k in w:
            self.lastw[k] = op
            self.readers[k] = []
        for k in r:
            self.readers.setdefault(k, []).append(op)
        deps.pop(op.gid, None)
        op.deps = list(deps.values())
        op.pos = len(self.q[eng])
        self.q[eng].append(op)
        return op

    def _needs_wait(self, op, d):
        if d.dma:
            return True
        if d.eng == op.eng:
            if op.eng == "pe":
                return False
            if op.dma:
                return True
            return STRICT or (op.pos - d.pos <= 2)
        return True

    def finalize(self):
        for e in self.ENG:
            for op in self.q[e]:
                for d in op.deps:
                    if (not d.dma) and self._needs_wait(op, d):
                        d.ms = True
        self.dma_count = {}
        for e in self.ENG:
            c = 0
            j = 0
            npool = N_DMA_SP if e == "sp" else N_DMA_POOL
            for op in self.q[e]:
                if op.dma:
                    op.sem = j % npool
                    op.val = 16 * (j // npool + 1)
                    j += 1
                elif op.ms:
                    c += 1
                    op.msn = c
            self.dma_count[e] = j
            assert c <= SEM_M * SEM_K, (e, c)

    def emit(self, e, E, sems, dsems):
        waited = {}

        def wait(sem, val):
            key = id(sem)
            if waited.get(key, 0) >= val:
                return
            E.wait_ge(sem, val)
            waited[key] = val

        def semfor(eng, n):
            return sems[eng][(n - 1) // SEM_M], (n - 1) % SEM_M + 1

        for op in self.q[e]:
            for d in op.deps:
                if not self._needs_wait(op, d):
                    continue
                if d.dma:
                    wait(dsems[d.eng][d.sem], d.val)
                else:
                    s, v = semfor(d.eng, d.msn)
                    wait(s, v)
            if op.dma:
                if op.val > 16:
                    wait(dsems[e][op.sem], op.val - 16)
                ins = op.fn(E)
                ins.then_inc(dsems[e][op.sem], 16)
            else:
                ins = op.fn(E)
                if op.ms:
                    s, v = semfor(e, op.msn)
                    ins.then_inc(s, 1)
        j = self.dma_count.get(e, 0)
        npool = N_DMA_SP if e == "sp" else N_DMA_POOL
        for si in range(min(j, npool)):
            last = (j - 1 - si) // npool * npool + si
            if last < 0:
                continue
            wait(dsems[e][si], 16 * (last // npool + 1))


def build_program(nt=NT, phase2=True, dbg=False):
    nc = bass.Bass("TRN2", target_bir_lowering=False)

    def din(name, shape, dt=F32):
        return nc.dram_tensor(name, list(shape), dt, kind="ExternalInput").ap()

    x = din("x", [NT * 128, D])
    cT = din("cT", [128, 16])
    w_ada = din("w_ada", [D, 6 * D])
    b_ada = din("b_ada", [6 * D])
    n1g = din("n1g", [128, 8])
    n2g = din("n2g", [128, 8])
    w_in = din("w_in", [D, IN_COLS])
    cw = din("cw", [128, 64])
    cb = din("cb", [128, 16])
    gate_b = din("gate_b", [8])
    ln_g = din("ln_g", [D])
    ln_b = din("ln_b", [D])
    hn_g = din("hn_g", [D])
    final_g = din("final_g", [D])
    wsT = din("wsT", [128, 8 * 128])
    bs_tok = din("bs_tok", [128, 8])
    w_out = din("w_out", [D, D])
    w_ff1 = din("w_ff1", [D, 4 * D])
    w_ff2 = din("w_ff2", [4 * D, D])
    cf = din("cf", [128, 4 * 128])
    cb16 = din("cb16", [128, 2 * 128], BF16)
    out = nc.dram_tensor("out", [NT * 128, D], F32, kind="ExternalOutput").ap()
    if dbg:
        merged_d = nc.dram_tensor("merged_d", [NT * 128, D], BF16, kind="ExternalOutput").ap()
        mod_d = nc.dram_tensor("mod_d", [2, 6 * D], F32, kind="ExternalOutput").ap()
    else:
        merged_d = nc.dram_tensor("merged_d", [NT * 128, D], BF16, kind="Internal").ap()
        mod_d = nc.dram_tensor("mod_d", [2, 6 * D], F32, kind="Internal").ap()

    P = Prog()
    es = ExitStack()
    ARENA_BYTES = 212800
    arena = es.enter_context(nc.sbuf_tensor("arena", [128, ARENA_BYTES // 2], BF16))
    psum = [es.enter_context(nc.psum_tensor(f"psb{i}", [128, 512], F32)) for i in range(8)]

    class Alloc:
        def __init__(self, base=0):
            self.off = base

        def take(self, nbytes):
            o = (self.off + 7) // 8 * 8
            self.off = o + nbytes
            assert self.off <= ARENA_BYTES, self.off
            return o

        def bf(self, shape):
            n = int(np.prod(shape[1:]))
            o = self.take(n * 2)
            return self._shape(arena[:, o // 2:o // 2 + n], shape)

        def f32(self, shape):
            n = int(np.prod(shape[1:]))
            o = self.take(n * 4)
            return self._shape(arena[:, o // 2:o // 2 + 2 * n].bitcast(F32), shape)

        @staticmethod
        def _shape(v, shape):
            if len(shape) == 3:
                return v.rearrange("p (a b) -> p a b", a=shape[1])
            if len(shape) == 4:
                return v.rearrange("p (a b c) -> p a b c", a=shape[1], b=shape[2])
            return v

    def psf(i, lo=0, n=512):
        return psum[i][:, lo:lo + n]

    def psb(i, lo=0, n=1024):
        return psum[i][:].bitcast(BF16)[:, lo:lo + n]

    A0 = Alloc(0)
    CF = A0.f32([128, 4, 128])
    CB = A0.bf([128, 2, 128])
    identf, Umat, onesf, mask2 = CF[:, 0, :], CF[:, 1, :], CF[:, 2, :], CF[:, 3, :]
    identb, maskT = CB[:, 0, :], CB[:, 1, :]
    modT = A0.f32([128, 96])
    n1g_s = A0.f32([128, 8])
    n2g_s = A0.f32([128, 8])
    Gsh = A0.f32([128, 4, 16])
    mhalf = A0.f32([128, 4])
    onesb = A0.bf([128, 4])
    scr = {e: A0.f32([128, 4]) for e in ("act", "dve", "pool")}
    base1 = A0.off

    def mm(outap, lhsT, rhs, start, stop, r=(), w=()):
        return P.add("pe", lambda E: E.matmul(outap, lhsT=lhsT, rhs=rhs, start=start, stop=stop), r=r, w=w)

    def tr(outap, inap, ident, r=(), w=()):
        return P.add("pe", lambda E: E.transpose(outap, inap, ident), r=r, w=w)

    def act(outap, inap, func, r=(), w=(), scale=1.0, bias=None):
        def fn(E):
            if bias is not None:
                return E.activation(out=outap, in_=inap, func=func, scale=scale, bias=bias)
            return E.activation(out=outap, in_=inap, func=func, scale=scale)
        return P.add("act", fn, r=r, w=w)

    def ts(eng, outap, in0, s1, s2, op0, op1=None, r=(), w=()):
        def fn(E):
            if op1 is None:
                return E.tensor_scalar(out=outap, in0=in0, scalar1=s1, scalar2=None, op0=op0)
            return E.tensor_scalar(out=outap, in0=in0, scalar1=s1, scalar2=s2, op0=op0, op1=op1)
        return P.add(eng, fn, r=r, w=w)

    def tt(eng, outap, in0, in1, op, r=(), w=()):
        return P.add(eng, lambda E: E.tensor_tensor(out=outap, in0=in0, in1=in1, op=op), r=r, w=w)

    def stt(outap, in0, scalar, in1, op0, op1, r=(), w=()):
        return P.add("dve", lambda E: E.scalar_tensor_tensor(out=outap, in0=in0, scalar=scalar, in1=in1,
                                                             op0=op0, op1=op1), r=r, w=w)

    def dma(eng, outap, inap, r=(), w=()):
        return P.add(eng, lambda E: E.dma_start(out=outap, in_=inap), r=r, w=w, dma=True)

    def memset(eng, ap, val, r=(), w=()):
        return P.add(eng, lambda E: E.memset(ap, val), r=r, w=w)

    def sumsq(src, junk, dst, rk, jk, dk):
        def fn(E):
            return E.activation(out=junk, in_=src, func=AF.Square, accum_out=dst)
        P.add("act", fn, r=rk, w=[jk, dk + "_raw"])
        P.add("act", lambda E: E.activation(out=scr["act"][:, 0:1], in_=scr["act"][:, 1:2], func=AF.Copy),
              r=[dk + "_raw"], w=[dk, "scr_act"])

    dma("sp", CF, cf.rearrange("p (a b) -> p a b", a=4), w=["CF"])
    dma("sp", CB, cb16.rearrange("p (a b) -> p a b", a=2), w=["CB"])
    dma("sp", n1g_s, n1g, w=["n1g"])
    dma("sp", n2g_s, n2g, w=["n2g"])
    memset("pool", mhalf, -0.5, w=["mhalf"])
    memset("pool", onesb, 1.0, w=["onesb"])
    memset("pool", scr["act"], 0.0, w=["scr_act"])

    A1 = Alloc(base1)
    WIN = A1.bf([128, 8, IN_COLS])
    wsTm = A1.bf([128, 8, 128])
    bst = A1.f32([128, 8])
    cw_s = A1.f32([128, 16, 4])
    cb_s = A1.f32([128, 16])
    gb8 = A1.f32([128, 8])
    lng = A1.f32([128, D])
    lnb = A1.f32([128, D])
    hng = A1.f32([128, D])
    Cst = A1.f32([128, 4, 512])
    nst = A1.f32([128, 4, 2])
    Cdec = A1.bf([128, 4, 512])
    vg = Cdec.rearrange("p a b -> p (a b)").bitcast(F32)
    ndec = A1.bf([128, 4, 2])
    carry = A1.f32([128, 4])
    Rprev = A1.f32([128, 4])
    hal = A1.f32([128, 16, 3])
    pq4 = [A1.f32([128, 4, 131]) for _ in range(2)]
    xt = A1.f32([128, D])
    xhat = A1.bf([128, D])
    xnT2 = [A1.bf([128, 8, 128]) for _ in range(2)]
    Vb = A1.bf([128, D])
    qkT2 = [A1.bf([128, 16, 128]) for _ in range(2)]
    acc4s = [A1.f32([128, 4, 128]) for _ in range(2)]
    th4s = [A1.f32([128, 4, 128]) for _ in range(2)]
    wk = A1.bf([128, D])
    PT = A1.bf([128, 4, 128])
    yb = A1.f32([128, D])
    tmpA = [A1.f32([128, 512]) for _ in range(2)]
    tmpB = [A1.f32([128, 512]) for _ in range(2)]
    mg = A1.bf([128, D])
    vn = mg
    sm = A1.f32([128, 160])
    ssq, mse, rstd = sm[:, 0:1], sm[:, 1:2], sm[:, 2:3]
    g8 = sm[:, 8:16]
    li, fp = g8[:, 0:4], g8[:, 4:8]
    af, ee, dd, uu = sm[:, 16:20], sm[:, 20:24], sm[:, 24:28], sm[:, 28:32]
    s2, tq, mn, lf = sm[:, 32:36], sm[:, 36:40], sm[:, 40:44], sm[:, 44:48]
    Bt, ah, amax, Rnew = sm[:, 48:52], sm[:, 52:56], sm[:, 56:60], sm[:, 60:64]
    eargs, eout = sm[:, 64:76], sm[:, 76:88]
    wv, flv, decv = eout[:, 0:4], eout[:, 4:8], eout[:, 8:12]
    den4, rr = sm[:, 88:92], sm[:, 92:96]
    bst6 = sm[:, 96:120].rearrange("p (a b) -> p a b", a=4)
    mv4 = sm[:, 120:128].rearrange("p (a b) -> p a b", a=4)
    rs4 = sm[:, 128:132]
    vst, vmv, vrs = sm[:, 132:144], sm[:, 144:146], sm[:, 146:147]
    sm2 = A1.f32([128, 16])
    uu2 = sm2[:, 0:4]
    eouts = [sm[:, 76:88], sm2[:, 4:16]]
    nb4 = sm[:, 148:152]
    nmean = sm[:, 152:153]
    end1 = A1.off

    wsT_f = tmpA[0].rearrange("p (a b) -> p a b", a=4)
    wsT_v = wsT.rearrange("p (a b) -> p a b", a=8)
    for hlf in range(2):
        dma("sp", wsT_f, wsT_v[:, hlf * 4:(hlf + 1) * 4, :], w=["tmpA0"])
        tt("dve", wsTm[:, hlf * 4:(hlf + 1) * 4, :], wsT_f, mask2.unsqueeze(1).to_broadcast([128, 4, 128]),
           ALU.mult, r=["tmpA0", "CF"], w=["wsTm"])
    dma("sp", bst, bs_tok, w=["bst"])
    dma("sp", cw_s, cw.rearrange("p (a b) -> p a b", a=16), w=["cw"])
    dma("sp", cb_s, cb, w=["cbs"])
    dma("sp", gb8, gate_b.partition_broadcast(128), w=["gb8"])
    dma("sp", lng, ln_g.partition_broadcast(128), w=["lng"])
    dma("sp", lnb, ln_b.partition_broadcast(128), w=["lnb"])
    dma("sp", hng, hn_g.partition_broadcast(128), w=["hng"])

    csT = tmpA[1][:, 0:16]
    cth = tmpA[1][:, 16:32]
    dma("sp", csT, cT, w=["csT"])
    act(cth, csT, AF.Tanh, scale=0.5, r=["csT"], w=["cth"])
    stt(cth, cth, 1.0, csT, ALU.add, ALU.mult, r=["cth", "csT"], w=["cth"])
    ts("dve", csT, cth, 0.5, None, ALU.mult, r=["cth"], w=["csT"])
    csT3 = csT.rearrange("p (a b) -> p a b", a=8)
    w_ada_v = w_ada.rearrange("(kc p) n -> p kc n", p=128)
    wst = [WIN[:, 6, 0:8192].bitcast(F32).rearrange("p (a b) -> p a b", a=8),
           WIN[:, 7, 0:8192].bitcast(F32).rearrange("p (a b) -> p a b", a=8)]
    wkeys = ["win6", "win7"]
    brow = [tmpB[1][0:2, 0:512], tmpB[1][0:2, 0:512]]
    mrow = [tmpA[0][0:2, 0:512], tmpB[0][0:2, 0:512]]
    for j in range(12):
        sb = j % 2
        dma("sp", wst[sb], w_ada_v[:, :, j * 512:(j + 1) * 512], w=[wkeys[sb]])
        dma("sp", brow[sb], b_ada[j * 512:(j + 1) * 512].partition_broadcast(2), w=["tmpB1"])
        for kc in range(8):
            mm(psf(1 + sb)[0:2, :], csT3[:, kc, :], wst[sb][:, kc, :], kc == 0, kc == 7,
               r=[wkeys[sb], "csT"], w=[f"ps{1 + sb}"])
        mk = "tmpA0" if sb == 0 else "tmpB0"
        tt("dve", mrow[sb], psf(1 + sb)[0:2, :], brow[sb], ALU.add, r=[f"ps{1 + sb}", "tmpB1"], w=[mk])
        dma("sp", mod_d[:, j * 512:(j + 1) * 512], mrow[sb], r=[mk], w=["mod_d"])
    modin = tmpB[1][0:96, 0:128]
    dma("sp", modin, mod_d.rearrange("b (c p) -> (b c) p", p=128), r=["mod_d"], w=["tmpB1"])
    tr(psf(1)[:, 0:96], modin, identf[0:96, 0:96], r=["tmpB1", "CF"], w=["ps1"])
    P.add("dve", lambda E: E.tensor_copy(out=modT, in_=psf(1)[:, 0:96]), r=["ps1"], w=["modT"])
    for b in range(2):
        o = b * 48
        stt(Gsh[:, 0, b * 8:(b + 1) * 8], modT[:, o + 8:o + 16], 1.0, n1g_s, ALU.add, ALU.mult,
            r=["modT", "n1g"], w=["Gsh"])
        P.add("dve", lambda E, b=b, o=o: E.tensor_copy(out=Gsh[:, 1, b * 8:(b + 1) * 8], in_=modT[:, o:o + 8]),
              r=["modT"], w=["Gsh"])
        stt(Gsh[:, 2, b * 8:(b + 1) * 8], modT[:, o + 32:o + 40], 1.0, n2g_s, ALU.add, ALU.mult,
            r=["modT", "n2g"], w=["Gsh"])
        P.add("dve", lambda E, b=b, o=o: E.tensor_copy(out=Gsh[:, 3, b * 8:(b + 1) * 8], in_=modT[:, o + 24:o + 32]),
              r=["modT"], w=["Gsh"])

    w_in_v = w_in.rearrange("(kc p) n -> p kc n", p=128)
    WGRP = [(OFF_Q, OFF_MV), (OFF_MV, OFF_GA), (0, OFF_Q), (OFF_GA, IN_COLS)]

    def wkey(c0):
        for gi, (lo, hi) in enumerate(WGRP):
            if lo <= c0 < hi:
                return f"winG{gi}"
        raise ValueError(c0)

    for gi, (lo, hi) in enumerate(WGRP):
        dma("pool", WIN[:, 0:3, lo:hi], w_in_v[:, 0:3, lo:hi], w=[f"winG{gi}"])
        dma("pool", WIN[:, 3:6, lo:hi], w_in_v[:, 3:6, lo:hi], w=[f"winG{gi}"])
    for gi, (lo, hi) in enumerate(WGRP):
        for kc in (6, 7):
            dma("pool", WIN[:, kc, lo:hi], w_in_v[:, kc, lo:hi], w=[f"winG{gi}", f"win{kc}"])

    tokbank = [0]

    def xn(i):
        return xnT2[i % 2], f"xnT{i % 2}"

    def tokblk(i, c0):
        bank = nextbank()
        X, xk = xn(i)
        for kc in range(8):
            mm(psf(bank), X[:, kc, :], WIN[:, kc, c0:c0 + 512], kc == 0, kc == 7,
               r=[xk, wkey(c0)], w=[f"ps{bank}"])
        return bank

    def eo(i):
        e = eouts[i % 2]
        return e, e[:, 0:4], e[:, 4:8], e[:, 8:12], f"eout{i % 2}"

    rot = [1, 2]

    def nextbank():
        tokbank[0] += 1
        return rot[tokbank[0] % len(rot)]

    def front(i):
        b = i // NT_SEQ
        X, xk = xn(i)
        dma("sp", xt, x[i * 128:(i + 1) * 128, :], w=["xt"])
        sumsq(xt, xhat, ssq, ["xt"], "xhat", "ssq")
        ts("dve", mse, ssq, 1.0 / D, EPS, ALU.mult, ALU.add, r=["ssq"], w=["mse"])
        tt("pool", rstd, mse, mhalf[:, 0:1], ALU.pow, r=["mse", "mhalf"], w=["rstd"])
        ts("pool", xhat, xt, rstd, 1.0, ALU.mult, ALU.mult, r=["xt", "rstd"], w=["xhat"])
        yield

    def front_b(i):
        b = i // NT_SEQ
        X, xk = xn(i)
        for kc in range(8):
            tr(psb(0, kc * 128, 128), xhat[:, kc * 128:(kc + 1) * 128], identb, r=["xhat", "CB"], w=["ps0"])
        for kc in range(8):
            act(X[:, kc, :], psb(0, kc * 128, 128), AF.Identity, scale=Gsh[:, 0, b * 8 + kc:b * 8 + kc + 1],
                bias=Gsh[:, 1, b * 8 + kc:b * 8 + kc + 1], r=["ps0", "Gsh"], w=[xk])
        yield

    def qkconv(i):
        X, xk = xn(i)
        qkT = qkT2[i % 2]
        qk = f"qkT{i % 2}"
        if i % NT_SEQ == 0:
            memset("pool", hal, 0.0, w=["hal"])

        def names(g4):
            return (3 + g4 % 2, pq4[g4 % 2], f"pq4_{g4 % 2}", acc4s[g4 % 2], th4s[g4 % 2], f"a{g4 % 2}_",
                    f"th4{'' if g4 % 2 == 0 else 'b'}")

        def stA(g4):
            bank, buf, bk, acc4, th4c, ak, tk = names(g4)
            for j in range(4):
                blk = g4 * 4 + j
                for kc in range(8):
                    mm(psf(bank, j * 128, 128), WIN[:, kc, OFF_Q + blk * 128:OFF_Q + (blk + 1) * 128], X[:, kc, :],
                       kc == 0, kc == 7, r=[xk, "winG0"], w=[f"ps{bank}"])
            P.add("pool", lambda E, buf=buf, g4=g4: E.tensor_copy(out=buf[:, :, 0:3], in_=hal[:, g4 * 4:(g4 + 1) * 4, :]),
                  r=["hal"], w=[bk + "h"])
            act(buf[:, :, 3:131], psf(bank).rearrange("p (a b) -> p a b", a=4), AF.Copy, r=[f"ps{bank}"], w=[bk])
            P.add("pool", lambda E, buf=buf, g4=g4: E.tensor_copy(out=hal[:, g4 * 4:(g4 + 1) * 4, :], in_=buf[:, :, 128:131]),
                  r=[bk], w=["hal"])
            for j in range(4):
                blk = g4 * 4 + j
                act(acc4[:, j, :], buf[:, j, 3:131], AF.Identity, scale=cw_s[:, blk, 3:4], bias=cb_s[:, blk:blk + 1],
                    r=[bk, "cw", "cbs"], w=[f"{ak}{j}"])

        def stB(g4):
            bank, buf, bk, acc4, th4c, ak, tk = names(g4)
            for tap in range(3):
                for j in range(4):
                    blk = g4 * 4 + j
                    stt(acc4[:, j, :], buf[:, j, tap:tap + 128], cw_s[:, blk, tap:tap + 1], acc4[:, j, :],
                        ALU.mult, ALU.add, r=[bk, bk + "h", "cw", f"{ak}{j}"], w=[f"{ak}{j}"])

        def stC(g4):
            bank, buf, bk, acc4, th4c, ak, tk = names(g4)
            act(th4c, acc4, AF.Tanh, scale=0.5, r=[f"{ak}{j}" for j in range(4)], w=[tk])
            stt(qkT[:, g4 * 4:(g4 + 1) * 4, :], th4c, 1.0, acc4, ALU.add, ALU.mult,
                r=[tk] + [f"{ak}{j}" for j in range(4)], w=[qk])

        for st, g in ((stA, 0), (stA, 1), (stB, 0), (stC, 0), (stB, 1), (stA, 2), (stC, 1), (stB, 2), (stA, 3),
                      (stC, 2), (stB, 3), (stC, 3)):
            st(g)
            yield

    def vblocks(i):
        for half in range(2):
            bank = tokblk(i, OFF_MV + half * 512)
            act(Vb[:, half * 512:(half + 1) * 512], psf(bank), AF.Copy, r=[f"ps{bank}"], w=["Vb"])
            yield

    def gates(i):
        X, xk = xn(i)
        eout, wv, flv, decv, ek = eo(i)
        if i % NT_SEQ == 0:
            memset("pool", carry, 0.0, w=["carry"])
            memset("pool", Rprev, 0.0, w=["Rprev"])
        for kc in range(8):
            mm(psf(5, 300, 8), X[:, kc, :], WIN[:, kc, OFF_I:OFF_I + 8], kc == 0, kc == 7,
               r=[xk, "winG1"], w=["ps5"])
        tt("dve", g8, psf(5, 300, 8), gb8, ALU.add, r=["ps5", "gb8"], w=["g8"])
        yield
        stt(af, fp, -1.0, fp, ALU.mult, ALU.max, r=["g8"], w=["af"])
        act(ee, af, AF.Exp, scale=-1.0, r=["af"], w=["ee"])
        ts("dve", mn, fp, 0.0, None, ALU.min, r=["g8"], w=["mn"])
        yield
        ts("dve", dd, ee, 2.0, None, ALU.add, r=["ee"], w=["dd"])
        P.add("dve", lambda E: E.reciprocal(out=uu2, in_=dd), r=["dd"], w=["uu2"])
        tt("dve", uu, ee, uu2, ALU.mult, r=["ee", "uu2"], w=["uu"])
        tt("dve", s2, uu, uu, ALU.mult, r=["uu"], w=["s2"])
        yield
        ts("dve", tq, s2, 1.0 / 13.0, None, ALU.mult, r=["s2"], w=["tq"])
        for cc in (1.0 / 11, 1.0 / 9, 1.0 / 7, 1.0 / 5, 1.0 / 3):
            stt(tq, tq, cc, s2, ALU.add, ALU.mult, r=["tq", "s2"], w=["tq"])
            yield
        stt(tq, tq, 1.0, uu, ALU.add, ALU.mult, r=["tq", "uu"], w=["tq"])
        stt(lf, tq, -2.0, mn, ALU.mult, ALU.add, r=["tq", "mn"], w=["lf"])
        yield
        yield
        mm(psf(5, 308, 4), Umat, lf, True, True, r=["lf", "CF"], w=["ps5"])
        mm(psf(5, 312, 4), onesf, lf, True, True, r=["lf", "CF"], w=["ps5"])
        yield
        tt("dve", Bt, psf(5, 308, 4), carry, ALU.add, r=["ps5", "carry"], w=["Bt"])
        tt("dve", carry, psf(5, 312, 4), carry, ALU.add, r=["ps5", "carry"], w=["carry"])
        tt("dve", ah, li, Bt, ALU.subtract, r=["g8", "Bt"], w=["ah"])
        Dmb = tmpA[0].rearrange("p (a b) -> p a b", a=4)
        tt("dve", Dmb, identf.unsqueeze(1).to_broadcast([128, 4, 128]),
           ah.unsqueeze(2).to_broadcast([128, 4, 128]), ALU.mult, r=["CF", "ah"], w=["tmpA0"])
        yield
        yield
        bank = nextbank()
        mm(psf(bank), onesf, tmpA[0], True, True, r=["tmpA0", "CF"], w=[f"ps{bank}"])
        P.add("dve", lambda E, bank=bank: E.tensor_reduce(out=amax, in_=psf(bank).rearrange("p (a b) -> p a b", a=4),
                                                          axis=AX.X, op=ALU.max), r=[f"ps{bank}"], w=["amax"])
        yield
        tt("dve", Rnew, amax, Rprev, ALU.max, r=["amax", "Rprev"], w=["Rnew"])
        tt("dve", eargs[:, 0:4], ah, Rnew, ALU.subtract, r=["ah", "Rnew"], w=["ea0"])
        stt(eargs[:, 4:8], Bt, -1.0, Rnew, ALU.mult, ALU.subtract, r=["Bt", "Rnew"], w=["ea1"])
        ts("dve", eargs[:, 4:8], eargs[:, 4:8], math.log(64.0), None, ALU.add, r=["ea1"], w=["ea1"])
        tt("dve", eargs[:, 8:12], Rprev, Rnew, ALU.subtract, r=["Rprev", "Rnew"], w=["ea2"])
        act(eout, eargs, AF.Exp, r=["ea0", "ea1", "ea2"], w=[ek])
        P.add("pool", lambda E: E.tensor_copy(out=Rprev, in_=Rnew), r=["Rnew", "ea2"], w=["Rprev"])
        yield

    def gmlp(i):
        for half in range(2):
            bank = tokblk(i, OFF_V + half * 512)
            act(vg[:, half * 512:(half + 1) * 512], psf(bank), AF.Gelu_apprx_tanh, r=[f"ps{bank}"], w=["Cdec"])
            yield
        P.add("dve", lambda E: E.bn_stats(out=vst[:, 0:6], in_=vg[:, 0:512]), r=["Cdec"], w=["vst0"])
        P.add("dve", lambda E: E.bn_stats(out=vst[:, 6:12], in_=vg[:, 512:1024]), r=["Cdec"], w=["vst1"])
        P.add("dve", lambda E: E.bn_aggr(out=vmv, in_=vst), r=["vst0", "vst1"], w=["vmv"])
        ts("dve", vrs, vmv[:, 1:2], EPS, None, ALU.add, r=["vmv"], w=["vrs"])
        tt("pool", vrs, vrs, mhalf[:, 0:1], ALU.pow, r=["vrs", "mhalf"], w=["vrs"])
        yield
        stt(vg, vg, vmv[:, 0:1], lng, ALU.subtract, ALU.mult, r=["Cdec", "vmv", "lng"], w=["Cdec"])
        yield
        stt(vn, vg, vrs, lnb, ALU.mult, ALU.add, r=["Cdec", "vrs", "lnb"], w=["mg"])
        yield
        yield
        for g in range(8):
            mm(psf(3 + g // 4, (g % 4) * 128, 128), wsTm[:, g, :], vn[:, g * 128:(g + 1) * 128], True, True,
               r=["wsTm", "mg"], w=[f"ps{3 + g // 4}"])
        yield
        for half in range(2):
            bank = tokblk(i, OFF_U + half * 512)
            act(tmpA[half], psf(bank), AF.Gelu_apprx_tanh, r=[f"ps{bank}"], w=[f"tmpA{half}"])
            for gg in range(4):
                g = half * 4 + gg
                stt(tmpB[half][:, gg * 128:(gg + 1) * 128], psf(3 + half, gg * 128, 128), bst[:, g:g + 1],
                    tmpA[half][:, gg * 128:(gg + 1) * 128], ALU.add, ALU.mult,
                    r=[f"ps{3 + half}", "bst", f"tmpA{half}"], w=[f"tmpB{half}"])
            yield
            bank = tokblk(i, OFF_GA + half * 512)
            act(tmpA[half], psf(bank), AF.Tanh, scale=0.5, r=[f"ps{bank}"], w=[f"tmpA{half}"])
            stt(tmpB[half], tmpA[half], 1.0, tmpB[half], ALU.add, ALU.mult, r=[f"tmpA{half}", f"tmpB{half}"],
                w=[f"tmpB{half}"])
            yield

    def mlstm_core(i):
        eout, wv, flv, decv, ek = eo(i)
        qkT = qkT2[i % 2]
        qk = f"qkT{i % 2}"
        if i % NT_SEQ == 0:
            memset("pool", Cst, 0.0, w=["C"])
            memset("pool", nst, 0.0, w=["n"])
        for h in range(4):
            ts("pool", Cdec[:, h, :], Cst[:, h, :], decv[:, h:h + 1], 1.0, ALU.mult, ALU.mult, r=["C", ek], w=["Cdec"])
        tt("dve", ndec, nst, decv.unsqueeze(2).to_broadcast([128, 4, 2]), ALU.mult, r=["n", ek], w=["ndec"])
        yield
        for kb in range(8):
            tr(psb(0, kb * 128, 128), qkT[:, 8 + kb, :], identb, r=[qk, "CB"], w=["ps0"])
        for h in range(4):
            act(wk[:, h * 256:(h + 1) * 256], psb(0, h * 256, 256), AF.Identity, scale=wv[:, h:h + 1],
                r=["ps0", ek], w=["wk"])
        yield
        for hp in range(2):
            for hh in range(2):
                h = 2 * hp + hh
                for kc in range(2):
                    mm(psf(5, hh * 128, 128), qkT[:, 8 + 2 * h + kc, :], qkT[:, 2 * h + kc, :], kc == 0, kc == 1,
                       r=[qk], w=["ps5"])
            for hh in range(2):
                h = 2 * hp + hh
                stt(PT[:, h, :], psf(5, hh * 128, 128), wv[:, h:h + 1], maskT, ALU.mult, ALU.mult,
                    r=["ps5", ek, "CB"], w=["PT"])
            yield
            for hh in range(2):
                h = 2 * hp + hh
                nb = 6 + hp
                Vh = Vb[:, h * 256:(h + 1) * 256]
                mm(psf(nb, hh * 256, 256), PT[:, h, :], Vh, True, False, r=["PT", "Vb"], w=[f"ps{nb}"])
                mm(psf(nb, hh * 256, 256), qkT[:, 2 * h, :], Cdec[:, h, 0:256], False, False, r=[qk, "Cdec"], w=[f"ps{nb}"])
                mm(psf(nb, hh * 256, 256), qkT[:, 2 * h + 1, :], Cdec[:, h, 256:512], False, True, r=[qk, "Cdec"], w=[f"ps{nb}"])
                mm(psf(5, 256 + h, 1), PT[:, h, :], onesb[:, 0:1], True, False, r=["PT", "onesb"], w=["ps5"])
                mm(psf(5, 256 + h, 1), qkT[:, 2 * h, :], ndec[:, h, 0:1], False, False, r=[qk, "ndec"], w=["ps5"])
                mm(psf(5, 256 + h, 1), qkT[:, 2 * h + 1, :], ndec[:, h, 1:2], False, True, r=[qk, "ndec"], w=["ps5"])
                bc = nextbank()
                for kc in range(2):
                    mm(psf(bc, kc * 256, 256), wk[:, h * 256 + kc * 128:h * 256 + (kc + 1) * 128], Vh, True, True,
                       r=["wk", "Vb"], w=[f"ps{bc}"])
                for kc in range(2):
                    mm(psf(5, 260 + 2 * h + kc, 1), wk[:, h * 256 + kc * 128:h * 256 + (kc + 1) * 128], onesb[:, 0:1],
                       True, True, r=["wk", "onesb"], w=["ps5"])
                stt(Cst[:, h, :], Cst[:, h, :], decv[:, h:h + 1], psf(bc), ALU.mult, ALU.add,
                    r=["C", ek, f"ps{bc}", "Cdec"], w=["C"])
                stt(nst[:, h, :], nst[:, h, :], decv[:, h:h + 1], psf(5, 260 + 2 * h, 2), ALU.mult, ALU.add,
                    r=["n", ek, "ps5", "ndec"], w=["n"])
                P.add("dve", lambda E, h=h, nb=nb, hh=hh: E.bn_stats(out=bst6[:, h, :], in_=psf(nb, hh * 256, 256)),
                      r=[f"ps{nb}"], w=[f"bst{h}"])
                yield

    def out_thread(i):
        eout, wv, flv, decv, ek = eo(i)
        P.add("dve", lambda E: E.tensor_copy(out=den4, in_=psf(5, 256, 4)), r=["ps5"], w=["den4"])
        stt(den4, den4, -1.0, den4, ALU.mult, ALU.max, r=["den4"], w=["den4"])
        tt("dve", den4, den4, flv, ALU.max, r=["den4", ek], w=["den4"])
        for h in range(4):
            P.add("dve", lambda E, h=h: E.bn_aggr(out=mv4[:, h, :], in_=bst6[:, h, :]), r=[f"bst{h}"], w=[f"mv{h}"])
        yield
        P.add("dve", lambda E: E.reciprocal(out=rr, in_=den4), r=["den4"], w=["rr"])
        tt("dve", rs4, rr, rr, ALU.mult, r=["rr"], w=["rs4"])
        tt("dve", rs4, rs4, mv4[:, :, 1], ALU.mult, r=["rs4"] + [f"mv{h}" for h in range(4)], w=["rs4"])
        ts("dve", rs4, rs4, EPS, None, ALU.add, r=["rs4"], w=["rs4"])
        tt("pool", rs4, rs4, mhalf, ALU.pow, r=["rs4", "mhalf"], w=["rs4"])
        yield
        stt(rs4, rs4, 0.25, rr, ALU.mult, ALU.mult, r=["rs4", "rr"], w=["rs4"])
        stt(nb4, mv4[:, :, 0], -1.0, rs4, ALU.mult, ALU.mult, r=["rs4"] + [f"mv{h}" for h in range(4)], w=["nb4"])
        for h in range(4):
            nb = 6 + h // 2
            act(yb[:, h * 256:(h + 1) * 256], psf(nb, (h % 2) * 256, 256), AF.Identity, scale=rs4[:, h:h + 1],
                bias=nb4[:, h:h + 1], r=[f"ps{nb}", "nb4", "rs4"], w=["yb"])
        tt("dve", yb, yb, hng, ALU.mult, r=["yb", "hng"], w=["yb"])
        rot[:] = [1, 2, 0, 6, 7]
        yield
        for c0 in (OFF_O, OFF_GB):
            for half in range(2):
                bank = tokblk(i, c0 + half * 512)
                act(tmpA[half], psf(bank), AF.Tanh, scale=0.5, r=[f"ps{bank}"], w=[f"tmpA{half}"])
                stt(yb[:, half * 512:(half + 1) * 512], tmpA[half], 1.0, yb[:, half * 512:(half + 1) * 512],
                    ALU.add, ALU.mult, r=[f"tmpA{half}", "yb"], w=["yb"])
                yield

    def merge(i):
        for half in range(2):
            hs = slice(half * 512, (half + 1) * 512)
            stt(mg[:, hs], tmpB[half], 0.5, yb[:, hs], ALU.mult, ALU.add, r=[f"tmpB{half}", "yb"], w=["mg"])
        dma("sp", merged_d[i * 128:(i + 1) * 128, :], mg, r=["mg"], w=[f"mgd{i}"])
        yield

    def chain(*gens):
        for g in gens:
            yield from g

    def interleave(*gens):
        gens = list(gens)
        while gens:
            for g in list(gens):
                try:
                    next(g)
                except StopIteration:
                    gens.remove(g)

    def zipg(*gens):
        gens = list(gens)
        while gens:
            for g in list(gens):
                try:
                    next(g)
                    yield
                except StopIteration:
                    gens.remove(g)

    interleave(chain(front(0), front_b(0), zipg(qkconv(0), gates(0)), vblocks(0)))
    if nt > 1:
        interleave(front(1))
    for i in range(nt):
        rot[:] = [1, 2]
        if i + 1 < nt:
            interleave(mlstm_core(i), chain(front_b(i + 1), gates(i + 1)), qkconv(i + 1))
            rot[:] = [1, 2, 0]
            if i + 2 < nt:
                interleave(out_thread(i), gmlp(i), vblocks(i + 1), front(i + 2))
            else:
                interleave(out_thread(i), gmlp(i), vblocks(i + 1))
        else:
            interleave(mlstm_core(i))
            rot[:] = [1, 2, 0]
            interleave(out_thread(i), gmlp(i))
        interleave(merge(i))

    P.add("act", lambda E: E.activation(out=scr["act"][:, 0:1], in_=scr["act"][:, 1:2], func=AF.Copy), w=["end_act", "scr_act"])
    memset("dve", scr["dve"], 0.0, w=["end_dve"])
    memset("pool", scr["pool"], 0.0, w=["end_pool"])
    tr(psb(0, 0, 128), identb, identb, r=["CB"], w=["ps0", "end_pe"])
    ENDK = ["end_act", "end_dve", "end_pool", "end_pe"] + [f"mgd{i}" for i in range(nt)]

    if phase2:
        A2 = Alloc(base1)
        WOUT = A2.bf([128, 8, D])
        FF1 = A2.bf([128, 8, 4 * D])
        FF2 = A2.bf([128, 32, D])
        g1b = A2.f32([128, D])
        g2b = A2.f32([128, D])
        fgb = A2.f32([128, D])
        xt2 = [A2.f32([128, D]) for _ in range(2)]
        mg2 = [A2.bf([128, D]) for _ in range(2)]
        mgT = A2.bf([128, 8, 128])
        xhat2 = A2.bf([128, D])
        xn2Ts = [A2.bf([128, 8, 128]) for _ in range(2)]
        hidT = A2.bf([128, 32, 128])
        tmpRA = [A2.f32([128, 512]) for _ in range(2)]
        tmpRB = [A2.f32([128, 512]) for _ in range(2)]
        tmpH = [A2.f32([128, 512]) for _ in range(2)]
        junkB = tmpH[0].bitcast(BF16)
        outb = A2.f32([128, D])
        sq = A2.f32([128, 16])
        ssq2, mse2, rstd2, ssq3, mse3, rstd3 = (sq[:, k:k + 1] for k in range(6))
        P.add("act", lambda E: E.activation(out=scr["act"][:, 0:1], in_=scr["act"][:, 1:2], func=AF.Copy), r=ENDK, w=["scr_act"])
        memset("dve", scr["dve"], 0.0, r=ENDK, w=["bar_dve"])
        memset("pool", scr["pool"], 0.0, r=ENDK, w=["bar_pool"])
        tr(psb(0, 0, 128), identb, identb, r=ENDK + ["CB"], w=["ps0"])
        w_out_v = w_out.rearrange("(kc p) n -> p kc n", p=128)
        w_ff1_v = w_ff1.rearrange("(kc p) n -> p kc n", p=128)
        w_ff2_v = w_ff2.rearrange("(fc p) n -> p fc n", p=128)

        def load_gb(b):
            dma("sp", g1b, mod_d[b, 2 * D:3 * D].partition_broadcast(128), r=ENDK + ["mod_d"], w=["g1b"])
            dma("sp", g2b, mod_d[b, 5 * D:6 * D].partition_broadcast(128), r=ENDK + ["mod_d"], w=["g2b"])

        load_gb(0)
        dma("sp", fgb, final_g.partition_broadcast(128), r=ENDK, w=["fgb"])
        for kc in range(8):
            dma("pool", WOUT[:, kc, :], w_out_v[:, kc, :], r=ENDK, w=[f"wout{kc}"])
        for q4 in range(4):
            for kh in range(2):
                dma("pool", FF1[:, kh * 4:(kh + 1) * 4, q4 * 1024:(q4 + 1) * 1024],
                    w_ff1_v[:, kh * 4:(kh + 1) * 4, q4 * 1024:(q4 + 1) * 1024], r=ENDK, w=[f"ff1_{q4}"])
        for f4 in range(8):
            dma("pool", FF2[:, f4 * 4:(f4 + 1) * 4, :], w_ff2_v[:, f4 * 4:(f4 + 1) * 4, :], r=ENDK, w=[f"ff2_{f4}"])

        def load_x(i):
            s = i % 2
            dma("sp", xt2[s], x[i * 128:(i + 1) * 128, :], r=ENDK, w=[f"xt2_{s}"])

        def load_mg(i):
            s = i % 2
            dma("sp", mg2[s], merged_d[i * 128:(i + 1) * 128, :], r=ENDK + [f"mgd{i}"], w=[f"mg2_{s}"])

        def A2t(i):
            b = i // NT_SEQ
            s = i % 2
            X = xt2[s]
            xk = f"xt2_{s}"
            xn2T = xn2Ts[s]
            nk = f"xn2T{s}"
            for kc in range(8):
                tr(psb(0, kc * 128, 128), mg2[s][:, kc * 128:(kc + 1) * 128], identb, r=[f"mg2_{s}", "CB"], w=["ps0"])
            act(mgT.rearrange("p a b -> p (a b)"), psb(0), AF.Copy, r=["ps0"], w=["mgT"])
            yield
            for half in range(2):
                hs = slice(half * 512, (half + 1) * 512)
                for kc in range(8):
                    mm(psf(1 + half), mgT[:, kc, :], WOUT[:, kc, hs], kc == 0, kc == 7, r=["mgT", f"wout{kc}"],
                       w=[f"ps{1 + half}"])
                tt("dve", tmpRA[half], psf(1 + half), g1b[:, hs], ALU.mult, r=[f"ps{1 + half}", "g1b"],
                   w=[f"tmpRA{half}"])
                tt("pool", X[:, hs], X[:, hs], tmpRA[half], ALU.add, r=[xk, f"tmpRA{half}"], w=[xk])
                yield
            sumsq(X, xhat2, ssq2, [xk], "xhat2", "ssq2")
            ts("dve", mse2, ssq2, 1.0 / D, EPS, ALU.mult, ALU.add, r=["ssq2"], w=["mse2"])
            tt("pool", rstd2, mse2, mhalf[:, 0:1], ALU.pow, r=["mse2", "mhalf"], w=["rstd2"])
            act(xhat2, X, AF.Identity, scale=rstd2, r=[xk, "rstd2"], w=["xhat2"])
            yield
            for kc in range(8):
                tr(psb(0, kc * 128, 128), xhat2[:, kc * 128:(kc + 1) * 128], identb, r=["xhat2", "CB"], w=["ps0"])
            for kc in range(8):
                act(xn2T[:, kc, :], psb(0, kc * 128, 128), AF.Identity, scale=Gsh[:, 2, b * 8 + kc:b * 8 + kc + 1],
                    bias=Gsh[:, 3, b * 8 + kc:b * 8 + kc + 1], r=["ps0", "Gsh"], w=[nk])
            yield

        def B2t(i):
            b = i // NT_SEQ
            s = i % 2
            X = xt2[s]
            xk = f"xt2_{s}"
            xn2T = xn2Ts[s]
            nk = f"xn2T{s}"
            for f4 in range(8):
                bank = 3 + f4 % 2
                for j in range(4):
                    fb = f4 * 4 + j
                    for kc in range(8):
                        mm(psf(bank, j * 128, 128), FF1[:, kc, fb * 128:(fb + 1) * 128], xn2T[:, kc, :], kc == 0, kc == 7,
                           r=[nk, f"ff1_{f4 // 2}"], w=[f"ps{bank}"])
                act(tmpH[f4 % 2], psf(bank), AF.Relu, r=[f"ps{bank}"], w=[f"tmpH{f4 % 2}"])
                tt("pool" if f4 % 2 else "dve", hidT[:, f4 * 4:(f4 + 1) * 4, :].rearrange("p a b -> p (a b)"),
                   tmpH[f4 % 2], tmpH[f4 % 2], ALU.mult, r=[f"tmpH{f4 % 2}"], w=[f"hid{f4}"])
                yield
            for half in range(2):
                hs = slice(half * 512, (half + 1) * 512)
                for q4 in range(4):
                    for fc in range(q4 * 8, (q4 + 1) * 8):
                        mm(psf(5 + half), hidT[:, fc, :], FF2[:, fc, hs], fc == 0, fc == 31,
                           r=[f"hid{fc // 4}", f"ff2_{fc // 4}"], w=[f"ps{5 + half}"])
                    if q4 < 3:
                        yield
                tt("dve", tmpRB[half], psf(5 + half), g2b[:, hs], ALU.mult, r=[f"ps{5 + half}", "g2b"],
                   w=[f"tmpRB{half}"])
                tt("pool", X[:, hs], X[:, hs], tmpRB[half], ALU.add, r=[xk, f"tmpRB{half}"], w=[xk])
                yield
            sumsq(X, junkB, ssq3, [xk], "tmpH0", "ssq3")
            ts("dve", mse3, ssq3, 1.0 / D, EPS, ALU.mult, ALU.add, r=["ssq3"], w=["mse3"])
            tt("pool", rstd3, mse3, mhalf[:, 0:1], ALU.pow, r=["mse3", "mhalf"], w=["rstd3"])
            stt(outb, X, rstd3, fgb, ALU.mult, ALU.mult, r=[xk, "rstd3", "fgb"], w=["outb"])
            dma("sp", out[i * 128:(i + 1) * 128, :], outb, r=["outb"], w=[f"out{i}"])
            if i + 2 < nt:
                load_x(i + 2)
            if i + 1 < nt and (i + 1) % NT_SEQ == 0:
                load_gb((i + 1) // NT_SEQ)
            yield

        load_x(0)
        load_mg(0)
        if nt > 1:
            load_x(1)
            load_mg(1)
        interleave(A2t(0))
        for i in range(nt):
            if i + 2 < nt:
                load_mg(i + 2)
            if i + 1 < nt and (i + 1) % NT_SEQ != 0:
                interleave(B2t(i), A2t(i + 1))
            elif i + 1 < nt:
                interleave(B2t(i))
                interleave(A2t(i + 1))
            else:
                interleave(B2t(i))

    P.finalize()
    sems = {e: [es.enter_context(nc.semaphore(f"s_{e}{k}")) for k in range(SEM_K)] for e in ("pe", "act", "dve", "pool")}
    dsems = {"sp": [es.enter_context(nc.semaphore(f"d_sp{k}")) for k in range(N_DMA_SP)],
             "pool": [es.enter_context(nc.semaphore(f"d_pl{k}")) for k in range(N_DMA_POOL)]}
    with nc.Block() as block:
        @block.sync
        def _(E):
            P.emit("sp", E, sems, dsems)

        @block.gpsimd
        def _(E):
            P.emit("pool", E, sems, dsems)

        @block.scalar
        def _(E):
            P.emit("act", E, sems, dsems)

        @block.vector
        def _(E):
            P.emit("dve", E, sems, dsems)

        @block.tensor
        def _(E):
            P.emit("pe", E, sems, dsems)
    es.close()
    return nc, P


def prep_inputs(inputs):
    f = lambda a: np.ascontiguousarray(np.asarray(a, dtype=np.float32))
    x = f(inputs["x"])
    c = f(inputs["c"])
    idn = np.eye(128, dtype=np.float32)
    U = np.triu(np.ones((128, 128), np.float32))
    ones = np.ones((128, 128), np.float32)
    cid = np.arange(128) // 64
    mask2 = (cid[:, None] <= cid[None, :]).astype(np.float32)
    cfc = np.ascontiguousarray(np.concatenate([idn, U, ones, mask2], axis=1))
    cb16 = np.ascontiguousarray(np.concatenate([idn, U], axis=1).astype(ml_dtypes.bfloat16))
    conv_w = f(inputs["conv_w"])[0]
    shared = {
        "w_ada": f(inputs["w_ada"])[0], "b_ada": f(inputs["b_ada"])[0],
        "n1g": np.ascontiguousarray(f(inputs["norm1_g"])[0].reshape(8, 128).T),
        "n2g": np.ascontiguousarray(f(inputs["norm2_g"])[0].reshape(8, 128).T),
        "w_in": f(inputs["w_in"])[0],
        "cw": np.ascontiguousarray(conv_w.reshape(4, 16, 128).transpose(2, 1, 0).reshape(128, 64)),
        "cb": np.ascontiguousarray(f(inputs["conv_b"])[0].reshape(16, 128).T),
        "gate_b": np.ascontiguousarray(f(inputs["mlstm_gate_b"])[0].reshape(8)),
        "ln_g": f(inputs["gmlp_ln_g"])[0], "ln_b": f(inputs["gmlp_ln_b"])[0],
        "hn_g": f(inputs["mlstm_hn_g"])[0], "final_g": f(inputs["final_g"]),
        "wsT": np.ascontiguousarray(f(inputs["gmlp_ws"])[0].transpose(2, 0, 1).reshape(128, 1024)),
        "bs_tok": np.ascontiguousarray(f(inputs["gmlp_bs"])[0].T),
        "w_out": f(inputs["w_out"])[0], "w_ff1": f(inputs["w_ff1"])[0], "w_ff2": f(inputs["w_ff2"])[0],
        "cf": cfc, "cb16": cb16,
    }
    maps = []
    for core in range(8):
        xc = np.ascontiguousarray(x[2 * core:2 * core + 2].reshape(NT * 128, D))
        cc = c[2 * core:2 * core + 2]
        cTc = np.ascontiguousarray(cc.reshape(2, 8, 128).transpose(2, 1, 0).reshape(128, 16))
        m = dict(shared)
        m["x"] = xc
        m["cT"] = cTc
        maps.append(m)
    return maps


def kernel(**inputs):
    maps = prep_inputs(inputs)
    nc, _ = build_program()
    res = run_bass_kernel_spmd(nc, maps, core_ids=list(range(8)))
    outs = [np.asarray(r["out"], dtype=np.float32).reshape(2, SEQ, D) for r in res.results]
    return np.concatenate(outs, axis=0)
```
